# Optimizing a Trainium2 kernel written in Bass

```python
import math
import jax, jax.numpy as jnp
from jax import lax
import numpy as np

D_MODEL = 1024
BATCH = 16
SEQ = 2048
DEPTH = 1

W_S5 = D_MODEL // 2
S5_GROUP_CH = 16
S5_GROUPS = W_S5 // S5_GROUP_CH
S5_STATE = 64
W_LRU = D_MODEL - W_S5
LRU_HEADS = 8
LRU_HEAD_DIM = W_LRU // LRU_HEADS
CONV_WIDTH = 4
RG_C = 8.0
D_FF = int(math.ceil(8 * D_MODEL / 3 / 256) * 256)
D_IN = W_S5 + 2 * W_LRU
EPS = 1e-6

kernel_name = "hybrid_s5_rglru_parallel_heads"


def rms_norm(x, g):
    xf = x.astype(jnp.float32)
    ms = jnp.mean(xf * xf, axis=-1, keepdims=True)
    return (xf * lax.rsqrt(ms + EPS) * g.astype(jnp.float32)).astype(x.dtype)


def _linear_combine(p, q):
    a_i, b_i = p
    a_j, b_j = q
    return a_j * a_i, a_j * b_i + b_j


def s5_mixer(u, lam_re, lam_im, log_step, b_re, b_im, c_re, c_im, d, w_glu, b_glu):
    bsz, seq, _ = u.shape
    f32 = jnp.float32
    uf = u.astype(f32).reshape(bsz, seq, S5_GROUPS, S5_GROUP_CH)
    lam = lax.complex(jnp.minimum(lam_re.astype(f32), -1e-4), lam_im.astype(f32))
    step = jnp.exp(log_step.astype(f32))[:, None]
    lam_bar = jnp.exp(lam * step)
    b = lax.complex(b_re.astype(f32), b_im.astype(f32))
    b_bar = ((lam_bar - 1.0) / lam)[..., None] * b
    bu = lax.complex(jnp.einsum('blgc,gnc->blgn', uf, jnp.real(b_bar)),
                     jnp.einsum('blgc,gnc->blgn', uf, jnp.imag(b_bar)))
    lam_elems = jnp.broadcast_to(lam_bar, (seq, S5_GROUPS, S5_STATE))
    states = jax.vmap(lambda e: lax.associative_scan(_linear_combine, (lam_elems, e))[1])(bu)
    y = (jnp.einsum('blgn,gcn->blgc', jnp.real(states), c_re.astype(f32))
         - jnp.einsum('blgn,gcn->blgc', jnp.imag(states), c_im.astype(f32))
         + d.astype(f32) * uf)
    y = y.reshape(bsz, seq, W_S5)
    z = jax.nn.gelu(y)
    z = z * jax.nn.sigmoid(z @ w_glu.astype(f32) + b_glu.astype(f32))
    return z.astype(u.dtype)


def rglru_mixer(xr, yr, conv_w, conv_b, w_a, b_a, w_x, b_x, lam):
    bsz, seq, _ = xr.shape
    f32 = jnp.float32
    xf = xr.astype(f32)
    xpad = jnp.pad(xf, ((0, 0), (CONV_WIDTH - 1, 0), (0, 0)))
    cw = conv_w.astype(f32)
    conv = conv_b.astype(f32) + sum(xpad[:, k:k + seq, :] * cw[k] for k in range(CONV_WIDTH))
    xh = conv.reshape(bsz, seq, LRU_HEADS, LRU_HEAD_DIM)
    r = jax.nn.sigmoid(jnp.einsum('blhi,hij->blhj', xh, w_a.astype(f32)) + b_a.astype(f32))
    i = jax.nn.sigmoid(jnp.einsum('blhi,hij->blhj', xh, w_x.astype(f32)) + b_x.astype(f32))
    log_a = RG_C * r * jax.nn.log_sigmoid(lam.astype(f32))
    a = jnp.exp(log_a)
    gated = jnp.sqrt(-jnp.expm1(2.0 * log_a)) * (i * xh)
    _, h = lax.associative_scan(_linear_combine, (a, gated), axis=1)
    out = h.reshape(bsz, seq, W_LRU) * jax.nn.gelu(yr.astype(f32))
    return out.astype(xr.dtype)


def setup_inputs(seed: int = 0) -> dict:
    key = jax.random.key(seed)
    ks = jax.random.split(key, 32)
    f32 = jnp.float32
    nrm = lambda k, shape, std: std * jax.random.normal(k, shape, f32)
    x = jax.random.normal(ks[0], (BATCH, SEQ, D_MODEL), f32)
    norm1_g = 1.0 + nrm(ks[1], (DEPTH, D_MODEL), 0.02)
    w_in = nrm(ks[2], (DEPTH, D_MODEL, D_IN), D_MODEL ** -0.5)
    n_idx = jnp.arange(S5_STATE, dtype=f32)
    s5_lambda_re = -0.5 + nrm(ks[3], (DEPTH, S5_GROUPS, S5_STATE), 0.01)
    s5_lambda_im = math.pi * n_idx + nrm(ks[4], (DEPTH, S5_GROUPS, S5_STATE), 0.01)
    s5_log_step = jax.random.uniform(ks[5], (DEPTH, S5_GROUPS), f32, math.log(1e-3), math.log(1e-1))
    bstd = (S5_GROUP_CH ** -0.5) / math.sqrt(2.0)
    s5_b_re = nrm(ks[6], (DEPTH, S5_GROUPS, S5_STATE, S5_GROUP_CH), bstd)
    s5_b_im = nrm(ks[7], (DEPTH, S5_GROUPS, S5_STATE, S5_GROUP_CH), bstd)
    cstd = (S5_STATE ** -0.5) / math.sqrt(2.0)
    s5_c_re = nrm(ks[8], (DEPTH, S5_GROUPS, S5_GROUP_CH, S5_STATE), cstd)
    s5_c_im = nrm(ks[9], (DEPTH, S5_GROUPS, S5_GROUP_CH, S5_STATE), cstd)
    s5_d = nrm(ks[10], (DEPTH, S5_GROUPS, S5_GROUP_CH), 1.0)
    s5_w_glu = nrm(ks[11], (DEPTH, W_S5, W_S5), W_S5 ** -0.5)
    s5_b_glu = nrm(ks[12], (DEPTH, W_S5), 0.01)
    lru_conv_w = nrm(ks[13], (DEPTH, CONV_WIDTH, W_LRU), CONV_WIDTH ** -0.5)
    lru_conv_b = nrm(ks[14], (DEPTH, W_LRU), 0.01)
    lru_w_a = nrm(ks[15], (DEPTH, LRU_HEADS, LRU_HEAD_DIM, LRU_HEAD_DIM), LRU_HEAD_DIM ** -0.5)
    lru_b_a = nrm(ks[16], (DEPTH, LRU_HEADS, LRU_HEAD_DIM), 0.01)
    lru_w_x = nrm(ks[17], (DEPTH, LRU_HEADS, LRU_HEAD_DIM, LRU_HEAD_DIM), LRU_HEAD_DIM ** -0.5)
    lru_b_x = nrm(ks[18], (DEPTH, LRU_HEADS, LRU_HEAD_DIM), 0.01)
    a_c = jax.random.uniform(ks[19], (DEPTH, LRU_HEADS, LRU_HEAD_DIM), f32, 0.9, 0.999)
    a0 = a_c ** (1.0 / RG_C)
    lru_lambda = jnp.log(a0) - jnp.log1p(-a0)
    s5_out_g = 1.0 + nrm(ks[20], (DEPTH, W_S5), 0.02)
    lru_out_g = 1.0 + nrm(ks[21], (DEPTH, W_LRU), 0.02)
    w_out = nrm(ks[22], (DEPTH, D_MODEL, D_MODEL), D_MODEL ** -0.5)
    norm2_g = 1.0 + nrm(ks[23], (DEPTH, D_MODEL), 0.02)
    w_gate = nrm(ks[24], (DEPTH, D_MODEL, D_FF), D_MODEL ** -0.5)
    w_up = nrm(ks[25], (DEPTH, D_MODEL, D_FF), D_MODEL ** -0.5)
    w_down = nrm(ks[26], (DEPTH, D_FF, D_MODEL), D_FF ** -0.5)
    final_g = 1.0 + nrm(ks[27], (D_MODEL,), 0.02)
    return {"x": x, "norm1_g": norm1_g, "w_in": w_in,
            "s5_lambda_re": s5_lambda_re, "s5_lambda_im": s5_lambda_im, "s5_log_step": s5_log_step,
            "s5_b_re": s5_b_re, "s5_b_im": s5_b_im, "s5_c_re": s5_c_re, "s5_c_im": s5_c_im,
            "s5_d": s5_d, "s5_w_glu": s5_w_glu, "s5_b_glu": s5_b_glu,
            "lru_conv_w": lru_conv_w, "lru_conv_b": lru_conv_b, "lru_w_a": lru_w_a, "lru_b_a": lru_b_a,
            "lru_w_x": lru_w_x, "lru_b_x": lru_b_x, "lru_lambda": lru_lambda,
            "s5_out_g": s5_out_g, "lru_out_g": lru_out_g, "w_out": w_out,
            "norm2_g": norm2_g, "w_gate": w_gate, "w_up": w_up, "w_down": w_down, "final_g": final_g}


def reference(x, norm1_g, w_in, s5_lambda_re, s5_lambda_im, s5_log_step, s5_b_re, s5_b_im,
              s5_c_re, s5_c_im, s5_d, s5_w_glu, s5_b_glu, lru_conv_w, lru_conv_b, lru_w_a, lru_b_a,
              lru_w_x, lru_b_x, lru_lambda, s5_out_g, lru_out_g, w_out, norm2_g, w_gate, w_up,
              w_down, final_g):
    for l in range(DEPTH):
        h = rms_norm(x, norm1_g[l])
        proj = h @ w_in[l]
        u = proj[..., :W_S5]
        xr = proj[..., W_S5:W_S5 + W_LRU]
        yr = proj[..., W_S5 + W_LRU:]
        s5_out = s5_mixer(u, s5_lambda_re[l], s5_lambda_im[l], s5_log_step[l], s5_b_re[l], s5_b_im[l],
                          s5_c_re[l], s5_c_im[l], s5_d[l], s5_w_glu[l], s5_b_glu[l])
        lru_out = rglru_mixer(xr, yr, lru_conv_w[l], lru_conv_b[l], lru_w_a[l], lru_b_a[l],
                              lru_w_x[l], lru_b_x[l], lru_lambda[l])
        mix = jnp.concatenate([rms_norm(s5_out, s5_out_g[l]), rms_norm(lru_out, lru_out_g[l])], axis=-1)
        x = x + mix @ w_out[l]
        h2 = rms_norm(x, norm2_g[l])
        x = x + (jax.nn.silu(h2 @ w_gate[l]) * (h2 @ w_up[l])) @ w_down[l]
    return rms_norm(x, final_g)
```

```python
import numpy as np
import concourse.bass as bass
import concourse.mybir as mybir
from concourse.bass_utils import run_bass_kernel_spmd

F32 = mybir.dt.float32
BF16 = mybir.dt.bfloat16
I32 = mybir.dt.int32
ALU = mybir.AluOpType
AF = mybir.ActivationFunctionType

NCORES = 8
D = 1024
L = 2048
NTOK = 4096
N = 512
NT = NTOK // N
TPS = L // N
TCH = 4
NCH = N // TCH
DFF = 2816
NF = DFF // 128
EPS = 1e-6
TWO_PI = 6.2831845
SLOT = 2816


class Region:
    __slots__ = ("name", "writer", "readers")

    def __init__(self, name):
        self.name = name
        self.writer = None
        self.readers = []


class Chan:
    def __init__(self, sem):
        self.sem = sem
        self.count = 0


class Op:
    __slots__ = ("idx", "eng", "fn", "deps", "signals", "tick", "sem", "is_dma", "chan", "dur", "prio", "odeps", "succs",
                 "ndeps", "ready_t", "fin")

    def __init__(self, idx, eng, fn, is_dma=False, chan=None):
        self.idx = idx
        self.eng = eng
        self.fn = fn
        self.deps = {}
        self.signals = False
        self.tick = None
        self.sem = None
        self.is_dma = is_dma
        self.chan = chan
        self.dur = 100.0
        self.prio = 0
        self.odeps = None
        self.succs = []
        self.ndeps = 0
        self.ready_t = 0.0
        self.fin = 0.0


class Sched:
    ENGS = ("pe", "act", "dve", "pool", "sp")

    def __init__(self):
        self.ops = []
        self.cur_prio = -1

    @staticmethod
    def _flat(x):
        out = []
        for i in x:
            if isinstance(i, (tuple, list)):
                out.extend(Sched._flat(i))
            else:
                out.append(i)
        return out

    def add(self, eng, fn, reads=(), writes=(), is_dma=False, chan=None, dur=100.0):
        op = Op(len(self.ops), eng, fn, is_dma, chan)
        op.dur = dur
        op.prio = self.cur_prio
        reads = self._flat(reads)
        writes = self._flat(writes)
        for r in reads:
            if r.writer is not None:
                op.deps[r.writer] = "raw"
            r.readers.append(op)
        for w in writes:
            if w.writer is not None and w.writer not in op.deps:
                op.deps[w.writer] = "waw"
            for rd in w.readers:
                if rd is not op and rd not in op.deps:
                    op.deps[rd] = "war"
            w.writer = op
            w.readers = []
        op.odeps = list(op.deps)
        for d in list(op.deps):
            if (not d.is_dma) and (not op.is_dma) and d.eng == op.eng and op.deps[d] != "raw":
                del op.deps[d]
        for d in op.deps:
            d.signals = True
        self.ops.append(op)
        return op

    def schedule(self):
        XLAT = 150.0
        for op in self.ops:
            op.succs = []
            op.ndeps = len(op.odeps)
            op.ready_t = 0.0
        for op in self.ops:
            for d in op.odeps:
                d.succs.append(op)
        ready = {e: [] for e in self.ENGS}
        free = {e: 0.0 for e in self.ENGS}
        for op in self.ops:
            if op.ndeps == 0:
                ready[op.eng].append(op)
        order = []
        n = len(self.ops)
        while len(order) < n:
            best = None
            bkey = None
            for e in self.ENGS:
                if not ready[e]:
                    continue
                fe = free[e]
                o = min(ready[e], key=lambda o_: (max(o_.ready_t, fe), o_.prio, o_.idx))
                key = (max(o.ready_t, fe), o.prio, o.idx)
                if bkey is None or key < bkey:
                    best, bkey = o, key
            st_ = bkey[0]
            e = best.eng
            ready[e].remove(best)
            if best.is_dma:
                free[e] = st_ + 60.0
                best.fin = st_ + best.dur
            else:
                best.fin = st_ + best.dur
                free[e] = best.fin
            order.append(best)
            for s_ in best.succs:
                lat = best.fin + (XLAT if (s_.eng != e or best.is_dma) else (0.0 if e == "pe" else 220.0))
                if lat > s_.ready_t:
                    s_.ready_t = lat
                s_.ndeps -= 1
                if s_.ndeps == 0:
                    ready[s_.eng].append(s_)
        self.ops = order
        self.makespan = max(o.fin for o in order)

    def emit(self, nc, block, eng_sems, engines):
        self.schedule()
        cnt = {e: 0 for e in self.ENGS}
        for op in self.ops:
            if op.is_dma:
                op.chan.count += 16
                op.tick = op.chan.count
                op.sem = op.chan.sem
            elif op.signals:
                cnt[op.eng] += 1
                op.tick = cnt[op.eng]
                op.sem = eng_sems[op.eng]

        def run(engname, e):
            waited = {}
            for op in self.ops:
                if op.eng != engname:
                    continue
                need = {}
                for d in op.deps:
                    k = id(d.sem)
                    if k not in need or need[k][1] < d.tick:
                        need[k] = (d.sem, d.tick)
                for k, (sem, v) in need.items():
                    if waited.get(k, 0) >= v:
                        continue
                    e.wait_ge(sem, v)
                    waited[k] = v
                ins = op.fn(e)
                if op.is_dma:
                    ins.then_inc(op.sem, 16)
                elif op.signals:
                    ins.then_inc(op.sem, 1)

        @block.tensor
        def _(e):
            run("pe", e)

        @block.scalar
        def _(e):
            run("act", e)

        @block.vector
        def _(e):
            run("dve", e)

        @block.gpsimd
        def _(e):
            run("pool", e)

        @block.sync
        def _(e):
            run("sp", e)


def build_program(debug=None):
    nc = bass.Bass("TRN2", target_bir_lowering=False)
    S = Sched()

    def din(name, shape, dt=F32):
        return nc.dram_tensor(name, list(shape), dt, kind="ExternalInput").ap()

    x_d = din("x", (128, 8, NTOK))
    out_d = nc.dram_tensor("out", [128, 8, NTOK], F32, kind="ExternalOutput").ap()
    w_in_d = din("w_in", (6, 128, 2048))
    w_glu_d = din("w_glu", (128, 2048))
    w_out_d = din("w_out", (4, 128, 2048))
    w_gu_d = din("w_gu", (NF, 128, 2048))
    w_dn_d = din("w_dn", (8, 128, 2816))
    vec_d = din("vecs", (128, 72))
    lamB_d = din("lamB", (128, 3, 16))
    lamC_d = din("lamC", (128, 3, 256))
    bC_d = din("bC", (128, 2, 256))
    bB_d = din("bB", (128, 2, 256))
    cB_d = din("cB", (128, 2, 256))
    gate_d = din("gatew", (128, 2, 512))
    const_d = din("consts", (128, 128 + 129 + 2 + 4))
    dbg_d = None
    if debug is not None:
        dbg_d = nc.dram_tensor("dbg", list(debug), F32, kind="ExternalOutput").ap()

    ctx = []

    def sb(name, shape, dt=F32):
        cm = nc.sbuf_tensor("s_" + name, list(shape), dt)
        t = cm.__enter__()
        ctx.append(cm)
        return t

    def ps(name, shape, dt=F32):
        cm = nc.psum_tensor(name, list(shape), dt)
        t = cm.__enter__()
        ctx.append(cm)
        return t

    def sem(name):
        cm = nc.semaphore(name)
        s = cm.__enter__()
        ctx.append(cm)
        return s

    R = Region
    R2 = lambda a_, b_: (a_, b_)

    vecs = sb("vecs", (128, 72)); r_vecs = R("vecs")
    consts = sb("consts", (128, 263)); r_consts = R("consts")
    ident = consts[:, 0:128]
    iota = consts[:, 128:257]
    mask_g2 = consts[:, 257:259]
    mask_r = consts[:, 259:263]
    V_G1, V_G2, V_GF = 0, 8, 16
    V_S5G, V_LRUG, V_BGLU, V_CB, V_BA, V_BX, V_LAM, V_D = 24, 28, 32, 36, 40, 44, 48, 52
    dv = sb("dv", (128, 32)); r_dv = R("dv")
    DV_S5GH, DV_HBGLU, DV_HBA, DV_HBX, DV_C, DV_HC = 0, 4, 8, 12, 16, 20
    ones_b = sb("ones_b", (128, 128), BF16); r_ones = R("ones")
    epsb = sb("epsb", (128, 4)); r_epsb = R("epsb")

    cosT = sb("cosT", (128, 16, 129)); sinT = sb("sinT", (128, 16, 129)); r_tab = R("tab")
    rho = sb("rho", (128, 16)); r_rho = R("rho")
    Wsi = sb("Wsi", (128, 4, 4, 2, 128), BF16); r_Wsi = R("Wsi")
    Wso = sb("Wso", (128, 16, 4, 2, 32), BF16); r_Wso = R("Wso")
    BD = sb("BD", (128, 4, 4, 128), BF16); r_BD = R("BD")
    Dconv = sb("Dconv", (128, 4, 4, 128), BF16); r_Dconv = R("Dconv")
    Wg_b = sb("Wg_b", (128, 2, 4, 128), BF16); r_Wg = R("Wg")
    hstate = sb("hstate", (128, 4)); r_hstate = [R(f"hst{m}") for m in range(4)]
    sinit = sb("sinit", (128, 16, 2)); r_sinit = R("sinit")
    wlast = sb("wlast", (128, 16, 2)); r_wlast = R("wlast")
    ctmp = sb("ctmp", (128, 64)); r_ctmp = R("ctmp")

    xbuf = [sb(f"xbuf{i}", (128, 8, N)) for i in range(2)]
    r_x = [[R(f"x{i}_{k}") for k in range(8)] for i in range(2)]
    rstd = sb("rstd", (128, N)); r_rstd = R("rstd")
    hb = sb("hb", (128, 8, N), BF16); r_h = [R(f"h{k}") for k in range(8)]
    h2b = sb("h2b", (128, 8, N), BF16); r_h2 = [R(f"h2_{k}") for k in range(8)]
    ub = sb("ub", (128, 4, N), BF16); r_u = [R(f"u{k}") for k in range(4)]
    xrb = sb("xrb", (128, 4, N + 4), BF16); r_xr = [R(f"xr{k}") for k in range(4)]
    gy = sb("gy", (128, 4, N)); r_gy = [R(f"gy{k}") for k in range(4)]
    lo = gy; r_lo = r_gy
    ltmp = sb("ltmp", (128, 3584)); r_lt = [R(f"lt{k}") for k in range(14)]
    r_zh = [R("zh0"), R("zh1")]
    zf = sb("zf", (128, 4, N)); r_zf = [R(f"zf{k}") for k in range(4)]
    zb = hb[:, 4:8, :]; r_zb = r_h[4:8]
    mix = sb("mix", (128, 8, N), BF16); r_mix = [R(f"mix{k}") for k in range(8)]
    Sbuf = sb("Sbuf", (128, 16, 2, NCH + 1), BF16); r_S = [R(f"S{q}") for q in range(16)]
    hh = sb("hh", (128, NF, N), BF16); r_hh = [R(f"hh{f}") for f in range(NF)]
    arena = hh[:].rearrange("p f n -> p (f n)").bitcast(F32)
    sgb = [sb(f"sg{i}", (128, N)) for i in range(2)]; r_sg = [R("sg0"), R("sg1")]
    ctmpA = sb("ctmpA", (128, 1024)); r_sgA = [R("sgA0"), R("sgA1")]
    NSF, NSM = 3, 2
    ringF = sb("ringF", (128, NSF, 2816), BF16); r_ringF = [R(f"ringF{i}") for i in range(NSF)]
    ringM = sb("ringM", (128, NSM, 2048), BF16); r_ringM = [R(f"ringM{i}") for i in range(NSM)]
    banks = [ps(f"bank{i}", (128, N)) for i in range(8)]
    r_bank = [R(f"bank{i}") for i in range(8)]

    eng_sems = {e: sem(f"sem_{e}") for e in Sched.ENGS}
    ringF_ch = [Chan(sem(f"ringFsem{i}")) for i in range(NSF)]
    ringM_ch = [Chan(sem(f"ringMsem{i}")) for i in range(NSM)]
    ringF_chH = [Chan(sem(f"ringFsemH{i}")) for i in range(NSF)]
    ringM_chH = [Chan(sem(f"ringMsemH{i}")) for i in range(NSM)]
    ringF_chS = [Chan(sem(f"ringFsemS{i}")) for i in range(NSF)]
    ringM_chS = [Chan(sem(f"ringMsemS{i}")) for i in range(NSM)]
    misc3_ch = Chan(sem("misc3sem"))
    x_ch = [Chan(sem(f"xsem{i}")) for i in range(2)]
    o_ch = [Chan(sem(f"osem{i}")) for i in range(2)]
    misc_ch = Chan(sem("miscsem"))
    dbg_ch = Chan(sem("dbgsem"))
    misc2_ch = Chan(sem("misc2sem"))
    r_out = [R("out0"), R("out1")]

    st = {"bank": 4, "slot": 0, "bankA": 0, "bankB": 0}

    def bankA():
        b = 3 + st["bankA"]
        st["bankA"] = (st["bankA"] + 1) % 3
        return b

    def bankB():
        b = st["bankB"]
        st["bankB"] = (st["bankB"] + 1) % 3
        return b
    r_dbg = R("dbg")

    def dump(it_, idx, ap, regs):
        if dbg_d is None or it_ >= 2:
            return
        k_ = it_ * 40 + idx
        S.add("pool", lambda e: e.dma_start(out=dbg_d[k_], in_=ap), tuple(regs), (r_dbg,), is_dma=True, chan=dbg_ch)

    def next_bank():
        b = st["bank"]
        st["bank"] = (b + 1) % 8
        return b

    def fsz(ap):
        n_ = 1
        for d_ in ap.shape[1:]:
            n_ *= int(d_)
        return n_

    def edur(eng, ap, k=1.04):
        if eng == "pool":
            return 170.0 + 2.7 * fsz(ap)
        if eng == "act":
            return 140.0 + 1.25 * fsz(ap)
        return 90.0 + 1.5 * k * fsz(ap)

    def dma(q, out, in_, reads, writes, chan):
        d_ = 2000.0 + fsz(out) * 128 * 2 / (70.0 if q == "pool" else 150.0)
        S.add(q, lambda e, out=out, in_=in_: e.dma_start(out=out, in_=in_), reads, writes, is_dma=True, chan=chan, dur=d_)

    def act(out, in_, func, reads, writes, bias=None, scale=None):
        kw = {}
        if bias is not None:
            kw["bias"] = bias
        if scale is not None:
            kw["scale"] = scale
        S.add("act", lambda e: e.activation(out=out, in_=in_, func=func, **kw), reads, writes, dur=edur("act", out))

    def tt(eng, out, in0, in1, op, reads, writes):
        S.add(eng, lambda e: e.tensor_tensor(out=out, in0=in0, in1=in1, op=op), reads, writes, dur=edur(eng, out))

    def ts(eng, out, in0, s1, op0, reads, writes, s2=None, op1=None):
        if op1 is None:
            S.add(eng, lambda e: e.tensor_scalar(out=out, in0=in0, scalar1=s1, scalar2=None, op0=op0), reads, writes, dur=edur(eng, out, 0.7))
        else:
            S.add(eng, lambda e: e.tensor_scalar(out=out, in0=in0, scalar1=s1, scalar2=s2, op0=op0, op1=op1), reads, writes, dur=edur(eng, out, 0.7))

    def stt(out, in0, scalar, in1, op0, op1, reads, writes):
        S.add("dve", lambda e: e.scalar_tensor_tensor(out=out, in0=in0, scalar=scalar, in1=in1, op0=op0, op1=op1), reads, writes, dur=edur("dve", out))

    def cp(eng, out, in_, reads, writes):
        S.add(eng, lambda e: e.tensor_copy(out=out, in_=in_), reads, writes, dur=edur(eng, out, 0.8))

    def memset(eng, ap, val, writes):
        S.add(eng, lambda e: e.memset(ap, val), (), writes, dur=edur(eng, ap, 0.5))

    def scan(out, d0, d1, init, reads, writes):
        S.add("dve", lambda e: e.tensor_tensor_scan(out=out, data0=d0, data1=d1, initial=init, op0=ALU.mult, op1=ALU.add), reads, writes,
              dur=edur("dve", out))

    def mmgroup(mms, reads, writes):
        def fn(e):
            ins = None
            for m_ in mms:
                kw = {}
                if m_.get("tp") is not None:
                    kw["tile_position"] = m_["tp"]
                ins = e.matmul(m_["out"], lhsT=m_["lhsT"], rhs=m_["rhs"], start=m_["start"], stop=m_["stop"], **kw)
            return ins
        d_ = 0.0
        for m_ in mms:
            f_ = 4.0 if m_["rhs"].dtype == F32 else 1.0
            d_ += f_ * max(fsz(m_["rhs"]), 64) / 2.2 + 6.0
        S.add("pe", fn, reads, writes, dur=d_)

    dma("sp", vecs[:], vec_d[:], (), (r_vecs,), misc_ch)
    dma("sp", consts[:], const_d[:], (), (r_consts,), misc2_ch)
    memset("pool", ones_b[:], 1.0, (r_ones,))
    memset("pool", epsb[:, 0:1], EPS, (r_epsb,))
    memset("pool", epsb[:, 1:2], 1.0, (r_epsb,))
    memset("pool", epsb[:, 2:3], 0.25, (r_epsb,))
    memset("pool", epsb[:, 3:4], 0.0, (r_epsb,))
    memset("pool", hstate[:], 0.0, tuple(r_hstate))

    r_pro = tuple(r_x[1]) + tuple(r_hh)
    pools = [(arena, 5632), (xbuf[1][:].rearrange("p a b -> p (a b)"), 4096)]
    pro_off = [0, 0]

    def palloc(n, pool=None):
        order = (0, 1) if pool is None else (pool,)
        for pi_ in order:
            if pro_off[pi_] + n <= pools[pi_][1]:
                o = pro_off[pi_]
                pro_off[pi_] += n
                return pools[pi_][0][:, o:o + n]
        raise AssertionError("prologue scratch exhausted")

    pro_i = gy[:].rearrange("p a b -> p (a b)").bitcast(I32)
    pro_f = zf[:].rearrange("p a b -> p (a b)")
    r_proi = tuple(r_gy) + tuple(r_zf)
    r_P = R("P")

    def P_ts(out, in0, s1, op0, s2=None, op1=None, extra_r=()):
        ts("dve", out, in0, s1, op0, (r_P,) + tuple(extra_r), (r_P,) + r_pro, s2, op1)

    def P_tt(out, in0, in1, op, extra_r=()):
        tt("dve", out, in0, in1, op, (r_P,) + tuple(extra_r), (r_P,) + r_pro)

    def P_act(out, in_, func, bias=None, scale=None, extra_r=()):
        act(out, in_, func, (r_P, r_epsb) + tuple(extra_r), (r_P,) + r_pro, bias, scale)

    def frac_round(out, x, n):
        assert n <= 2048
        ii = pro_i[:, 0:n]
        ff = pro_f[:, 0:n]
        S.add("dve", lambda e: e.tensor_copy(out=ii, in_=x), (r_P,) + r_proi, r_proi)
        S.add("dve", lambda e: e.tensor_copy(out=ff, in_=ii), r_proi, r_proi)
        tt("dve", out, x, ff, ALU.subtract, (r_P,) + r_proi, (r_P,) + r_pro)

    def cis(cos_out, sin_out, turns, n, tmp):
        frac_round(tmp, turns, n)
        P_act(sin_out, tmp, AF.Sin, scale=TWO_PI)
        P_ts(tmp, turns, 0.25, ALU.add)
        frac_round(tmp, tmp, n)
        P_act(cos_out, tmp, AF.Sin, scale=TWO_PI)

    def cmul(or_, oi_, ar, ai, br, bi, t1, t2):
        P_tt(t1, ar, br, ALU.mult)
        P_tt(t2, ai, bi, ALU.mult)
        P_tt(or_, t1, t2, ALU.subtract)
        P_tt(t1, ar, bi, ALU.mult)
        P_tt(t2, ai, br, ALU.mult)
        P_tt(oi_, t1, t2, ALU.add)

    def lam_powers(src_d, F, npow):
        lam = palloc(3 * F)
        dma("sp", lam, src_d, (), (r_P,) + r_pro, misc_ch)
        lr, li, ls = lam[:, 0:F], lam[:, F:2 * F], lam[:, 2 * F:3 * F]
        dl = palloc(F); a = palloc(F); fb = palloc(F); t1 = palloc(F); t2 = palloc(F); t3 = palloc(F)
        P_act(dl, ls, AF.Exp)
        P_ts(lr, lr, -1e-4, ALU.min)
        P_tt(a, lr, dl, ALU.mult)
        P_tt(fb, li, dl, ALU.mult)
        P_ts(fb, fb, 1.0 / (2.0 * np.pi), ALU.mult)
        em1 = palloc(F); E = palloc(F)
        P_ts(t1, a, 0.2, ALU.mult, 1.0, ALU.add)
        for cdiv in (0.25, 1.0 / 3.0, 0.5):
            P_tt(t1, t1, a, ALU.mult)
            P_ts(t1, t1, cdiv, ALU.mult, 1.0, ALU.add)
        P_tt(em1, t1, a, ALU.mult)
        P_ts(E, em1, 1.0, ALU.add)
        c1 = palloc(F); s1 = palloc(F); sh = palloc(F)
        cis(c1, s1, fb, F, t1)
        P_ts(t2, fb, 0.5, ALU.mult)
        frac_round(t1, t2, F)
        P_act(sh, t1, AF.Sin, scale=TWO_PI)
        Pw = {0: None}
        p1r = palloc(F); p1i = palloc(F)
        P_tt(p1r, E, c1, ALU.mult)
        P_tt(p1i, E, s1, ALU.mult)
        Pw[1] = (p1r, p1i)
        for k in range(2, npow + 1):
            pr = palloc(F); pi_ = palloc(F)
            cmul(pr, pi_, Pw[k - 1][0], Pw[k - 1][1], p1r, p1i, t1, t2)
            Pw[k] = (pr, pi_)
        nr = palloc(F)
        P_tt(nr, em1, c1, ALU.mult)
        P_tt(t1, sh, sh, ALU.mult)
        P_ts(t1, t1, -2.0, ALU.mult)
        P_tt(nr, nr, t1, ALU.add)
        den = palloc(F)
        P_tt(t1, lr, lr, ALU.mult)
        P_tt(t2, li, li, ALU.mult)
        P_tt(den, t1, t2, ALU.add)
        S.add("dve", lambda e: e.reciprocal(out=den, in_=den), (r_P,), (r_P,) + r_pro)
        kr = palloc(F); ki = palloc(F)
        P_tt(t1, nr, lr, ALU.mult)
        P_tt(t2, p1i, li, ALU.mult)
        P_tt(t1, t1, t2, ALU.add)
        P_tt(kr, t1, den, ALU.mult)
        P_tt(t1, p1i, lr, ALU.mult)
        P_tt(t2, nr, li, ALU.mult)
        P_tt(t1, t1, t2, ALU.subtract)
        P_tt(ki, t1, den, ALU.mult)
        return dict(P=Pw, kr=kr, ki=ki, fb=fb, E=E, t=(t1, t2, t3))

    FC = 256
    LC = lam_powers(lamC_d[:].rearrange("p a f -> p (a f)"), FC, 3)
    bC = palloc(2 * FC)
    dma("sp", bC, bC_d[:].rearrange("p a f -> p (a f)"), (), (r_P,) + r_pro, misc_ch)
    bre, bim = bC[:, 0:FC], bC[:, FC:2 * FC]
    bbr = palloc(FC); bbi = palloc(FC)
    t1, t2, t3 = LC["t"]
    cmul(bbr, bbi, LC["kr"], LC["ki"], bre, bim, t1, t2)
    wr = palloc(FC); wi = palloc(FC)
    for k in range(TCH):
        pw = TCH - 1 - k
        if pw == 0:
            srcr, srci = bbr, bbi
        else:
            cmul(wr, wi, LC["P"][pw][0], LC["P"][pw][1], bbr, bbi, t1, t2)
            srcr, srci = wr, wi
        for reim, src in ((0, srcr), (1, srci)):
            for g2 in range(2):
                ts("dve", Wsi[:, :, k, reim, 64 * g2:64 * g2 + 64], src.rearrange("p (m n) -> p m n", m=4),
                   mask_g2[:, g2:g2 + 1], ALU.mult, (r_P, r_consts), (r_Wsi,))
    pro_off[0] = 0; pro_off[1] = 0
    FB = 16
    LB = lam_powers(lamB_d[:].rearrange("p a f -> p (a f)"), FB, 4)
    t1, t2, t3 = LB["t"]
    E = LB["E"]
    P_tt(t1, E, E, ALU.mult)
    tt("dve", rho[:], t1, t1, ALU.mult, (r_P,), (r_rho,))
    f4 = palloc(FB); fr = palloc(FB)
    P_ts(f4, LB["fb"], float(TCH), ALU.mult)
    frac_round(fr, f4, FB)
    NJ = NCH + 1
    HQ = FB // 2
    mark_ = list(pro_off)
    xt_ = palloc(HQ * NJ)
    xt3 = xt_.rearrange("p (q j) -> p q j", q=HQ)
    ang = palloc(HQ * NJ)
    for hq in range(2):
        for q in range(HQ):
            qq = hq * HQ + q
            ts("dve", xt3[:, q, :], iota, fr[:, qq:qq + 1], ALU.mult, (r_P, r_consts), (r_P,) + r_pro)
        frac_round(ang, xt_, HQ * NJ)
        act(sinT[:, hq * HQ:(hq + 1) * HQ, :].rearrange("p q j -> p (q j)"), ang, AF.Sin, (r_P,), (r_tab,), scale=TWO_PI)
        P_ts(xt_, xt_, 0.25, ALU.add)
        frac_round(ang, xt_, HQ * NJ)
        act(cosT[:, hq * HQ:(hq + 1) * HQ, :].rearrange("p q j -> p (q j)"), ang, AF.Sin, (r_P,), (r_tab,), scale=TWO_PI)
    pro_off[0], pro_off[1] = mark_
    FQ = 256
    bB = palloc(2 * FQ); cB = palloc(2 * FQ)
    dma("sp", bB, bB_d[:].rearrange("p a f -> p (a f)"), (), (r_P,) + r_pro, misc_ch)
    dma("sp", cB, cB_d[:].rearrange("p a f -> p (a f)"), (), (r_P,) + r_pro, misc_ch)

    def bc(v):
        return v.unsqueeze(2).to_broadcast([128, 16, 16])

    def v3(v):
        return v.rearrange("p (q c) -> p q c", q=16)

    u1 = palloc(FQ); u2 = palloc(FQ)

    def cmul_bc(or_, oi_, sr, si, xr_, xi_):
        P_tt(v3(u1), v3(xr_), bc(sr), ALU.mult)
        P_tt(v3(u2), v3(xi_), bc(si), ALU.mult)
        P_tt(or_, u1, u2, ALU.subtract)
        P_tt(v3(u1), v3(xi_), bc(sr), ALU.mult)
        P_tt(v3(u2), v3(xr_), bc(si), ALU.mult)
        P_tt(oi_, u1, u2, ALU.add)

    bbBr = palloc(FQ); bbBi = palloc(FQ)
    cmul_bc(bbBr, bbBi, LB["kr"], LB["ki"], bB[:, 0:FQ], bB[:, FQ:2 * FQ])
    Bblk = palloc(2 * 16 * 32)
    Cblk = palloc(2 * 16 * 4 * 32, pool=1)
    S.add("dve", lambda e: e.memset(Bblk, 0.0), (r_P,), (r_P,) + r_pro)
    S.add("dve", lambda e: e.memset(Cblk, 0.0), (r_P,), (r_P,) + r_pro)
    S.add("pool", lambda e: e.memset(Wso[:].rearrange("p a b c d -> p (a b c d)"), 0.0), (), (r_Wso,))
    Bblk4 = Bblk.rearrange("p (a q c) -> p a q c", a=2, q=16)
    Cblk5 = Cblk.rearrange("p (a q d c) -> p a q d c", a=2, q=16, d=4)
    for g2 in range(2):
        pl = slice(64 * g2, 64 * g2 + 64)
        for reim, src in ((0, bbBr), (1, bbBi)):
            S.add("dve", lambda e, o=Bblk4[pl, reim, :, 16 * g2:16 * g2 + 16], i_=v3(src)[pl]: e.tensor_copy(out=o, in_=i_),
                  (r_P,), (r_P,) + r_pro)
    cpr = palloc(FQ); cpi = palloc(FQ)
    cre, cim = cB[:, 0:FQ], cB[:, FQ:2 * FQ]
    for pw in range(0, TCH + 1):
        if pw == 0:
            srcr, srci = cre, cim
        else:
            cmul_bc(cpr, cpi, LB["P"][pw][0], LB["P"][pw][1], cre, cim)
            srcr, srci = cpr, cpi
        for g2 in range(2):
            pl = slice(64 * g2, 64 * g2 + 64)
            cs = slice(16 * g2, 16 * g2 + 16)
            if pw < TCH:
                S.add("dve", lambda e, o=Cblk5[pl, 0, :, pw, cs], i_=v3(srcr)[pl]: e.tensor_copy(out=o, in_=i_), (r_P,), (r_P,) + r_pro)
                S.add("dve", lambda e, o=Cblk5[pl, 1, :, pw, cs], i_=v3(srci)[pl]: e.tensor_scalar(out=o, in0=i_, scalar1=-1.0, scalar2=None, op0=ALU.mult),
                      (r_P,), (r_P,) + r_pro)
            if pw >= 1:
                t_ = pw - 1
                S.add("dve", lambda e, o=Wso[pl, :, t_, 0, cs], i_=v3(srcr)[pl]: e.tensor_copy(out=o, in_=i_), (r_P,), (r_Wso,))
                S.add("dve", lambda e, o=Wso[pl, :, t_, 1, cs], i_=v3(srci)[pl]: e.tensor_scalar(out=o, in0=i_, scalar1=-1.0, scalar2=None, op0=ALU.mult),
                      (r_P,), (r_Wso,))
    kb = 4
    KD = banks[kb][:].rearrange("p (m d c) -> p m d c", m=4, d=4)
    mms = []
    for q in range(16):
        m_, r_ = q // 4, q % 4
        for reim in range(2):
            mms.append(dict(out=KD[32 * r_:32 * r_ + 32, m_, :, :], lhsT=Bblk4[:, reim, q, :], rhs=Cblk5[:, reim, q, :, :],
                            start=(reim == 0), stop=(reim == 1), tp=(0, 32 * r_)))
    mmgroup(mms, (r_P,), (r_bank[kb],))
    BDf = Cblk[:, 0:2048]
    BDf4 = BDf.rearrange("p (m d c) -> p m d c", m=4, d=4)
    for r_ in range(4):
        ts("dve", BDf4[:, :, :, 32 * r_:32 * r_ + 32], KD, mask_r[:, r_:r_ + 1], ALU.mult, (r_bank[kb], r_consts, r_P), (r_P,) + r_pro)
    for m_ in range(4):
        stt(BDf4[:, m_, 0, :], ident, vecs[:, V_D + m_:V_D + m_ + 1], BDf4[:, m_, 0, :], ALU.mult, ALU.add,
            (r_P, r_consts, r_vecs), (r_P,) + r_pro)
    cp("dve", BD[:].rearrange("p m d c -> p (m d c)"), BDf, (r_P,), (r_BD,))

    dma("pool", Wg_b[:].rearrange("p a m c -> p (a m c)"), gate_d[:].rearrange("p a f -> p (a f)"), (), (r_Wg,), misc3_ch)
    for m_ in range(4):
        for k in range(4):
            ts("dve", Dconv[:, m_, k, :], ident, vecs[:, 56 + 4 * m_ + k:56 + 4 * m_ + k + 1], ALU.mult, (r_consts, r_vecs), (r_Dconv,))
    ts("dve", dv[:, DV_S5GH:DV_S5GH + 4], vecs[:, V_S5G:V_S5G + 4], 0.5, ALU.mult, (r_vecs,), (r_dv,))
    ts("dve", dv[:, DV_HBGLU:DV_HBGLU + 4], vecs[:, V_BGLU:V_BGLU + 4], 0.5, ALU.mult, (r_vecs,), (r_dv,))
    ts("dve", dv[:, DV_HBA:DV_HBA + 4], vecs[:, V_BA:V_BA + 4], 0.5, ALU.mult, (r_vecs,), (r_dv,))
    ts("dve", dv[:, DV_HBX:DV_HBX + 4], vecs[:, V_BX:V_BX + 4], 0.5, ALU.mult, (r_vecs,), (r_dv,))
    ev = palloc(4); pl_ = palloc(4)
    P_act(ev, vecs[:, V_LAM:V_LAM + 4], AF.Exp, scale=-1.0, extra_r=(r_vecs,))
    P_ts(pl_, ev, -0.25, ALU.mult, 1.0 / 3.0, ALU.add)
    P_tt(pl_, pl_, ev, ALU.mult)
    P_ts(pl_, pl_, -0.5, ALU.add)
    P_tt(pl_, pl_, ev, ALU.mult)
    P_ts(pl_, pl_, 1.0, ALU.add)
    P_tt(pl_, pl_, ev, ALU.mult)
    ts("dve", dv[:, DV_C:DV_C + 4], pl_, -8.0, ALU.mult, (r_P,), (r_dv,))
    ts("dve", dv[:, DV_HC:DV_HC + 4], pl_, -4.0, ALU.mult, (r_P,), (r_dv,))

    wlistM = []
    wlistF = []
    for _it in range(NT):
        wlistM += [(w_in_d[b_], 2048) for b_ in range(6)]
        wlistM += [(w_glu_d[:], 2048)]
        wlistM += [(w_out_d[b_], 2048) for b_ in range(4)]
        wlistF += [(w_gu_d[f_], 2048) for f_ in range(NF)]
        wlistF += [(w_dn_d[o_], 2816) for o_ in range(8)]
    wsM = {"issued": 0, "used": 0}
    wsF = {"issued": 0, "used": 0}
    scrM = nc.dram_tensor("scrM", [11, 128, 2048], BF16, kind="Internal").ap()
    scrF = nc.dram_tensor("scrF", [NF + 8, 128, 2816], BF16, kind="Internal").ap()

    def load_gen(ws, wl, ring_, rr, chs_sw, chs_hw, chs_st, ns, scr, per_tile):
        r_scr = [R(f"scr{id(scr)}_{i}") for i in range(per_tile)]

        def load_block(n):
            i_ = ws["used"]
            assert wl[i_][1] == n
            while ws["issued"] < min(len(wl), i_ + ns):
                j_ = ws["issued"]
                sj = j_ % ns
                nj = wl[j_][1]
                bj = j_ % per_tile
                if j_ < per_tile:
                    dma("pool", ring_[:, sj, 0:nj], wl[j_][0], (), (rr[sj],), chs_sw[sj])
                    dma("sp", scr[bj, :, 0:nj], ring_[:, sj, 0:nj], (rr[sj],), (r_scr[bj],), chs_st[sj])
                else:
                    dma("sp", ring_[:, sj, 0:nj], scr[bj, :, 0:nj], (r_scr[bj],), (rr[sj],), chs_hw[sj])
                ws["issued"] += 1
            ws["used"] += 1
            return i_ % ns
        return load_block

    loadM = load_gen(wsM, wlistM, ringM, r_ringM, ringM_ch, ringM_chH, ringM_chS, NSM, scrM, 11)
    loadF = load_gen(wsF, wlistF, ringF, r_ringF, ringF_ch, ringF_chH, ringF_chS, NSF, scrF, NF + 8)

    def rms(bankfn, src_regions, src_aps, nchunks, dim, sq_aps, sq_regs, rs_ap, rs_reg, stage_scale=1.0):
        for k in range(nchunks):
            act(sq_aps[k], src_aps[k], AF.Square, (src_regions[k],), (sq_regs[k],), scale=stage_scale)
        b = bankfn()
        mms = [dict(out=banks[b][:], lhsT=ones_b[:], rhs=sq_aps[k], start=(k == 0), stop=(k == nchunks - 1)) for k in range(nchunks)]
        mmgroup(mms, tuple(sq_regs[:nchunks]) + (r_ones,), (r_bank[b],))
        act(rs_ap, banks[b][:], AF.Ln, (r_bank[b], r_epsb), (rs_reg,), bias=epsb[:, 0:1], scale=1.0 / dim)
        act(rs_ap, rs_ap, AF.Exp, (rs_reg,), (rs_reg,), scale=-0.5)

    sqA = [hb[:, k, :] for k in range(8)]
    r_sq = r_h
    sq2 = [h2b[:, k, :] for k in range(8)]

    def stageA(it):
        xi = it % 2
        xb = xbuf[xi]
        rx = r_x[xi]
        seq_start = (it % TPS == 0)
        rms(bankA, rx, [xb[:, k, :] for k in range(8)], 8, float(D), sqA, r_sq, rstd[:], r_rstd)
        for k in range(8):
            stt(hb[:, k, :], xb[:, k, :], vecs[:, V_G1 + k:V_G1 + k + 1], rstd[:], ALU.mult, ALU.mult,
                (rx[k], r_vecs, r_rstd), (r_h[k],))
        yield 8
        for blk in range(6):
            s_ = loadM(2048)
            wv = ringM[:, s_, 0:2048].rearrange("p (o k m) -> p o k m", o=2, k=8)
            for o2 in range(2):
                oc = 2 * blk + o2
                b = bankA()
                mms = [dict(out=banks[b][:], lhsT=wv[:, o2, k, :], rhs=hb[:, k, :], start=(k == 0), stop=(k == 7)) for k in range(8)]
                mmgroup(mms, tuple(r_h) + (r_ringM[s_],), (r_bank[b],))
                m_ = oc % 4
                if oc < 4:
                    act(ub[:, m_, :], banks[b][:], AF.Copy, (r_bank[b],), (r_u[m_],))
                elif oc < 8:
                    if seq_start:
                        memset("pool", xrb[:, m_, 0:3], 0.0, (r_xr[m_],))
                    else:
                        cp("pool", xrb[:, m_, 0:3], xrb[:, m_, N:N + 3], (r_xr[m_],), (r_xr[m_],))
                    cp("dve", xrb[:, m_, 3:3 + N], banks[b][:], (r_bank[b], r_xr[m_]), (r_xr[m_],))
                else:
                    act(gy[:, m_, :], banks[b][:], AF.Gelu_apprx_tanh, (r_bank[b],), (r_gy[m_],))
                yield 8
        for m_ in range(4):
            dump(it, m_, ub[:, m_, :], (r_u[m_],))
        xhb = hb[:, 4:8, :]
        if seq_start and it > 0:
            for m_ in range(4):
                memset("dve", hstate[:, m_:m_ + 1], 0.0, (r_hstate[m_],))
        for half in range(2):
            xh = [ltmp[:, 512 * j:512 * j + 512] for j in range(2)]
            tr = [ltmp[:, 1024 + 512 * j:1024 + 512 * j + 512] for j in range(2)]
            ti = [ltmp[:, 2048 + 512 * j:2048 + 512 * j + 512] for j in range(2)]
            r_xh = [R2(r_lt[0], r_lt[1]), R2(r_lt[2], r_lt[3])]
            r_tr = [R2(r_lt[4], r_lt[5]), R2(r_lt[6], r_lt[7])]
            r_ti = [R2(r_lt[8], r_lt[9]), R2(r_lt[10], r_lt[11])]
            for j in range(2):
                m_ = 2 * half + j
                b = bankA()
                mms = [dict(out=banks[b][:], lhsT=Dconv[:, m_, k, :], rhs=xrb[:, m_, k:k + N], start=(k == 0), stop=(k == 3)) for k in range(4)]
                mmgroup(mms, (r_xr[m_], r_Dconv), (r_bank[b],))
                act(xh[j], banks[b][:], AF.Identity, (r_bank[b], r_vecs), (r_xh[j],), bias=vecs[:, V_CB + m_:V_CB + m_ + 1])
                cp("pool", xhb[:, m_, :], xh[j], (r_xh[j],), (r_h[4 + m_],))
                for gi, (dst, rdst, hbcol) in enumerate(((tr, r_tr, DV_HBA), (ti, r_ti, DV_HBX))):
                    b2 = bankA()
                    mmgroup([dict(out=banks[b2][:], lhsT=Wg_b[:, gi, m_, :], rhs=xhb[:, m_, :], start=True, stop=True)],
                            (r_h[4 + m_], r_Wg), (r_bank[b2],))
                    act(dst[j], banks[b2][:], AF.Tanh, (r_bank[b2], r_dv), (rdst[j],),
                        bias=dv[:, hbcol + m_:hbcol + m_ + 1], scale=0.5)
                yield 6
            for j in range(2):
                m_ = 2 * half + j
                a_ = tr[j]
                act(a_, tr[j], AF.Exp, (r_tr[j], r_dv), (r_tr[j],), bias=dv[:, DV_HC + m_:DV_HC + m_ + 1], scale=dv[:, DV_HC + m_:DV_HC + m_ + 1])
                w1 = sgA[j]; rw1 = r_sgA[j]
                act(w1, a_, AF.Square, (r_tr[j],), (rw1,))
                act(w1, w1, AF.Ln, (rw1, r_epsb), (rw1,), bias=epsb[:, 2:3], scale=-0.25)
                act(w1, w1, AF.Exp, (rw1,), (rw1,), scale=0.5)
                stt(ti[j], ti[j], 1.0, xh[j], ALU.add, ALU.mult, (r_ti[j], r_xh[j]), (r_ti[j],))
                tt("dve", ti[j], ti[j], w1, ALU.mult, (r_ti[j], rw1), (r_ti[j],))
                scan(xh[j], a_, ti[j], hstate[:, m_:m_ + 1], (r_tr[j], r_ti[j], r_hstate[m_]), (r_xh[j],))
                cp("dve", hstate[:, m_:m_ + 1], xh[j][:, N - 1:N], (r_xh[j],), (r_hstate[m_],))
                tt("dve", lo[:, m_, :], xh[j], gy[:, m_, :], ALU.mult, (r_xh[j], r_gy[m_]), (r_lo[m_],))
                yield 1
        for m_ in range(4):
            dump(it, 8 + m_, lo[:, m_, :], (r_lo[m_],))
        rms(bankA, r_lo, [lo[:, k, :] for k in range(4)], 4, 512.0, sqA, r_sq, rstd[:], r_rstd)
        for m_ in range(4):
            stt(mix[:, 4 + m_, :], lo[:, m_, :], vecs[:, V_LRUG + m_:V_LRUG + m_ + 1], rstd[:], ALU.mult, ALU.mult,
                (r_lo[m_], r_vecs, r_rstd), (r_mix[4 + m_],))
        yield 4
        Sc = Sbuf
        rS = r_S
        if seq_start:
            memset("pool", sinit[:], 0.0, (r_sinit,))
        for q in range(16):
            if seq_start:
                memset("pool", Sc[:, q, :, 0:1], 0.0, (rS[q],))
            else:
                cp("pool", Sc[:, q, :, 0:1], Sc[:, q, :, NCH:NCH + 1], (rS[q],), (rS[q],))

        def tbuf(idx, par):
            o_ = (2 * idx + par) * 256
            return ltmp[:, o_:o_ + 256].rearrange("p (a j) -> p a j", a=2), r_lt[2 * idx + par]
        def zhalf(par):
            return banks[6 + par][:, 0:256].rearrange("p (a j) -> p a j", a=2), r_bank[6 + par]

        def s0(q0):
            mms = []
            for reim in range(2):
                for k in range(TCH):
                    for q in (q0, q0 + 1):
                        m_, r_ = q // 4, q % 4
                        Z, rz = zhalf(q % 2)
                        mms.append(dict(out=Z[:, reim, :], lhsT=Wsi[32 * r_:32 * r_ + 32, m_, k, reim, :],
                                        rhs=ub[32 * r_:32 * r_ + 32, m_, k::TCH], start=(k == 0), stop=(k == TCH - 1),
                                        tp=(32 * r_, 0)))
            mmgroup(mms, (r_u[q0 // 4], r_Wsi), (r_bank[6], r_bank[7]))

        def s1(q):
            Z, rz = zhalf(q % 2)
            Zs, rZs = tbuf(0, q % 2)
            act(Zs, Z, AF.Copy, (rz,), (rZs,))

        def s2(q):
            Zs, rZs = tbuf(0, q % 2)
            T1, rT1 = tbuf(1, q % 2)
            M_, rM = tbuf(2, q % 2)
            cosb = cosT[:, q, 0:NCH].unsqueeze(1).to_broadcast([128, 2, NCH])
            sinq = sinT[:, q, 0:NCH]
            tt("pool", T1, Zs, cosb, ALU.mult, (rZs, r_tab), (rT1,))
            tt("pool", M_[:, 0, :], Zs[:, 1, :], sinq, ALU.mult, (rZs, r_tab), (rM,))
            tt("pool", M_[:, 1, :], Zs[:, 0, :], sinq, ALU.mult, (rZs, r_tab), (rM,))

        def s3(q):
            T1, rT1 = tbuf(1, q % 2)
            M_, rM = tbuf(2, q % 2)
            Wn, rWn = tbuf(3, q % 2)
            Wo, rWo = tbuf(4, q % 2)
            tt("dve", Wn[:, 0, :], T1[:, 0, :], M_[:, 0, :], ALU.add, (rT1, rM), (rWn,))
            tt("dve", Wn[:, 1, :], T1[:, 1, :], M_[:, 1, :], ALU.subtract, (rT1, rM), (rWn,))
            rho_b = rho[:, q:q + 1].to_broadcast([128, NCH])
            for reim in range(2):
                scan(Wo[:, reim, :], rho_b, Wn[:, reim, :], sinit[:, q, reim:reim + 1], (rWn, r_rho, r_sinit), (rWo,))

        def s4(q):
            Wo, rWo = tbuf(4, q % 2)
            P1, rP1 = tbuf(5, q % 2)
            Mp, rMp = tbuf(6, q % 2)
            cosb = cosT[:, q, 0:NCH].unsqueeze(1).to_broadcast([128, 2, NCH])
            sinq = sinT[:, q, 0:NCH]
            cp("pool", wlast[:, q, :], Wo[:, :, NCH - 1], (rWo,), (r_wlast,))
            tt("pool", P1, Wo, cosb, ALU.mult, (rWo, r_tab), (rP1,))
            tt("pool", Mp[:, 0, :], Wo[:, 1, :], sinq, ALU.mult, (rWo, r_tab), (rMp,))
            tt("pool", Mp[:, 1, :], Wo[:, 0, :], sinq, ALU.mult, (rWo, r_tab), (rMp,))

        def s5(q):
            P1, rP1 = tbuf(5, q % 2)
            Mp, rMp = tbuf(6, q % 2)
            tt("dve", Sc[:, q, 0, 1:NCH + 1], P1[:, 0, :], Mp[:, 0, :], ALU.subtract, (rP1, rMp), (rS[q],))
            tt("dve", Sc[:, q, 1, 1:NCH + 1], P1[:, 1, :], Mp[:, 1, :], ALU.add, (rP1, rMp), (rS[q],))

        def s6(m_):
            yb = bankA()
            Y = banks[yb][:].rearrange("p (t j) -> p t j", t=TCH)
            for t_ in range(TCH):
                mms = []
                for d_ in range(t_ + 1):
                    mms.append(dict(out=Y[:, t_, :], lhsT=BD[:, m_, d_, :], rhs=ub[:, m_, (t_ - d_)::TCH], start=(d_ == 0), stop=False))
                for r_ in range(4):
                    q = 4 * m_ + r_
                    for reim in range(2):
                        mms.append(dict(out=Y[32 * r_:32 * r_ + 32, t_, :], lhsT=Wso[:, q, t_, reim, :], rhs=Sc[:, q, reim, 0:NCH],
                                        start=False, stop=(r_ == 3 and reim == 1), tp=(0, 32 * r_)))
                mmgroup(mms, (r_u[m_], r_BD, r_Wso) + tuple(rS[4 * m_:4 * m_ + 4]), (r_bank[yb],))
            zview = zf[:, m_, :].rearrange("p (j t) -> p t j", t=TCH)
            act(zview, Y, AF.Gelu_apprx_tanh, (r_bank[yb],), (r_zf[m_],))
            cp("pool", zb[:, m_, :], zf[:, m_, :], (r_zf[m_],), (r_zb[m_],))
            dump(it, 4 + m_, zf[:, m_, :], (r_zf[m_],))

        for w in range(-1, 16 + 5):
            if 0 <= w < 16:
                s1(w)
            if 0 <= w + 1 < 16 and (w + 1) % 2 == 0:
                s0(w + 1)
            if 0 <= w - 1 < 16:
                s2(w - 1)
            if 0 <= w - 2 < 16:
                s3(w - 2)
            if 0 <= w - 3 < 16:
                s4(w - 3)
            if 0 <= w - 4 < 16:
                s5(w - 4)
                if (w - 4) % 4 == 3:
                    s6((w - 4) // 4)
            yield 3
        if not ((it + 1) % TPS == 0):
            c128 = cosT[:, :, NCH]; s128 = sinT[:, :, NCH]
            ta = ctmp[:, 0:16]; tb_ = ctmp[:, 16:32]; tc_ = ctmp[:, 32:48]; td = ctmp[:, 48:64]
            rc = r_ctmp
            tt("dve", ta, wlast[:, :, 0], c128, ALU.mult, (r_wlast, r_tab), (rc,))
            tt("dve", tb_, wlast[:, :, 1], s128, ALU.mult, (r_wlast, r_tab), (rc,))
            tt("dve", tc_, wlast[:, :, 0], s128, ALU.mult, (r_wlast, r_tab), (rc,))
            tt("dve", td, wlast[:, :, 1], c128, ALU.mult, (r_wlast, r_tab), (rc,))
            tt("dve", sinit[:, :, 0], ta, tb_, ALU.subtract, (rc,), (r_sinit,))
            tt("dve", sinit[:, :, 1], tc_, td, ALU.add, (rc,), (r_sinit,))
        s_ = loadM(2048)
        wv = ringM[:, s_, 0:2048].rearrange("p (k n) -> p k n", k=4)
        for oc in range(4):
            b = bankA()
            mms = [dict(out=banks[b][:], lhsT=wv[:, k, 128 * oc:128 * oc + 128], rhs=zb[:, k, :], start=(k == 0), stop=(k == 3)) for k in range(4)]
            mmgroup(mms, tuple(r_zb) + (r_ringM[s_],), (r_bank[b],))
            tg = sgA[oc % 2]; rtg = r_sgA[oc % 2]
            act(tg, banks[b][:], AF.Tanh, (r_bank[b], r_dv), (rtg,), bias=dv[:, DV_HBGLU + oc:DV_HBGLU + oc + 1], scale=0.5)
            stt(zf[:, oc, :], tg, 1.0, zf[:, oc, :], ALU.add, ALU.mult, (rtg, r_zf[oc]), (r_zf[oc],))
            yield 4
        rms(bankA, r_zf, [zf[:, k, :] for k in range(4)], 4, 512.0, sqA, r_sq, rstd[:], r_rstd, stage_scale=0.5)
        for m_ in range(4):
            stt(mix[:, m_, :], zf[:, m_, :], dv[:, DV_S5GH + m_:DV_S5GH + m_ + 1], rstd[:], ALU.mult, ALU.mult,
                (r_zf[m_], r_dv, r_rstd), (r_mix[m_],))
        yield 4
        for k in range(8):
            dump(it, 12 + k, mix[:, k, :], (r_mix[k],))
        for blk in range(4):
            s_ = loadM(2048)
            wv = ringM[:, s_, 0:2048].rearrange("p (o k m) -> p o k m", o=2, k=8)
            for o2 in range(2):
                oc = 2 * blk + o2
                b = bankA()
                mms = [dict(out=banks[b][:], lhsT=wv[:, o2, k, :], rhs=mix[:, k, :], start=(k == 0), stop=(k == 7)) for k in range(8)]
                mmgroup(mms, tuple(r_mix) + (r_ringM[s_],), (r_bank[b],))
                tt("dve", xb[:, oc, :], xb[:, oc, :], banks[b][:], ALU.add, (rx[oc], r_bank[b]), (rx[oc],))
                yield 8
        for k in range(8):
            dump(it, 20 + k, xb[:, k, :], (rx[k],))
        rms(bankA, rx, [xb[:, k, :] for k in range(8)], 8, float(D), sq2, r_h2, rstd[:], r_rstd)
        for k in range(8):
            stt(h2b[:, k, :], xb[:, k, :], vecs[:, V_G2 + k:V_G2 + k + 1], rstd[:], ALU.mult, ALU.mult,
                (rx[k], r_vecs, r_rstd), (r_h2[k],))
        yield 8

    def stageB(it):
        xi = it % 2
        xb = xbuf[xi]
        rx = r_x[xi]
        t0 = it * N
        for f in range(NF):
            s_ = loadF(2048)
            wv = ringF[:, s_, 0:2048].rearrange("p (g k m) -> p g k m", g=2, k=8)
            bg = bankB(); bu = bankB()
            mmgroup([dict(out=banks[bg][:], lhsT=wv[:, 0, k, :], rhs=h2b[:, k, :], start=(k == 0), stop=(k == 7)) for k in range(8)],
                    tuple(r_h2) + (r_ringF[s_],), (r_bank[bg],))
            mmgroup([dict(out=banks[bu][:], lhsT=wv[:, 1, k, :], rhs=h2b[:, k, :], start=(k == 0), stop=(k == 7)) for k in range(8)],
                    tuple(r_h2) + (r_ringF[s_],), (r_bank[bu],))
            sg = sgb[f % 2]; rsg = r_sg[f % 2]
            act(sg[:], banks[bg][:], AF.Silu, (r_bank[bg],), (rsg,))
            tt("dve", hh[:, f, :], sg[:], banks[bu][:], ALU.mult, (rsg, r_bank[bu]), (r_hh[f],))
            yield 16
        for oc in range(8):
            s_ = loadF(2816)
            wv = ringF[:, s_, 0:2816].rearrange("p (f m) -> p f m", f=NF)
            b = bankB()
            mmgroup([dict(out=banks[b][:], lhsT=wv[:, f, :], rhs=hh[:, f, :], start=(f == 0), stop=(f == NF - 1)) for f in range(NF)],
                    tuple(r_hh) + (r_ringF[s_],), (r_bank[b],))
            tt("dve", xb[:, oc, :], xb[:, oc, :], banks[b][:], ALU.add, (rx[oc], r_bank[b]), (rx[oc],))
            yield 22
        for k in range(8):
            dump(it, 28 + k, xb[:, k, :], (rx[k],))
        sqF = [hh[:, k, :] for k in range(8)]
        rms(bankB, rx, [xb[:, k, :] for k in range(8)], 8, float(D), sqF, r_hh, sgb[0][:], r_sg[0])
        for k in range(8):
            stt(xb[:, k, :], xb[:, k, :], vecs[:, V_GF + k:V_GF + k + 1], sgb[0][:], ALU.mult, ALU.mult,
                (rx[k], r_vecs, r_sg[0]), (rx[k],))
        dma("sp", out_d[:, :, t0:t0 + N], xb[:], tuple(rx), (r_out[xi],), o_ch[xi])
        if it + 2 < NT:
            dma("sp", xb[:], x_d[:, :, t0 + 2 * N:t0 + 3 * N], (), tuple(rx), x_ch[xi])
        yield 8

    sgA = [ctmpA[:, 0:512], ctmpA[:, 512:1024]]
    dma("sp", xbuf[0][:], x_d[:, :, 0:N], (), tuple(r_x[0]), x_ch[0])
    dma("sp", xbuf[1][:], x_d[:, :, N:2 * N], (), tuple(r_x[1]), x_ch[1])
    S.cur_prio = 0
    for _ in stageA(0):
        pass
    for it in range(NT):
        S.cur_prio = 2 * it + 3
        for _ in stageB(it):
            pass
        if it + 1 < NT:
            S.cur_prio = 2 * (it + 1)
            for _ in stageA(it + 1):
                pass
    S.cur_prio = 10 ** 6

    S.add("sp", lambda e: e.nop(), tuple(r_out) + (r_dbg,), ())

    with nc.Block() as block:
        S.emit(nc, block, eng_sems, None)
    build_program.last_makespan = getattr(S, "makespan", None)
    for cm in reversed(ctx):
        cm.__exit__(None, None, None)
    return nc


def _prep_shared(inp):
    f = np.float32
    g = lambda k: np.asarray(inp[k], dtype=f)
    sh = {}
    w_in = g("w_in")[0]
    sh["w_in"] = np.ascontiguousarray(w_in.reshape(8, 128, 6, 2, 128).transpose(2, 1, 3, 0, 4).reshape(6, 128, 2048))
    w_glu = g("s5_w_glu")[0]
    sh["w_glu"] = np.ascontiguousarray(w_glu.reshape(4, 128, 512).transpose(1, 0, 2).reshape(128, 2048))
    w_out = g("w_out")[0]
    sh["w_out"] = np.ascontiguousarray(w_out.reshape(8, 128, 4, 2, 128).transpose(2, 1, 3, 0, 4).reshape(4, 128, 2048))
    wg = g("w_gate")[0].reshape(8, 128, NF, 128)
    wu = g("w_up")[0].reshape(8, 128, NF, 128)
    gu = np.stack([wg, wu], axis=0)
    sh["w_gu"] = np.ascontiguousarray(gu.transpose(3, 2, 0, 1, 4).reshape(NF, 128, 2048))
    wd = g("w_down")[0].reshape(NF, 128, 8, 128)
    sh["w_dn"] = np.ascontiguousarray(wd.transpose(2, 1, 0, 3).reshape(8, 128, 2816))
    vecs = np.zeros((128, 72), f)
    vecs[:, 0:8] = g("norm1_g")[0].reshape(8, 128).T
    vecs[:, 8:16] = g("norm2_g")[0].reshape(8, 128).T
    vecs[:, 16:24] = g("final_g").reshape(8, 128).T
    vecs[:, 24:28] = g("s5_out_g")[0].reshape(4, 128).T
    vecs[:, 28:32] = g("lru_out_g")[0].reshape(4, 128).T
    vecs[:, 32:36] = g("s5_b_glu")[0].reshape(4, 128).T
    vecs[:, 36:40] = g("lru_conv_b")[0].reshape(4, 128).T
    vecs[:, 40:44] = g("lru_b_a")[0].reshape(4, 128).T
    vecs[:, 44:48] = g("lru_b_x")[0].reshape(4, 128).T
    vecs[:, 48:52] = g("lru_lambda")[0].reshape(4, 128).T
    vecs[:, 52:56] = g("s5_d")[0].reshape(4, 128).T
    cw = g("lru_conv_w")[0]
    cwl = cw.reshape(4, 4, 128).transpose(2, 1, 0)
    vecs[:, 56:72] = cwl.reshape(128, 16)
    sh["vecs"] = vecs
    lam_re = g("s5_lambda_re")[0]; lam_im = g("s5_lambda_im")[0]; ls = g("s5_log_step")[0]
    def layB(a):
        return a.reshape(16, 2, 64).transpose(1, 2, 0).reshape(128, 16)
    lsb = np.broadcast_to(ls[:, None], (32, 64))
    sh["lamB"] = np.ascontiguousarray(np.stack([layB(lam_re), layB(lam_im), layB(lsb)], axis=1))
    def layC(a):
        a4 = a.reshape(4, 4, 2, 64)
        a5 = np.broadcast_to(a4[:, :, :, None, :], (4, 4, 2, 16, 64))
        return a5.transpose(1, 2, 3, 0, 4).reshape(128, 256)
    sh["lamC"] = np.ascontiguousarray(np.stack([layC(lam_re), layC(lam_im), layC(lsb)], axis=1))
    b_re = g("s5_b_re")[0]; b_im = g("s5_b_im")[0]
    def layC_b(a):
        a5 = a.reshape(4, 4, 2, 64, 16)
        return a5.transpose(1, 2, 4, 0, 3).reshape(128, 256)
    sh["bC"] = np.ascontiguousarray(np.stack([layC_b(b_re), layC_b(b_im)], axis=1))
    def layB_b(a):
        return a.reshape(16, 2, 64, 16).transpose(1, 2, 0, 3).reshape(128, 256)
    sh["bB"] = np.ascontiguousarray(np.stack([layB_b(b_re), layB_b(b_im)], axis=1))
    c_re = g("s5_c_re")[0]; c_im = g("s5_c_im")[0]
    def layB_c(a):
        return a.reshape(16, 2, 16, 64).transpose(1, 3, 0, 2).reshape(128, 256)
    sh["cB"] = np.ascontiguousarray(np.stack([layB_c(c_re), layB_c(c_im)], axis=1))
    def bdiag(w):
        o = np.zeros((128, 4, 128), f)
        for h in range(8):
            m_, hh_ = h // 2, h % 2
            o[64 * hh_:64 * hh_ + 64, m_, 64 * hh_:64 * hh_ + 64] = w[h]
        return o.reshape(128, 512)
    sh["gatew"] = np.ascontiguousarray(np.stack([bdiag(g("lru_w_a")[0]), bdiag(g("lru_w_x")[0])], axis=1))
    consts = np.zeros((128, 263), f)
    consts[:, 0:128] = np.eye(128, dtype=f)
    consts[:, 128:257] = np.arange(129, dtype=f)[None, :]
    p = np.arange(128)
    for v in range(2):
        consts[:, 257 + v] = ((p % 32) // 16 == v).astype(f)
    for v in range(4):
        consts[:, 259 + v] = (p // 32 == v).astype(f)
    sh["consts"] = consts
    return sh


def _prep_x(x, core):
    xs = np.asarray(x[2 * core:2 * core + 2], dtype=np.float32).reshape(NTOK, 8, 128)
    return np.ascontiguousarray(xs.transpose(2, 1, 0))


_CACHE = {}


def kernel(**inputs):
    sh = _prep_shared(inputs)
    if "nc" not in _CACHE:
        _CACHE["nc"] = build_program()
    nc = _CACHE["nc"]
    in_maps = []
    for c in range(NCORES):
        m = dict(sh)
        m["x"] = _prep_x(inputs["x"], c)
        in_maps.append(m)
    res = run_bass_kernel_spmd(nc, in_maps, core_ids=list(range(NCORES)))
    outs = []
    for c in range(NCORES):
        o = np.asarray(res.results[c]["out"], dtype=np.float32)
        outs.append(o.transpose(2, 1, 0).reshape(2, L, D))
    return np.concatenate(outs, axis=0)
```

```python
import numpy as np
import concourse.bass as bass
import concourse.mybir as mybir
from concourse.bass_utils import run_bass_kernel_spmd

F32 = mybir.dt.float32
BF16 = mybir.dt.bfloat16
I32 = mybir.dt.int32
ALU = mybir.AluOpType
AF = mybir.ActivationFunctionType

NCORES = 8
D = 1024
L = 2048
NTOK = 4096
N = 512
NT = NTOK // N
TPS = L // N
TCH = 4
NCH = N // TCH
DFF = 2816
NF = DFF // 128
EPS = 1e-6
TWO_PI = 6.2831845
SLOT = 2816


class Region:
    __slots__ = ("name", "writer", "readers")

    def __init__(self, name):
        self.name = name
        self.writer = None
        self.readers = []


class Chan:
    def __init__(self, sem):
        self.sem = sem
        self.count = 0


class Op:
    __slots__ = ("idx", "eng", "fn", "deps", "signals", "tick", "sem", "is_dma", "chan", "dur", "prio", "odeps", "succs",
                 "ndeps", "ready_t", "fin")

    def __init__(self, idx, eng, fn, is_dma=False, chan=None):
        self.idx = idx
        self.eng = eng
        self.fn = fn
        self.deps = {}
        self.signals = False
        self.tick = None
        self.sem = None
        self.is_dma = is_dma
        self.chan = chan
        self.dur = 100.0
        self.prio = 0
        self.odeps = None
        self.succs = []
        self.ndeps = 0
        self.ready_t = 0.0
        self.fin = 0.0


class Sched:
    ENGS = ("pe", "act", "dve", "pool", "sp")

    def __init__(self):
        self.ops = []
        self.cur_prio = -1

    @staticmethod
    def _flat(x):
        out = []
        for i in x:
            if isinstance(i, (tuple, list)):
                out.extend(Sched._flat(i))
            else:
                out.append(i)
        return out

    def add(self, eng, fn, reads=(), writes=(), is_dma=False, chan=None, dur=100.0):
        op = Op(len(self.ops), eng, fn, is_dma, chan)
        op.dur = dur
        op.prio = self.cur_prio
        reads = self._flat(reads)
        writes = self._flat(writes)
        for r in reads:
            if r.writer is not None:
                op.deps[r.writer] = "raw"
            r.readers.append(op)
        for w in writes:
            if w.writer is not None and w.writer not in op.deps:
                op.deps[w.writer] = "waw"
            for rd in w.readers:
                if rd is not op and rd not in op.deps:
                    op.deps[rd] = "war"
            w.writer = op
            w.readers = []
        op.odeps = list(op.deps)
        for d in list(op.deps):
            if (not d.is_dma) and (not op.is_dma) and d.eng == op.eng and op.deps[d] != "raw":
                del op.deps[d]
        for d in op.deps:
            d.signals = True
        self.ops.append(op)
        return op

    def schedule(self):
        XLAT = 150.0
        for op in self.ops:
            op.succs = []
            op.ndeps = len(op.odeps)
            op.ready_t = 0.0
        for op in self.ops:
            for d in op.odeps:
                d.succs.append(op)
        ready = {e: [] for e in self.ENGS}
        free = {e: 0.0 for e in self.ENGS}
        for op in self.ops:
            if op.ndeps == 0:
                ready[op.eng].append(op)
        order = []
        n = len(self.ops)
        while len(order) < n:
            best = None
            bkey = None
            for e in self.ENGS:
                if not ready[e]:
                    continue
                fe = free[e]
                o = min(ready[e], key=lambda o_: (max(o_.ready_t, fe), o_.prio, o_.idx))
                key = (max(o.ready_t, fe), o.prio, o.idx)
                if bkey is None or key < bkey:
                    best, bkey = o, key
            st_ = bkey[0]
            e = best.eng
            ready[e].remove(best)
            if best.is_dma:
                free[e] = st_ + 60.0
                best.fin = st_ + best.dur
            else:
                best.fin = st_ + best.dur
                free[e] = best.fin
            order.append(best)
            for s_ in best.succs:
                lat = best.fin + (XLAT if (s_.eng != e or best.is_dma) else (0.0 if e == "pe" else 220.0))
                if lat > s_.ready_t:
                    s_.ready_t = lat
                s_.ndeps -= 1
                if s_.ndeps == 0:
                    ready[s_.eng].append(s_)
        self.ops = order
        self.makespan = max(o.fin for o in order)

    def emit(self, nc, block, eng_sems, engines):
        self.schedule()
        cnt = {e: 0 for e in self.ENGS}
        for op in self.ops:
            if op.is_dma:
                op.chan.count += 16
                op.tick = op.chan.count
                op.sem = op.chan.sem
            elif op.signals:
                cnt[op.eng] += 1
                op.tick = cnt[op.eng]
                op.sem = eng_sems[op.eng]

        def run(engname, e):
            waited = {}
            for op in self.ops:
                if op.eng != engname:
                    continue
                need = {}
                for d in op.deps:
                    k = id(d.sem)
                    if k not in need or need[k][1] < d.tick:
                        need[k] = (d.sem, d.tick)
                for k, (sem, v) in need.items():
                    if waited.get(k, 0) >= v:
                        continue
                    e.wait_ge(sem, v)
                    waited[k] = v
                ins = op.fn(e)
                if op.is_dma:
                    ins.then_inc(op.sem, 16)
                elif op.signals:
                    ins.then_inc(op.sem, 1)

        @block.tensor
        def _(e):
            run("pe", e)

        @block.scalar
        def _(e):
            run("act", e)

        @block.vector
        def _(e):
            run("dve", e)

        @block.gpsimd
        def _(e):
            run("pool", e)

        @block.sync
        def _(e):
            run("sp", e)


def build_program(debug=None):
    nc = bass.Bass("TRN2", target_bir_lowering=False)
    S = Sched()

    def din(name, shape, dt=F32):
        return nc.dram_tensor(name, list(shape), dt, kind="ExternalInput").ap()

    x_d = din("x", (128, 8, NTOK))
    out_d = nc.dram_tensor("out", [128, 8, NTOK], F32, kind="ExternalOutput").ap()
    w_in_d = din("w_in", (6, 128, 2048))
    w_glu_d = din("w_glu", (128, 2048))
    w_out_d = din("w_out", (4, 128, 2048))
    w_gu_d = din("w_gu", (NF, 128, 2048))
    w_dn_d = din("w_dn", (8, 128, 2816))
    vec_d = din("vecs", (128, 72))
    lamB_d = din("lamB", (128, 3, 16))
    lamC_d = din("lamC", (128, 3, 256))
    bC_d = din("bC", (128, 2, 256))
    bB_d = din("bB", (128, 2, 256))
    cB_d = din("cB", (128, 2, 256))
    gate_d = din("gatew", (128, 2, 512))
    const_d = din("consts", (128, 128 + 129 + 2 + 4))
    dbg_d = None
    if debug is not None:
        dbg_d = nc.dram_tensor("dbg", list(debug), F32, kind="ExternalOutput").ap()

    ctx = []

    def sb(name, shape, dt=F32):
        cm = nc.sbuf_tensor("s_" + name, list(shape), dt)
        t = cm.__enter__()
        ctx.append(cm)
        return t

    def ps(name, shape, dt=F32):
        cm = nc.psum_tensor(name, list(shape), dt)
        t = cm.__enter__()
        ctx.append(cm)
        return t

    def sem(name):
        cm = nc.semaphore(name)
        s = cm.__enter__()
        ctx.append(cm)
        return s

    R = Region
    R2 = lambda a_, b_: (a_, b_)

    vecs = sb("vecs", (128, 72)); r_vecs = R("vecs")
    consts = sb("consts", (128, 263)); r_consts = R("consts")
    ident = consts[:, 0:128]
    iota = consts[:, 128:257]
    mask_g2 = consts[:, 257:259]
    mask_r = consts[:, 259:263]
    V_G1, V_G2, V_GF = 0, 8, 16
    V_S5G, V_LRUG, V_BGLU, V_CB, V_BA, V_BX, V_LAM, V_D = 24, 28, 32, 36, 40, 44, 48, 52
    dv = sb("dv", (128, 32)); r_dv = R("dv")
    DV_S5GH, DV_HBGLU, DV_HBA, DV_HBX, DV_C, DV_HC = 0, 4, 8, 12, 16, 20
    ones_b = sb("ones_b", (128, 128), BF16); r_ones = R("ones")
    epsb = sb("epsb", (128, 4)); r_epsb = R("epsb")

    cosT = sb("cosT", (128, 16, 129)); sinT = sb("sinT", (128, 16, 129)); r_tab = R("tab")
    rho = sb("rho", (128, 16)); r_rho = R("rho")
    Wsi = sb("Wsi", (128, 4, 4, 2, 128), BF16); r_Wsi = R("Wsi")
    Wso = sb("Wso", (128, 16, 4, 2, 32), BF16); r_Wso = R("Wso")
    BD = sb("BD", (128, 4, 4, 128), BF16); r_BD = R("BD")
    Dconv = sb("Dconv", (128, 4, 4, 128), BF16); r_Dconv = R("Dconv")
    Wg_b = sb("Wg_b", (128, 2, 4, 128), BF16); r_Wg = R("Wg")
    hstate = sb("hstate", (128, 4)); r_hstate = [R(f"hst{m}") for m in range(4)]
    sinit = sb("sinit", (128, 16, 2)); r_sinit = R("sinit")
    wlast = sb("wlast", (128, 16, 2)); r_wlast = R("wlast")
    ctmp = sb("ctmp", (128, 64)); r_ctmp = R("ctmp")

    xbuf = [sb(f"xbuf{i}", (128, 8, N)) for i in range(2)]
    r_x = [[R(f"x{i}_{k}") for k in range(8)] for i in range(2)]
    rstd = sb("rstd", (128, N)); r_rstd = R("rstd")
    hb = sb("hb", (128, 8, N), BF16); r_h = [R(f"h{k}") for k in range(8)]
    h2b = sb("h2b", (128, 8, N), BF16); r_h2 = [R(f"h2_{k}") for k in range(8)]
    ub = sb("ub", (128, 4, N), BF16); r_u = [R(f"u{k}") for k in range(4)]
    xrb = sb("xrb", (128, 4, N + 4), BF16); r_xr = [R(f"xr{k}") for k in range(4)]
    gy = sb("gy", (128, 4, N)); r_gy = [R(f"gy{k}") for k in range(4)]
    lo = gy; r_lo = r_gy
    ltmp = sb("ltmp", (128, 3584)); r_lt = [R(f"lt{k}") for k in range(14)]
    r_zh = [R("zh0"), R("zh1")]
    zf = sb("zf", (128, 4, N)); r_zf = [R(f"zf{k}") for k in range(4)]
    zb = hb[:, 4:8, :]; r_zb = r_h[4:8]
    mix = sb("mix", (128, 8, N), BF16); r_mix = [R(f"mix{k}") for k in range(8)]
    Sbuf = sb("Sbuf", (128, 16, 2, NCH + 1), BF16); r_S = [R(f"S{q}") for q in range(16)]
    hh = sb("hh", (128, NF, N), BF16); r_hh = [R(f"hh{f}") for f in range(NF)]
    arena = hh[:].rearrange("p f n -> p (f n)").bitcast(F32)
    sgb = [sb(f"sg{i}", (128, N)) for i in range(2)]; r_sg = [R("sg0"), R("sg1")]
    ctmpA = sb("ctmpA", (128, 1024)); r_sgA = [R("sgA0"), R("sgA1")]
    NSF, NSM = 3, 2
    ringF = sb("ringF", (128, NSF, 2816), BF16); r_ringF = [R(f"ringF{i}") for i in range(NSF)]
    ringM = sb("ringM", (128, NSM, 2048), BF16); r_ringM = [R(f"ringM{i}") for i in range(NSM)]
    banks = [ps(f"bank{i}", (128, N)) for i in range(8)]
    r_bank = [R(f"bank{i}") for i in range(8)]

    eng_sems = {e: sem(f"sem_{e}") for e in Sched.ENGS}
    ringF_ch = [Chan(sem(f"ringFsem{i}")) for i in range(NSF)]
    ringM_ch = [Chan(sem(f"ringMsem{i}")) for i in range(NSM)]
    ringF_chH = [Chan(sem(f"ringFsemH{i}")) for i in range(NSF)]
    ringM_chH = [Chan(sem(f"ringMsemH{i}")) for i in range(NSM)]
    ringF_chS = [Chan(sem(f"ringFsemS{i}")) for i in range(NSF)]
    ringM_chS = [Chan(sem(f"ringMsemS{i}")) for i in range(NSM)]
    misc3_ch = Chan(sem("misc3sem"))
    x_ch = [Chan(sem(f"xsem{i}")) for i in range(2)]
    o_ch = [Chan(sem(f"osem{i}")) for i in range(2)]
    misc_ch = Chan(sem("miscsem"))
    dbg_ch = Chan(sem("dbgsem"))
    misc2_ch = Chan(sem("misc2sem"))
    vecs_ch = Chan(sem("vecssem"))
    r_out = [R("out0"), R("out1")]

    st = {"bank": 4, "slot": 0, "bankA": 0, "bankB": 0}

    def bankA():
        b = 3 + st["bankA"]
        st["bankA"] = (st["bankA"] + 1) % 3
        return b

    def bankB():
        b = st["bankB"]
        st["bankB"] = (st["bankB"] + 1) % 3
        return b
    r_dbg = R("dbg")

    def dump(it_, idx, ap, regs):
        if dbg_d is None or it_ >= 2:
            return
        k_ = it_ * 40 + idx
        S.add("pool", lambda e: e.dma_start(out=dbg_d[k_], in_=ap), tuple(regs), (r_dbg,), is_dma=True, chan=dbg_ch)

    def next_bank():
        b = st["bank"]
        st["bank"] = (b + 1) % 8
        return b

    def fsz(ap):
        n_ = 1
        for d_ in ap.shape[1:]:
            n_ *= int(d_)
        return n_

    def edur(eng, ap, k=1.04):
        if eng == "pool":
            return 170.0 + 2.7 * fsz(ap)
        if eng == "act":
            return 140.0 + 1.25 * fsz(ap)
        return 90.0 + 1.5 * k * fsz(ap)

    def dma(q, out, in_, reads, writes, chan):
        d_ = 2000.0 + fsz(out) * 128 * 2 / (70.0 if q == "pool" else 150.0)
        S.add(q, lambda e, out=out, in_=in_: e.dma_start(out=out, in_=in_), reads, writes, is_dma=True, chan=chan, dur=d_)

    def act(out, in_, func, reads, writes, bias=None, scale=None):
        kw = {}
        if bias is not None:
            kw["bias"] = bias
        if scale is not None:
            kw["scale"] = scale
        S.add("act", lambda e: e.activation(out=out, in_=in_, func=func, **kw), reads, writes, dur=edur("act", out))

    def tt(eng, out, in0, in1, op, reads, writes):
        S.add(eng, lambda e: e.tensor_tensor(out=out, in0=in0, in1=in1, op=op), reads, writes, dur=edur(eng, out))

    def ts(eng, out, in0, s1, op0, reads, writes, s2=None, op1=None):
        if op1 is None:
            S.add(eng, lambda e: e.tensor_scalar(out=out, in0=in0, scalar1=s1, scalar2=None, op0=op0), reads, writes, dur=edur(eng, out, 0.7))
        else:
            S.add(eng, lambda e: e.tensor_scalar(out=out, in0=in0, scalar1=s1, scalar2=s2, op0=op0, op1=op1), reads, writes, dur=edur(eng, out, 0.7))

    def stt(out, in0, scalar, in1, op0, op1, reads, writes):
        S.add("dve", lambda e: e.scalar_tensor_tensor(out=out, in0=in0, scalar=scalar, in1=in1, op0=op0, op1=op1), reads, writes, dur=edur("dve", out))

    def cp(eng, out, in_, reads, writes):
        S.add(eng, lambda e: e.tensor_copy(out=out, in_=in_), reads, writes, dur=edur(eng, out, 0.8))

    def memset(eng, ap, val, writes):
        S.add(eng, lambda e: e.memset(ap, val), (), writes, dur=edur(eng, ap, 0.5))

    def scan(out, d0, d1, init, reads, writes):
        S.add("dve", lambda e: e.tensor_tensor_scan(out=out, data0=d0, data1=d1, initial=init, op0=ALU.mult, op1=ALU.add), reads, writes,
              dur=edur("dve", out))

    def mmgroup(mms, reads, writes):
        def fn(e):
            ins = None
            for m_ in mms:
                kw = {}
                if m_.get("tp") is not None:
                    kw["tile_position"] = m_["tp"]
                ins = e.matmul(m_["out"], lhsT=m_["lhsT"], rhs=m_["rhs"], start=m_["start"], stop=m_["stop"], **kw)
            return ins
        d_ = 0.0
        for m_ in mms:
            f_ = 4.0 if m_["rhs"].dtype == F32 else 1.0
            d_ += f_ * max(fsz(m_["rhs"]), 64) / 2.2 + 6.0
        S.add("pe", fn, reads, writes, dur=d_)

    dma("sp", vecs[:], vec_d[:], (), (r_vecs,), vecs_ch)
    dma("sp", consts[:], const_d[:], (), (r_consts,), misc2_ch)
    memset("pool", ones_b[:], 1.0, (r_ones,))
    memset("pool", epsb[:, 0:1], EPS, (r_epsb,))
    memset("pool", epsb[:, 1:2], 1.0, (r_epsb,))
    memset("pool", epsb[:, 2:3], 0.25, (r_epsb,))
    memset("pool", epsb[:, 3:4], 0.0, (r_epsb,))
    memset("pool", hstate[:], 0.0, tuple(r_hstate))

    r_pro = tuple(r_x[1]) + tuple(r_hh)
    pools = [(arena, 5632), (xbuf[1][:].rearrange("p a b -> p (a b)"), 4096)]
    pro_off = [0, 0]

    def palloc(n, pool=None):
        order = (0, 1) if pool is None else (pool,)
        for pi_ in order:
            if pro_off[pi_] + n <= pools[pi_][1]:
                o = pro_off[pi_]
                pro_off[pi_] += n
                return pools[pi_][0][:, o:o + n]
        raise AssertionError("prologue scratch exhausted")

    pro_i = gy[:].rearrange("p a b -> p (a b)").bitcast(I32)
    pro_f = zf[:].rearrange("p a b -> p (a b)")
    r_proi = tuple(r_gy) + tuple(r_zf)
    r_P = R("P")

    def P_ts(out, in0, s1, op0, s2=None, op1=None, extra_r=()):
        ts("dve", out, in0, s1, op0, (r_P,) + tuple(extra_r), (r_P,) + r_pro, s2, op1)

    def P_tt(out, in0, in1, op, extra_r=()):
        tt("dve", out, in0, in1, op, (r_P,) + tuple(extra_r), (r_P,) + r_pro)

    def P_act(out, in_, func, bias=None, scale=None, extra_r=()):
        act(out, in_, func, (r_P, r_epsb) + tuple(extra_r), (r_P,) + r_pro, bias, scale)

    def frac_round(out, x, n):
        assert n <= 2048
        ii = pro_i[:, 0:n]
        ff = pro_f[:, 0:n]
        S.add("dve", lambda e: e.tensor_copy(out=ii, in_=x), (r_P,) + r_proi, r_proi)
        S.add("dve", lambda e: e.tensor_copy(out=ff, in_=ii), r_proi, r_proi)
        tt("dve", out, x, ff, ALU.subtract, (r_P,) + r_proi, (r_P,) + r_pro)

    def cis(cos_out, sin_out, turns, n, tmp):
        frac_round(tmp, turns, n)
        P_act(sin_out, tmp, AF.Sin, scale=TWO_PI)
        P_ts(tmp, turns, 0.25, ALU.add)
        frac_round(tmp, tmp, n)
        P_act(cos_out, tmp, AF.Sin, scale=TWO_PI)

    def cmul(or_, oi_, ar, ai, br, bi, t1, t2):
        P_tt(t1, ar, br, ALU.mult)
        P_tt(t2, ai, bi, ALU.mult)
        P_tt(or_, t1, t2, ALU.subtract)
        P_tt(t1, ar, bi, ALU.mult)
        P_tt(t2, ai, br, ALU.mult)
        P_tt(oi_, t1, t2, ALU.add)

    def lam_powers(src_d, F, npow):
        lam = palloc(3 * F)
        dma("sp", lam, src_d, (), (r_P,) + r_pro, misc_ch)
        lr, li, ls = lam[:, 0:F], lam[:, F:2 * F], lam[:, 2 * F:3 * F]
        dl = palloc(F); a = palloc(F); fb = palloc(F); t1 = palloc(F); t2 = palloc(F); t3 = palloc(F)
        P_act(dl, ls, AF.Exp)
        P_ts(lr, lr, -1e-4, ALU.min)
        P_tt(a, lr, dl, ALU.mult)
        P_tt(fb, li, dl, ALU.mult)
        P_ts(fb, fb, 1.0 / (2.0 * np.pi), ALU.mult)
        em1 = palloc(F); E = palloc(F)
        P_ts(t1, a, 0.2, ALU.mult, 1.0, ALU.add)
        for cdiv in (0.25, 1.0 / 3.0, 0.5):
            P_tt(t1, t1, a, ALU.mult)
            P_ts(t1, t1, cdiv, ALU.mult, 1.0, ALU.add)
        P_tt(em1, t1, a, ALU.mult)
        P_ts(E, em1, 1.0, ALU.add)
        c1 = palloc(F); s1 = palloc(F); sh = palloc(F)
        cis(c1, s1, fb, F, t1)
        P_ts(t2, fb, 0.5, ALU.mult)
        frac_round(t1, t2, F)
        P_act(sh, t1, AF.Sin, scale=TWO_PI)
        Pw = {0: None}
        p1r = palloc(F); p1i = palloc(F)
        P_tt(p1r, E, c1, ALU.mult)
        P_tt(p1i, E, s1, ALU.mult)
        Pw[1] = (p1r, p1i)
        for k in range(2, npow + 1):
            pr = palloc(F); pi_ = palloc(F)
            cmul(pr, pi_, Pw[k - 1][0], Pw[k - 1][1], p1r, p1i, t1, t2)
            Pw[k] = (pr, pi_)
        nr = palloc(F)
        P_tt(nr, em1, c1, ALU.mult)
        P_tt(t1, sh, sh, ALU.mult)
        P_ts(t1, t1, -2.0, ALU.mult)
        P_tt(nr, nr, t1, ALU.add)
        den = palloc(F)
        P_tt(t1, lr, lr, ALU.mult)
        P_tt(t2, li, li, ALU.mult)
        P_tt(den, t1, t2, ALU.add)
        S.add("dve", lambda e: e.reciprocal(out=den, in_=den), (r_P,), (r_P,) + r_pro)
        kr = palloc(F); ki = palloc(F)
        P_tt(t1, nr, lr, ALU.mult)
        P_tt(t2, p1i, li, ALU.mult)
        P_tt(t1, t1, t2, ALU.add)
        P_tt(kr, t1, den, ALU.mult)
        P_tt(t1, p1i, lr, ALU.mult)
        P_tt(t2, nr, li, ALU.mult)
        P_tt(t1, t1, t2, ALU.subtract)
        P_tt(ki, t1, den, ALU.mult)
        return dict(P=Pw, kr=kr, ki=ki, fb=fb, E=E, t=(t1, t2, t3))

    FC = 256
    LC = lam_powers(lamC_d[:].rearrange("p a f -> p (a f)"), FC, 3)
    bC = palloc(2 * FC)
    dma("sp", bC, bC_d[:].rearrange("p a f -> p (a f)"), (), (r_P,) + r_pro, misc_ch)
    bre, bim = bC[:, 0:FC], bC[:, FC:2 * FC]
    bbr = palloc(FC); bbi = palloc(FC)
    t1, t2, t3 = LC["t"]
    cmul(bbr, bbi, LC["kr"], LC["ki"], bre, bim, t1, t2)
    wr = palloc(FC); wi = palloc(FC)
    for k in range(TCH):
        pw = TCH - 1 - k
        if pw == 0:
            srcr, srci = bbr, bbi
        else:
            cmul(wr, wi, LC["P"][pw][0], LC["P"][pw][1], bbr, bbi, t1, t2)
            srcr, srci = wr, wi
        for reim, src in ((0, srcr), (1, srci)):
            for g2 in range(2):
                ts("dve", Wsi[:, :, k, reim, 64 * g2:64 * g2 + 64], src.rearrange("p (m n) -> p m n", m=4),
                   mask_g2[:, g2:g2 + 1], ALU.mult, (r_P, r_consts), (r_Wsi,))
    pro_off[0] = 0; pro_off[1] = 0
    FB = 16
    LB = lam_powers(lamB_d[:].rearrange("p a f -> p (a f)"), FB, 4)
    t1, t2, t3 = LB["t"]
    E = LB["E"]
    P_tt(t1, E, E, ALU.mult)
    tt("dve", rho[:], t1, t1, ALU.mult, (r_P,), (r_rho,))
    f4 = palloc(FB); fr = palloc(FB)
    P_ts(f4, LB["fb"], float(TCH), ALU.mult)
    frac_round(fr, f4, FB)
    NJ = NCH + 1
    HQ = FB // 2
    mark_ = list(pro_off)
    xt_ = palloc(HQ * NJ)
    xt3 = xt_.rearrange("p (q j) -> p q j", q=HQ)
    ang = palloc(HQ * NJ)
    for hq in range(2):
        for q in range(HQ):
            qq = hq * HQ + q
            ts("dve", xt3[:, q, :], iota, fr[:, qq:qq + 1], ALU.mult, (r_P, r_consts), (r_P,) + r_pro)
        frac_round(ang, xt_, HQ * NJ)
        act(sinT[:, hq * HQ:(hq + 1) * HQ, :].rearrange("p q j -> p (q j)"), ang, AF.Sin, (r_P,), (r_tab,), scale=TWO_PI)
        P_ts(xt_, xt_, 0.25, ALU.add)
        frac_round(ang, xt_, HQ * NJ)
        act(cosT[:, hq * HQ:(hq + 1) * HQ, :].rearrange("p q j -> p (q j)"), ang, AF.Sin, (r_P,), (r_tab,), scale=TWO_PI)
    pro_off[0], pro_off[1] = mark_
    FQ = 256
    bB = palloc(2 * FQ); cB = palloc(2 * FQ)
    dma("sp", bB, bB_d[:].rearrange("p a f -> p (a f)"), (), (r_P,) + r_pro, misc_ch)
    dma("sp", cB, cB_d[:].rearrange("p a f -> p (a f)"), (), (r_P,) + r_pro, misc_ch)

    def bc(v):
        return v.unsqueeze(2).to_broadcast([128, 16, 16])

    def v3(v):
        return v.rearrange("p (q c) -> p q c", q=16)

    u1 = palloc(FQ); u2 = palloc(FQ)

    def cmul_bc(or_, oi_, sr, si, xr_, xi_):
        P_tt(v3(u1), v3(xr_), bc(sr), ALU.mult)
        P_tt(v3(u2), v3(xi_), bc(si), ALU.mult)
        P_tt(or_, u1, u2, ALU.subtract)
        P_tt(v3(u1), v3(xi_), bc(sr), ALU.mult)
        P_tt(v3(u2), v3(xr_), bc(si), ALU.mult)
        P_tt(oi_, u1, u2, ALU.add)

    bbBr = palloc(FQ); bbBi = palloc(FQ)
    cmul_bc(bbBr, bbBi, LB["kr"], LB["ki"], bB[:, 0:FQ], bB[:, FQ:2 * FQ])
    Bblk = palloc(2 * 16 * 32)
    Cblk = palloc(2 * 16 * 4 * 32, pool=1)
    S.add("dve", lambda e: e.memset(Bblk, 0.0), (r_P,), (r_P,) + r_pro)
    S.add("dve", lambda e: e.memset(Cblk, 0.0), (r_P,), (r_P,) + r_pro)
    S.add("pool", lambda e: e.memset(Wso[:].rearrange("p a b c d -> p (a b c d)"), 0.0), (), (r_Wso,))
    Bblk4 = Bblk.rearrange("p (a q c) -> p a q c", a=2, q=16)
    Cblk5 = Cblk.rearrange("p (a q d c) -> p a q d c", a=2, q=16, d=4)
    for g2 in range(2):
        pl = slice(64 * g2, 64 * g2 + 64)
        for reim, src in ((0, bbBr), (1, bbBi)):
            S.add("dve", lambda e, o=Bblk4[pl, reim, :, 16 * g2:16 * g2 + 16], i_=v3(src)[pl]: e.tensor_copy(out=o, in_=i_),
                  (r_P,), (r_P,) + r_pro)
    cpr = palloc(FQ); cpi = palloc(FQ)
    cre, cim = cB[:, 0:FQ], cB[:, FQ:2 * FQ]
    for pw in range(0, TCH + 1):
        if pw == 0:
            srcr, srci = cre, cim
        else:
            cmul_bc(cpr, cpi, LB["P"][pw][0], LB["P"][pw][1], cre, cim)
            srcr, srci = cpr, cpi
        for g2 in range(2):
            pl = slice(64 * g2, 64 * g2 + 64)
            cs = slice(16 * g2, 16 * g2 + 16)
            if pw < TCH:
                S.add("dve", lambda e, o=Cblk5[pl, 0, :, pw, cs], i_=v3(srcr)[pl]: e.tensor_copy(out=o, in_=i_), (r_P,), (r_P,) + r_pro)
                S.add("dve", lambda e, o=Cblk5[pl, 1, :, pw, cs], i_=v3(srci)[pl]: e.tensor_scalar(out=o, in0=i_, scalar1=-1.0, scalar2=None, op0=ALU.mult),
                      (r_P,), (r_P,) + r_pro)
            if pw >= 1:
                t_ = pw - 1
                S.add("dve", lambda e, o=Wso[pl, :, t_, 0, cs], i_=v3(srcr)[pl]: e.tensor_copy(out=o, in_=i_), (r_P,), (r_Wso,))
                S.add("dve", lambda e, o=Wso[pl, :, t_, 1, cs], i_=v3(srci)[pl]: e.tensor_scalar(out=o, in0=i_, scalar1=-1.0, scalar2=None, op0=ALU.mult),
                      (r_P,), (r_Wso,))
    kb = 4
    KD = banks[kb][:].rearrange("p (m d c) -> p m d c", m=4, d=4)
    mms = []
    for q in range(16):
        m_, r_ = q // 4, q % 4
        for reim in range(2):
            mms.append(dict(out=KD[32 * r_:32 * r_ + 32, m_, :, :], lhsT=Bblk4[:, reim, q, :], rhs=Cblk5[:, reim, q, :, :],
                            start=(reim == 0), stop=(reim == 1), tp=(0, 32 * r_)))
    mmgroup(mms, (r_P,), (r_bank[kb],))
    BDf = Cblk[:, 0:2048]
    BDf4 = BDf.rearrange("p (m d c) -> p m d c", m=4, d=4)
    for r_ in range(4):
        ts("dve", BDf4[:, :, :, 32 * r_:32 * r_ + 32], KD, mask_r[:, r_:r_ + 1], ALU.mult, (r_bank[kb], r_consts, r_P), (r_P,) + r_pro)
    for m_ in range(4):
        stt(BDf4[:, m_, 0, :], ident, vecs[:, V_D + m_:V_D + m_ + 1], BDf4[:, m_, 0, :], ALU.mult, ALU.add,
            (r_P, r_consts, r_vecs), (r_P,) + r_pro)
    cp("dve", BD[:].rearrange("p m d c -> p (m d c)"), BDf, (r_P,), (r_BD,))

    dma("pool", Wg_b[:].rearrange("p a m c -> p (a m c)"), gate_d[:].rearrange("p a f -> p (a f)"), (), (r_Wg,), misc3_ch)
    for m_ in range(4):
        for k in range(4):
            ts("dve", Dconv[:, m_, k, :], ident, vecs[:, 56 + 4 * m_ + k:56 + 4 * m_ + k + 1], ALU.mult, (r_consts, r_vecs), (r_Dconv,))
    ts("dve", dv[:, DV_S5GH:DV_S5GH + 4], vecs[:, V_S5G:V_S5G + 4], 0.5, ALU.mult, (r_vecs,), (r_dv,))
    ts("dve", dv[:, DV_HBGLU:DV_HBGLU + 4], vecs[:, V_BGLU:V_BGLU + 4], 0.5, ALU.mult, (r_vecs,), (r_dv,))
    ts("dve", dv[:, DV_HBA:DV_HBA + 4], vecs[:, V_BA:V_BA + 4], 0.5, ALU.mult, (r_vecs,), (r_dv,))
    ts("dve", dv[:, DV_HBX:DV_HBX + 4], vecs[:, V_BX:V_BX + 4], 0.5, ALU.mult, (r_vecs,), (r_dv,))
    ev = palloc(4); pl_ = palloc(4)
    P_act(ev, vecs[:, V_LAM:V_LAM + 4], AF.Exp, scale=-1.0, extra_r=(r_vecs,))
    P_ts(pl_, ev, -0.25, ALU.mult, 1.0 / 3.0, ALU.add)
    P_tt(pl_, pl_, ev, ALU.mult)
    P_ts(pl_, pl_, -0.5, ALU.add)
    P_tt(pl_, pl_, ev, ALU.mult)
    P_ts(pl_, pl_, 1.0, ALU.add)
    P_tt(pl_, pl_, ev, ALU.mult)
    ts("dve", dv[:, DV_C:DV_C + 4], pl_, -8.0, ALU.mult, (r_P,), (r_dv,))
    ts("dve", dv[:, DV_HC:DV_HC + 4], pl_, -4.0, ALU.mult, (r_P,), (r_dv,))

    wlistM = []
    wlistF = []
    for _it in range(NT):
        wlistM += [(w_in_d[b_], 2048) for b_ in range(6)]
        wlistM += [(w_glu_d[:], 2048)]
        wlistM += [(w_out_d[b_], 2048) for b_ in range(4)]
        wlistF += [(w_gu_d[f_], 2048) for f_ in range(NF)]
        wlistF += [(w_dn_d[o_], 2816) for o_ in range(8)]
    wsM = {"issued": 0, "used": 0}
    wsF = {"issued": 0, "used": 0}
    scrM = nc.dram_tensor("scrM", [11, 128, 2048], BF16, kind="Internal").ap()
    scrF = nc.dram_tensor("scrF", [NF + 8, 128, 2816], BF16, kind="Internal").ap()

    def load_gen(ws, wl, ring_, rr, chs_sw, chs_hw, chs_st, ns, scr, per_tile):
        r_scr = [R(f"scr{id(scr)}_{i}") for i in range(per_tile)]

        def load_block(n):
            i_ = ws["used"]
            assert wl[i_][1] == n
            while ws["issued"] < min(len(wl), i_ + ns):
                j_ = ws["issued"]
                sj = j_ % ns
                nj = wl[j_][1]
                bj = j_ % per_tile
                if j_ < per_tile:
                    dma("pool", ring_[:, sj, 0:nj], wl[j_][0], (), (rr[sj],), chs_sw[sj])
                    dma("sp", scr[bj, :, 0:nj], ring_[:, sj, 0:nj], (rr[sj],), (r_scr[bj],), chs_st[sj])
                else:
                    dma("sp", ring_[:, sj, 0:nj], scr[bj, :, 0:nj], (r_scr[bj],), (rr[sj],), chs_hw[sj])
                ws["issued"] += 1
            ws["used"] += 1
            return i_ % ns
        return load_block

    loadM = load_gen(wsM, wlistM, ringM, r_ringM, ringM_ch, ringM_chH, ringM_chS, NSM, scrM, 11)
    loadF = load_gen(wsF, wlistF, ringF, r_ringF, ringF_ch, ringF_chH, ringF_chS, NSF, scrF, NF + 8)

    def rms(bankfn, src_regions, src_aps, nchunks, dim, sq_aps, sq_regs, rs_ap, rs_reg, stage_scale=1.0):
        for k in range(nchunks):
            act(sq_aps[k], src_aps[k], AF.Square, (src_regions[k],), (sq_regs[k],), scale=stage_scale)
        b = bankfn()
        mms = [dict(out=banks[b][:], lhsT=ones_b[:], rhs=sq_aps[k], start=(k == 0), stop=(k == nchunks - 1)) for k in range(nchunks)]
        mmgroup(mms, tuple(sq_regs[:nchunks]) + (r_ones,), (r_bank[b],))
        act(rs_ap, banks[b][:], AF.Ln, (r_bank[b], r_epsb), (rs_reg,), bias=epsb[:, 0:1], scale=1.0 / dim)
        act(rs_ap, rs_ap, AF.Exp, (rs_reg,), (rs_reg,), scale=-0.5)

    sqA = [hb[:, k, :] for k in range(8)]
    r_sq = r_h
    sq2 = [h2b[:, k, :] for k in range(8)]

    def stageA(it):
        xi = it % 2
        xb = xbuf[xi]
        rx = r_x[xi]
        seq_start = (it % TPS == 0)
        rms(bankA, rx, [xb[:, k, :] for k in range(8)], 8, float(D), sqA, r_sq, rstd[:], r_rstd)
        for k in range(8):
            stt(hb[:, k, :], xb[:, k, :], vecs[:, V_G1 + k:V_G1 + k + 1], rstd[:], ALU.mult, ALU.mult,
                (rx[k], r_vecs, r_rstd), (r_h[k],))
        yield 8
        for blk in range(6):
            s_ = loadM(2048)
            wv = ringM[:, s_, 0:2048].rearrange("p (o k m) -> p o k m", o=2, k=8)
            for o2 in range(2):
                oc = 2 * blk + o2
                b = bankA()
                mms = [dict(out=banks[b][:], lhsT=wv[:, o2, k, :], rhs=hb[:, k, :], start=(k == 0), stop=(k == 7)) for k in range(8)]
                mmgroup(mms, tuple(r_h) + (r_ringM[s_],), (r_bank[b],))
                m_ = oc % 4
                if oc < 4:
                    act(ub[:, m_, :], banks[b][:], AF.Copy, (r_bank[b],), (r_u[m_],))
                elif oc < 8:
                    if seq_start:
                        memset("pool", xrb[:, m_, 0:3], 0.0, (r_xr[m_],))
                    else:
                        cp("pool", xrb[:, m_, 0:3], xrb[:, m_, N:N + 3], (r_xr[m_],), (r_xr[m_],))
                    cp("dve", xrb[:, m_, 3:3 + N], banks[b][:], (r_bank[b], r_xr[m_]), (r_xr[m_],))
                else:
                    act(gy[:, m_, :], banks[b][:], AF.Gelu_apprx_tanh, (r_bank[b],), (r_gy[m_],))
                yield 8
        for m_ in range(4):
            dump(it, m_, ub[:, m_, :], (r_u[m_],))
        xhb = hb[:, 4:8, :]
        if seq_start and it > 0:
            for m_ in range(4):
                memset("dve", hstate[:, m_:m_ + 1], 0.0, (r_hstate[m_],))
        for half in range(2):
            xh = [ltmp[:, 512 * j:512 * j + 512] for j in range(2)]
            tr = [ltmp[:, 1024 + 512 * j:1024 + 512 * j + 512] for j in range(2)]
            ti = [ltmp[:, 2048 + 512 * j:2048 + 512 * j + 512] for j in range(2)]
            r_xh = [R2(r_lt[0], r_lt[1]), R2(r_lt[2], r_lt[3])]
            r_tr = [R2(r_lt[4], r_lt[5]), R2(r_lt[6], r_lt[7])]
            r_ti = [R2(r_lt[8], r_lt[9]), R2(r_lt[10], r_lt[11])]
            for j in range(2):
                m_ = 2 * half + j
                b = bankA()
                mms = [dict(out=banks[b][:], lhsT=Dconv[:, m_, k, :], rhs=xrb[:, m_, k:k + N], start=(k == 0), stop=(k == 3)) for k in range(4)]
                mmgroup(mms, (r_xr[m_], r_Dconv), (r_bank[b],))
                act(xh[j], banks[b][:], AF.Identity, (r_bank[b], r_vecs), (r_xh[j],), bias=vecs[:, V_CB + m_:V_CB + m_ + 1])
                cp("pool", xhb[:, m_, :], xh[j], (r_xh[j],), (r_h[4 + m_],))
                for gi, (dst, rdst, hbcol) in enumerate(((tr, r_tr, DV_HBA), (ti, r_ti, DV_HBX))):
                    b2 = bankA()
                    mmgroup([dict(out=banks[b2][:], lhsT=Wg_b[:, gi, m_, :], rhs=xhb[:, m_, :], start=True, stop=True)],
                            (r_h[4 + m_], r_Wg), (r_bank[b2],))
                    act(dst[j], banks[b2][:], AF.Tanh, (r_bank[b2], r_dv), (rdst[j],),
                        bias=dv[:, hbcol + m_:hbcol + m_ + 1], scale=0.5)
                yield 6
            for j in range(2):
                m_ = 2 * half + j
                a_ = tr[j]
                act(a_, tr[j], AF.Exp, (r_tr[j], r_dv), (r_tr[j],), bias=dv[:, DV_HC + m_:DV_HC + m_ + 1], scale=dv[:, DV_HC + m_:DV_HC + m_ + 1])
                w1 = sgA[j]; rw1 = r_sgA[j]
                act(w1, a_, AF.Square, (r_tr[j],), (rw1,))
                act(w1, w1, AF.Ln, (rw1, r_epsb), (rw1,), bias=epsb[:, 2:3], scale=-0.25)
                act(w1, w1, AF.Exp, (rw1,), (rw1,), scale=0.5)
                stt(ti[j], ti[j], 1.0, xh[j], ALU.add, ALU.mult, (r_ti[j], r_xh[j]), (r_ti[j],))
                tt("dve", ti[j], ti[j], w1, ALU.mult, (r_ti[j], rw1), (r_ti[j],))
                scan(xh[j], a_, ti[j], hstate[:, m_:m_ + 1], (r_tr[j], r_ti[j], r_hstate[m_]), (r_xh[j],))
                cp("dve", hstate[:, m_:m_ + 1], xh[j][:, N - 1:N], (r_xh[j],), (r_hstate[m_],))
                tt("dve", lo[:, m_, :], xh[j], gy[:, m_, :], ALU.mult, (r_xh[j], r_gy[m_]), (r_lo[m_],))
                yield 1
        for m_ in range(4):
            dump(it, 8 + m_, lo[:, m_, :], (r_lo[m_],))
        rms(bankA, r_lo, [lo[:, k, :] for k in range(4)], 4, 512.0, sqA, r_sq, rstd[:], r_rstd)
        for m_ in range(4):
            stt(mix[:, 4 + m_, :], lo[:, m_, :], vecs[:, V_LRUG + m_:V_LRUG + m_ + 1], rstd[:], ALU.mult, ALU.mult,
                (r_lo[m_], r_vecs, r_rstd), (r_mix[4 + m_],))
        yield 4
        Sc = Sbuf
        rS = r_S
        if seq_start:
            memset("pool", sinit[:], 0.0, (r_sinit,))
        for q in range(16):
            if seq_start:
                memset("pool", Sc[:, q, :, 0:1], 0.0, (rS[q],))
            else:
                cp("pool", Sc[:, q, :, 0:1], Sc[:, q, :, NCH:NCH + 1], (rS[q],), (rS[q],))

        def tbuf(idx, par):
            o_ = (2 * idx + par) * 256
            return ltmp[:, o_:o_ + 256].rearrange("p (a j) -> p a j", a=2), r_lt[2 * idx + par]
        def zhalf(par):
            return banks[6 + par][:, 0:256].rearrange("p (a j) -> p a j", a=2), r_bank[6 + par]

        def s0(q0):
            mms = []
            for reim in range(2):
                for k in range(TCH):
                    for q in (q0, q0 + 1):
                        m_, r_ = q // 4, q % 4
                        Z, rz = zhalf(q % 2)
                        mms.append(dict(out=Z[:, reim, :], lhsT=Wsi[32 * r_:32 * r_ + 32, m_, k, reim, :],
                                        rhs=ub[32 * r_:32 * r_ + 32, m_, k::TCH], start=(k == 0), stop=(k == TCH - 1),
                                        tp=(32 * r_, 0)))
            mmgroup(mms, (r_u[q0 // 4], r_Wsi), (r_bank[6], r_bank[7]))

        def s1(q):
            Z, rz = zhalf(q % 2)
            Zs, rZs = tbuf(0, q % 2)
            act(Zs, Z, AF.Copy, (rz,), (rZs,))

        def s2(q):
            Zs, rZs = tbuf(0, q % 2)
            T1, rT1 = tbuf(1, q % 2)
            M_, rM = tbuf(2, q % 2)
            cosb = cosT[:, q, 0:NCH].unsqueeze(1).to_broadcast([128, 2, NCH])
            sinq = sinT[:, q, 0:NCH]
            tt("pool", T1, Zs, cosb, ALU.mult, (rZs, r_tab), (rT1,))
            tt("pool", M_[:, 0, :], Zs[:, 1, :], sinq, ALU.mult, (rZs, r_tab), (rM,))
            tt("pool", M_[:, 1, :], Zs[:, 0, :], sinq, ALU.mult, (rZs, r_tab), (rM,))

        def s3(q):
            T1, rT1 = tbuf(1, q % 2)
            M_, rM = tbuf(2, q % 2)
            Wn, rWn = tbuf(3, q % 2)
            Wo, rWo = tbuf(4, q % 2)
            tt("dve", Wn[:, 0, :], T1[:, 0, :], M_[:, 0, :], ALU.add, (rT1, rM), (rWn,))
            tt("dve", Wn[:, 1, :], T1[:, 1, :], M_[:, 1, :], ALU.subtract, (rT1, rM), (rWn,))
            rho_b = rho[:, q:q + 1].to_broadcast([128, NCH])
            for reim in range(2):
                scan(Wo[:, reim, :], rho_b, Wn[:, reim, :], sinit[:, q, reim:reim + 1], (rWn, r_rho, r_sinit), (rWo,))

        def s4(q):
            Wo, rWo = tbuf(4, q % 2)
            P1, rP1 = tbuf(5, q % 2)
            Mp, rMp = tbuf(6, q % 2)
            cosb = cosT[:, q, 0:NCH].unsqueeze(1).to_broadcast([128, 2, NCH])
            sinq = sinT[:, q, 0:NCH]
            cp("pool", wlast[:, q, :], Wo[:, :, NCH - 1], (rWo,), (r_wlast,))
            tt("pool", P1, Wo, cosb, ALU.mult, (rWo, r_tab), (rP1,))
            tt("pool", Mp[:, 0, :], Wo[:, 1, :], sinq, ALU.mult, (rWo, r_tab), (rMp,))
            tt("pool", Mp[:, 1, :], Wo[:, 0, :], sinq, ALU.mult, (rWo, r_tab), (rMp,))

        def s5(q):
            P1, rP1 = tbuf(5, q % 2)
            Mp, rMp = tbuf(6, q % 2)
            tt("dve", Sc[:, q, 0, 1:NCH + 1], P1[:, 0, :], Mp[:, 0, :], ALU.subtract, (rP1, rMp), (rS[q],))
            tt("dve", Sc[:, q, 1, 1:NCH + 1], P1[:, 1, :], Mp[:, 1, :], ALU.add, (rP1, rMp), (rS[q],))

        def s6(m_):
            yb = bankA()
            Y = banks[yb][:].rearrange("p (t j) -> p t j", t=TCH)
            for t_ in range(TCH):
                mms = []
                for d_ in range(t_ + 1):
                    mms.append(dict(out=Y[:, t_, :], lhsT=BD[:, m_, d_, :], rhs=ub[:, m_, (t_ - d_)::TCH], start=(d_ == 0), stop=False))
                for r_ in range(4):
                    q = 4 * m_ + r_
                    for reim in range(2):
                        mms.append(dict(out=Y[32 * r_:32 * r_ + 32, t_, :], lhsT=Wso[:, q, t_, reim, :], rhs=Sc[:, q, reim, 0:NCH],
                                        start=False, stop=(r_ == 3 and reim == 1), tp=(0, 32 * r_)))
                mmgroup(mms, (r_u[m_], r_BD, r_Wso) + tuple(rS[4 * m_:4 * m_ + 4]), (r_bank[yb],))
            zview = zf[:, m_, :].rearrange("p (j t) -> p t j", t=TCH)
            act(zview, Y, AF.Gelu_apprx_tanh, (r_bank[yb],), (r_zf[m_],))
            cp("pool", zb[:, m_, :], zf[:, m_, :], (r_zf[m_],), (r_zb[m_],))
            dump(it, 4 + m_, zf[:, m_, :], (r_zf[m_],))

        for w in range(-1, 16 + 5):
            if 0 <= w < 16:
                s1(w)
            if 0 <= w + 1 < 16 and (w + 1) % 2 == 0:
                s0(w + 1)
            if 0 <= w - 1 < 16:
                s2(w - 1)
            if 0 <= w - 2 < 16:
                s3(w - 2)
            if 0 <= w - 3 < 16:
                s4(w - 3)
            if 0 <= w - 4 < 16:
                s5(w - 4)
                if (w - 4) % 4 == 3:
                    s6((w - 4) // 4)
            yield 3
        if not ((it + 1) % TPS == 0):
            c128 = cosT[:, :, NCH]; s128 = sinT[:, :, NCH]
            ta = ctmp[:, 0:16]; tb_ = ctmp[:, 16:32]; tc_ = ctmp[:, 32:48]; td = ctmp[:, 48:64]
            rc = r_ctmp
            tt("dve", ta, wlast[:, :, 0], c128, ALU.mult, (r_wlast, r_tab), (rc,))
            tt("dve", tb_, wlast[:, :, 1], s128, ALU.mult, (r_wlast, r_tab), (rc,))
            tt("dve", tc_, wlast[:, :, 0], s128, ALU.mult, (r_wlast, r_tab), (rc,))
            tt("dve", td, wlast[:, :, 1], c128, ALU.mult, (r_wlast, r_tab), (rc,))
            tt("dve", sinit[:, :, 0], ta, tb_, ALU.subtract, (rc,), (r_sinit,))
            tt("dve", sinit[:, :, 1], tc_, td, ALU.add, (rc,), (r_sinit,))
        s_ = loadM(2048)
        wv = ringM[:, s_, 0:2048].rearrange("p (k n) -> p k n", k=4)
        for oc in range(4):
            b = bankA()
            mms = [dict(out=banks[b][:], lhsT=wv[:, k, 128 * oc:128 * oc + 128], rhs=zb[:, k, :], start=(k == 0), stop=(k == 3)) for k in range(4)]
            mmgroup(mms, tuple(r_zb) + (r_ringM[s_],), (r_bank[b],))
            tg = sgA[oc % 2]; rtg = r_sgA[oc % 2]
            act(tg, banks[b][:], AF.Tanh, (r_bank[b], r_dv), (rtg,), bias=dv[:, DV_HBGLU + oc:DV_HBGLU + oc + 1], scale=0.5)
            stt(zf[:, oc, :], tg, 1.0, zf[:, oc, :], ALU.add, ALU.mult, (rtg, r_zf[oc]), (r_zf[oc],))
            yield 4
        rms(bankA, r_zf, [zf[:, k, :] for k in range(4)], 4, 512.0, sqA, r_sq, rstd[:], r_rstd, stage_scale=0.5)
        for m_ in range(4):
            stt(mix[:, m_, :], zf[:, m_, :], dv[:, DV_S5GH + m_:DV_S5GH + m_ + 1], rstd[:], ALU.mult, ALU.mult,
                (r_zf[m_], r_dv, r_rstd), (r_mix[m_],))
        yield 4
        for k in range(8):
            dump(it, 12 + k, mix[:, k, :], (r_mix[k],))
        for blk in range(4):
            s_ = loadM(2048)
            wv = ringM[:, s_, 0:2048].rearrange("p (o k m) -> p o k m", o=2, k=8)
            for o2 in range(2):
                oc = 2 * blk + o2
                b = bankA()
                mms = [dict(out=banks[b][:], lhsT=wv[:, o2, k, :], rhs=mix[:, k, :], start=(k == 0), stop=(k == 7)) for k in range(8)]
                mmgroup(mms, tuple(r_mix) + (r_ringM[s_],), (r_bank[b],))
                tt("dve", xb[:, oc, :], xb[:, oc, :], banks[b][:], ALU.add, (rx[oc], r_bank[b]), (rx[oc],))
                yield 8
        for k in range(8):
            dump(it, 20 + k, xb[:, k, :], (rx[k],))
        rms(bankA, rx, [xb[:, k, :] for k in range(8)], 8, float(D), sq2, r_h2, rstd[:], r_rstd)
        for k in range(8):
            stt(h2b[:, k, :], xb[:, k, :], vecs[:, V_G2 + k:V_G2 + k + 1], rstd[:], ALU.mult, ALU.mult,
                (rx[k], r_vecs, r_rstd), (r_h2[k],))
        yield 8

    def stageB(it):
        xi = it % 2
        xb = xbuf[xi]
        rx = r_x[xi]
        t0 = it * N
        for f in range(NF):
            s_ = loadF(2048)
            wv = ringF[:, s_, 0:2048].rearrange("p (g k m) -> p g k m", g=2, k=8)
            bg = bankB(); bu = bankB()
            mmgroup([dict(out=banks[bg][:], lhsT=wv[:, 0, k, :], rhs=h2b[:, k, :], start=(k == 0), stop=(k == 7)) for k in range(8)],
                    tuple(r_h2) + (r_ringF[s_],), (r_bank[bg],))
            mmgroup([dict(out=banks[bu][:], lhsT=wv[:, 1, k, :], rhs=h2b[:, k, :], start=(k == 0), stop=(k == 7)) for k in range(8)],
                    tuple(r_h2) + (r_ringF[s_],), (r_bank[bu],))
            sg = sgb[f % 2]; rsg = r_sg[f % 2]
            act(sg[:], banks[bg][:], AF.Silu, (r_bank[bg],), (rsg,))
            tt("dve", hh[:, f, :], sg[:], banks[bu][:], ALU.mult, (rsg, r_bank[bu]), (r_hh[f],))
            yield 16
        for oc in range(8):
            s_ = loadF(2816)
            wv = ringF[:, s_, 0:2816].rearrange("p (f m) -> p f m", f=NF)
            b = bankB()
            mmgroup([dict(out=banks[b][:], lhsT=wv[:, f, :], rhs=hh[:, f, :], start=(f == 0), stop=(f == NF - 1)) for f in range(NF)],
                    tuple(r_hh) + (r_ringF[s_],), (r_bank[b],))
            tt("dve", xb[:, oc, :], xb[:, oc, :], banks[b][:], ALU.add, (rx[oc], r_bank[b]), (rx[oc],))
            yield 22
        for k in range(8):
            dump(it, 28 + k, xb[:, k, :], (rx[k],))
        sqF = [hh[:, k, :] for k in range(8)]
        rms(bankB, rx, [xb[:, k, :] for k in range(8)], 8, float(D), sqF, r_hh, sgb[0][:], r_sg[0])
        for k in range(8):
            stt(xb[:, k, :], xb[:, k, :], vecs[:, V_GF + k:V_GF + k + 1], sgb[0][:], ALU.mult, ALU.mult,
                (rx[k], r_vecs, r_sg[0]), (rx[k],))
        dma("sp", out_d[:, :, t0:t0 + N], xb[:], tuple(rx), (r_out[xi],), o_ch[xi])
        if it + 2 < NT:
            dma("sp", xb[:], x_d[:, :, t0 + 2 * N:t0 + 3 * N], (), tuple(rx), x_ch[xi])
        yield 8

    sgA = [ctmpA[:, 0:512], ctmpA[:, 512:1024]]
    dma("sp", xbuf[0][:], x_d[:, :, 0:N], (), tuple(r_x[0]), x_ch[0])
    dma("sp", xbuf[1][:], x_d[:, :, N:2 * N], (), tuple(r_x[1]), x_ch[1])
    S.cur_prio = 0
    for _ in stageA(0):
        pass
    for it in range(NT):
        S.cur_prio = 2 * it + 3
        for _ in stageB(it):
            pass
        if it + 1 < NT:
            S.cur_prio = 2 * (it + 1)
            for _ in stageA(it + 1):
                pass
    S.cur_prio = 10 ** 6

    S.add("sp", lambda e: e.nop(), tuple(r_out) + (r_dbg,), ())

    with nc.Block() as block:
        S.emit(nc, block, eng_sems, None)
    build_program.last_makespan = getattr(S, "makespan", None)
    for cm in reversed(ctx):
        cm.__exit__(None, None, None)
    return nc


def _prep_shared(inp):
    f = np.float32
    g = lambda k: np.asarray(inp[k], dtype=f)
    sh = {}
    w_in = g("w_in")[0]
    sh["w_in"] = np.ascontiguousarray(w_in.reshape(8, 128, 6, 2, 128).transpose(2, 1, 3, 0, 4).reshape(6, 128, 2048))
    w_glu = g("s5_w_glu")[0]
    sh["w_glu"] = np.ascontiguousarray(w_glu.reshape(4, 128, 512).transpose(1, 0, 2).reshape(128, 2048))
    w_out = g("w_out")[0]
    sh["w_out"] = np.ascontiguousarray(w_out.reshape(8, 128, 4, 2, 128).transpose(2, 1, 3, 0, 4).reshape(4, 128, 2048))
    wg = g("w_gate")[0].reshape(8, 128, NF, 128)
    wu = g("w_up")[0].reshape(8, 128, NF, 128)
    gu = np.stack([wg, wu], axis=0)
    sh["w_gu"] = np.ascontiguousarray(gu.transpose(3, 2, 0, 1, 4).reshape(NF, 128, 2048))
    wd = g("w_down")[0].reshape(NF, 128, 8, 128)
    sh["w_dn"] = np.ascontiguousarray(wd.transpose(2, 1, 0, 3).reshape(8, 128, 2816))
    vecs = np.zeros((128, 72), f)
    vecs[:, 0:8] = g("norm1_g")[0].reshape(8, 128).T
    vecs[:, 8:16] = g("norm2_g")[0].reshape(8, 128).T
    vecs[:, 16:24] = g("final_g").reshape(8, 128).T
    vecs[:, 24:28] = g("s5_out_g")[0].reshape(4, 128).T
    vecs[:, 28:32] = g("lru_out_g")[0].reshape(4, 128).T
    vecs[:, 32:36] = g("s5_b_glu")[0].reshape(4, 128).T
    vecs[:, 36:40] = g("lru_conv_b")[0].reshape(4, 128).T
    vecs[:, 40:44] = g("lru_b_a")[0].reshape(4, 128).T
    vecs[:, 44:48] = g("lru_b_x")[0].reshape(4, 128).T
    vecs[:, 48:52] = g("lru_lambda")[0].reshape(4, 128).T
    vecs[:, 52:56] = g("s5_d")[0].reshape(4, 128).T
    cw = g("lru_conv_w")[0]
    cwl = cw.reshape(4, 4, 128).transpose(2, 1, 0)
    vecs[:, 56:72] = cwl.reshape(128, 16)
    sh["vecs"] = vecs
    lam_re = g("s5_lambda_re")[0]; lam_im = g("s5_lambda_im")[0]; ls = g("s5_log_step")[0]
    def layB(a):
        return a.reshape(16, 2, 64).transpose(1, 2, 0).reshape(128, 16)
    lsb = np.broadcast_to(ls[:, None], (32, 64))
    sh["lamB"] = np.ascontiguousarray(np.stack([layB(lam_re), layB(lam_im), layB(lsb)], axis=1))
    def layC(a):
        a4 = a.reshape(4, 4, 2, 64)
        a5 = np.broadcast_to(a4[:, :, :, None, :], (4, 4, 2, 16, 64))
        return a5.transpose(1, 2, 3, 0, 4).reshape(128, 256)
    sh["lamC"] = np.ascontiguousarray(np.stack([layC(lam_re), layC(lam_im), layC(lsb)], axis=1))
    b_re = g("s5_b_re")[0]; b_im = g("s5_b_im")[0]
    def layC_b(a):
        a5 = a.reshape(4, 4, 2, 64, 16)
        return a5.transpose(1, 2, 4, 0, 3).reshape(128, 256)
    sh["bC"] = np.ascontiguousarray(np.stack([layC_b(b_re), layC_b(b_im)], axis=1))
    def layB_b(a):
        return a.reshape(16, 2, 64, 16).transpose(1, 2, 0, 3).reshape(128, 256)
    sh["bB"] = np.ascontiguousarray(np.stack([layB_b(b_re), layB_b(b_im)], axis=1))
    c_re = g("s5_c_re")[0]; c_im = g("s5_c_im")[0]
    def layB_c(a):
        return a.reshape(16, 2, 16, 64).transpose(1, 3, 0, 2).reshape(128, 256)
    sh["cB"] = np.ascontiguousarray(np.stack([layB_c(c_re), layB_c(c_im)], axis=1))
    def bdiag(w):
        o = np.zeros((128, 4, 128), f)
        for h in range(8):
            m_, hh_ = h // 2, h % 2
            o[64 * hh_:64 * hh_ + 64, m_, 64 * hh_:64 * hh_ + 64] = w[h]
        return o.reshape(128, 512)
    sh["gatew"] = np.ascontiguousarray(np.stack([bdiag(g("lru_w_a")[0]), bdiag(g("lru_w_x")[0])], axis=1))
    consts = np.zeros((128, 263), f)
    consts[:, 0:128] = np.eye(128, dtype=f)
    consts[:, 128:257] = np.arange(129, dtype=f)[None, :]
    p = np.arange(128)
    for v in range(2):
        consts[:, 257 + v] = ((p % 32) // 16 == v).astype(f)
    for v in range(4):
        consts[:, 259 + v] = (p // 32 == v).astype(f)
    sh["consts"] = consts
    return sh


def _prep_x(x, core):
    xs = np.asarray(x[2 * core:2 * core + 2], dtype=np.float32).reshape(NTOK, 8, 128)
    return np.ascontiguousarray(xs.transpose(2, 1, 0))


_CACHE = {}


def kernel(**inputs):
    sh = _prep_shared(inputs)
    if "nc" not in _CACHE:
        _CACHE["nc"] = build_program()
    nc = _CACHE["nc"]
    in_maps = []
    for c in range(NCORES):
        m = dict(sh)
        m["x"] = _prep_x(inputs["x"], c)
        in_maps.append(m)
    res = run_bass_kernel_spmd(nc, in_maps, core_ids=list(range(NCORES)))
    outs = []
    for c in range(NCORES):
        o = np.asarray(res.results[c]["out"], dtype=np.float32)
        outs.append(o.transpose(2, 1, 0).reshape(2, L, D))
    return np.concatenate(outs, axis=0)
```

```python
import numpy as np
import concourse.bass as bass
import concourse.mybir as mybir
from concourse.bass_utils import run_bass_kernel_spmd

F32 = mybir.dt.float32
BF16 = mybir.dt.bfloat16
I32 = mybir.dt.int32
ALU = mybir.AluOpType
AF = mybir.ActivationFunctionType

NCORES = 8
D = 1024
L = 2048
NTOK = 4096
N = 512
NT = NTOK // N
TPS = L // N
TCH = 4
NCH = N // TCH
DFF = 2816
NF = DFF // 128
EPS = 1e-6
TWO_PI = 6.2831845
SLOT = 2816


class Region:
    __slots__ = ("name", "writer", "readers")

    def __init__(self, name):
        self.name = name
        self.writer = None
        self.readers = []


class Chan:
    def __init__(self, sem):
        self.sem = sem
        self.count = 0


class Op:
    __slots__ = ("idx", "eng", "fn", "deps", "signals", "tick", "sem", "is_dma", "chan", "dur", "prio", "odeps", "succs",
                 "ndeps", "ready_t", "fin")

    def __init__(self, idx, eng, fn, is_dma=False, chan=None):
        self.idx = idx
        self.eng = eng
        self.fn = fn
        self.deps = {}
        self.signals = False
        self.tick = None
        self.sem = None
        self.is_dma = is_dma
        self.chan = chan
        self.dur = 100.0
        self.prio = 0
        self.odeps = None
        self.succs = []
        self.ndeps = 0
        self.ready_t = 0.0
        self.fin = 0.0


class Sched:
    ENGS = ("pe", "act", "dve", "pool", "sp")

    def __init__(self):
        self.ops = []
        self.cur_prio = -1

    @staticmethod
    def _flat(x):
        out = []
        for i in x:
            if isinstance(i, (tuple, list)):
                out.extend(Sched._flat(i))
            else:
                out.append(i)
        return out

    def add(self, eng, fn, reads=(), writes=(), is_dma=False, chan=None, dur=100.0):
        op = Op(len(self.ops), eng, fn, is_dma, chan)
        op.dur = dur
        op.prio = self.cur_prio
        reads = self._flat(reads)
        writes = self._flat(writes)
        for r in reads:
            if r.writer is not None:
                op.deps[r.writer] = "raw"
            r.readers.append(op)
        for w in writes:
            if w.writer is not None and w.writer not in op.deps:
                op.deps[w.writer] = "waw"
            for rd in w.readers:
                if rd is not op and rd not in op.deps:
                    op.deps[rd] = "war"
            w.writer = op
            w.readers = []
        op.odeps = list(op.deps)
        for d in list(op.deps):
            if (not d.is_dma) and (not op.is_dma) and d.eng == op.eng and op.deps[d] != "raw":
                del op.deps[d]
        for d in op.deps:
            d.signals = True
        self.ops.append(op)
        return op

    def schedule(self):
        XLAT = 150.0
        for op in self.ops:
            op.succs = []
            op.ndeps = len(op.odeps)
            op.ready_t = 0.0
        for op in self.ops:
            for d in op.odeps:
                d.succs.append(op)
        ready = {e: [] for e in self.ENGS}
        free = {e: 0.0 for e in self.ENGS}
        for op in self.ops:
            if op.ndeps == 0:
                ready[op.eng].append(op)
        order = []
        n = len(self.ops)
        while len(order) < n:
            best = None
            bkey = None
            for e in self.ENGS:
                if not ready[e]:
                    continue
                fe = free[e]
                o = min(ready[e], key=lambda o_: (max(o_.ready_t, fe), o_.prio, o_.idx))
                key = (max(o.ready_t, fe), o.prio, o.idx)
                if bkey is None or key < bkey:
                    best, bkey = o, key
            st_ = bkey[0]
            e = best.eng
            ready[e].remove(best)
            if best.is_dma:
                free[e] = st_ + 60.0
                best.fin = st_ + best.dur
            else:
                best.fin = st_ + best.dur
                free[e] = best.fin
            order.append(best)
            for s_ in best.succs:
                lat = best.fin + (XLAT if (s_.eng != e or best.is_dma) else 30.0)
                if lat > s_.ready_t:
                    s_.ready_t = lat
                s_.ndeps -= 1
                if s_.ndeps == 0:
                    ready[s_.eng].append(s_)
        self.ops = order
        self.makespan = max(o.fin for o in order)

    def emit(self, nc, block, eng_sems, engines):
        self.schedule()
        cnt = {e: 0 for e in self.ENGS}
        for op in self.ops:
            if op.is_dma:
                op.chan.count += 16
                op.tick = op.chan.count
                op.sem = op.chan.sem
            elif op.signals:
                cnt[op.eng] += 1
                op.tick = cnt[op.eng]
                op.sem = eng_sems[op.eng]

        def run(engname, e):
            waited = {}
            for op in self.ops:
                if op.eng != engname:
                    continue
                need = {}
                for d in op.deps:
                    k = id(d.sem)
                    if k not in need or need[k][1] < d.tick:
                        need[k] = (d.sem, d.tick)
                for k, (sem, v) in need.items():
                    if waited.get(k, 0) >= v:
                        continue
                    e.wait_ge(sem, v)
                    waited[k] = v
                ins = op.fn(e)
                if op.is_dma:
                    ins.then_inc(op.sem, 16)
                elif op.signals:
                    ins.then_inc(op.sem, 1)

        @block.tensor
        def _(e):
            run("pe", e)

        @block.scalar
        def _(e):
            run("act", e)

        @block.vector
        def _(e):
            run("dve", e)

        @block.gpsimd
        def _(e):
            run("pool", e)

        @block.sync
        def _(e):
            run("sp", e)


def build_program(debug=None):
    nc = bass.Bass("TRN2", target_bir_lowering=False)
    S = Sched()

    def din(name, shape, dt=F32):
        return nc.dram_tensor(name, list(shape), dt, kind="ExternalInput").ap()

    x_d = din("x", (128, 8, NTOK))
    out_d = nc.dram_tensor("out", [128, 8, NTOK], F32, kind="ExternalOutput").ap()
    w_in_d = din("w_in", (6, 128, 2048))
    w_glu_d = din("w_glu", (128, 2048))
    w_out_d = din("w_out", (4, 128, 2048))
    w_gu_d = din("w_gu", (NF, 128, 2048))
    w_dn_d = din("w_dn", (8, 128, 2816))
    vec_d = din("vecs", (128, 72))
    lamB_d = din("lamB", (128, 3, 16))
    lamC_d = din("lamC", (128, 3, 256))
    bC_d = din("bC", (128, 2, 256))
    bB_d = din("bB", (128, 2, 256))
    cB_d = din("cB", (128, 2, 256))
    gate_d = din("gatew", (128, 2, 512))
    const_d = din("consts", (128, 128 + 129 + 2 + 4))
    dbg_d = None
    if debug is not None:
        dbg_d = nc.dram_tensor("dbg", list(debug), F32, kind="ExternalOutput").ap()

    ctx = []

    def sb(name, shape, dt=F32):
        cm = nc.sbuf_tensor("s_" + name, list(shape), dt)
        t = cm.__enter__()
        ctx.append(cm)
        return t

    def ps(name, shape, dt=F32):
        cm = nc.psum_tensor(name, list(shape), dt)
        t = cm.__enter__()
        ctx.append(cm)
        return t

    def sem(name):
        cm = nc.semaphore(name)
        s = cm.__enter__()
        ctx.append(cm)
        return s

    R = Region
    R2 = lambda a_, b_: (a_, b_)

    vecs = sb("vecs", (128, 72)); r_vecs = R("vecs")
    consts = sb("consts", (128, 263)); r_consts = R("consts")
    ident = consts[:, 0:128]
    iota = consts[:, 128:257]
    mask_g2 = consts[:, 257:259]
    mask_r = consts[:, 259:263]
    V_G1, V_G2, V_GF = 0, 8, 16
    V_S5G, V_LRUG, V_BGLU, V_CB, V_BA, V_BX, V_LAM, V_D = 24, 28, 32, 36, 40, 44, 48, 52
    dv = sb("dv", (128, 32)); r_dv = R("dv")
    DV_S5GH, DV_HBGLU, DV_HBA, DV_HBX, DV_C, DV_HC = 0, 4, 8, 12, 16, 20
    ones_b = sb("ones_b", (128, 128), BF16); r_ones = R("ones")
    epsb = sb("epsb", (128, 4)); r_epsb = R("epsb")

    cosT = sb("cosT", (128, 16, 129)); sinT = sb("sinT", (128, 16, 129)); r_tab = R("tab")
    rho = sb("rho", (128, 16)); r_rho = R("rho")
    Wsi = sb("Wsi", (128, 4, 4, 2, 128), BF16); r_Wsi = R("Wsi")
    Wso = sb("Wso", (128, 16, 4, 2, 32), BF16); r_Wso = R("Wso")
    BD = sb("BD", (128, 4, 4, 128), BF16); r_BD = R("BD")
    Dconv = sb("Dconv", (128, 4, 4, 128), BF16); r_Dconv = R("Dconv")
    Wg_b = sb("Wg_b", (128, 2, 4, 128), BF16); r_Wg = R("Wg")
    hstate = sb("hstate", (128, 4)); r_hstate = [R(f"hst{m}") for m in range(4)]
    sinit = sb("sinit", (128, 16, 2)); r_sinit = R("sinit")
    wlast = sb("wlast", (128, 16, 2)); r_wlast = R("wlast")
    ctmp = sb("ctmp", (128, 64)); r_ctmp = R("ctmp")

    xbuf = [sb(f"xbuf{i}", (128, 8, N)) for i in range(2)]
    r_x = [[R(f"x{i}_{k}") for k in range(8)] for i in range(2)]
    rstd = sb("rstd", (128, N)); r_rstd = R("rstd")
    hb = sb("hb", (128, 8, N), BF16); r_h = [R(f"h{k}") for k in range(8)]
    h2b = sb("h2b", (128, 8, N), BF16); r_h2 = [R(f"h2_{k}") for k in range(8)]
    ub = sb("ub", (128, 4, N), BF16); r_u = [R(f"u{k}") for k in range(4)]
    xrb = sb("xrb", (128, 4, N + 4), BF16); r_xr = [R(f"xr{k}") for k in range(4)]
    gy = sb("gy", (128, 4, N)); r_gy = [R(f"gy{k}") for k in range(4)]
    lo = gy; r_lo = r_gy
    ltmp = sb("ltmp", (128, 3584)); r_lt = [R(f"lt{k}") for k in range(14)]
    r_zh = [R("zh0"), R("zh1")]
    zf = sb("zf", (128, 4, N)); r_zf = [R(f"zf{k}") for k in range(4)]
    zb = hb[:, 4:8, :]; r_zb = r_h[4:8]
    mix = sb("mix", (128, 8, N), BF16); r_mix = [R(f"mix{k}") for k in range(8)]
    Sbuf = sb("Sbuf", (128, 16, 2, NCH + 1), BF16); r_S = [R(f"S{q}") for q in range(16)]
    hh = sb("hh", (128, NF, N), BF16); r_hh = [R(f"hh{f}") for f in range(NF)]
    arena = hh[:].rearrange("p f n -> p (f n)").bitcast(F32)
    sgb = [sb(f"sg{i}", (128, N)) for i in range(2)]; r_sg = [R("sg0"), R("sg1")]
    ctmpA = sb("ctmpA", (128, 1024)); r_sgA = [R("sgA0"), R("sgA1")]
    NSF, NSM = 3, 2
    ringF = sb("ringF", (128, NSF, 2816), BF16); r_ringF = [R(f"ringF{i}") for i in range(NSF)]
    ringM = sb("ringM", (128, NSM, 2048), BF16); r_ringM = [R(f"ringM{i}") for i in range(NSM)]
    banks = [ps(f"bank{i}", (128, N)) for i in range(8)]
    r_bank = [R(f"bank{i}") for i in range(8)]

    eng_sems = {e: sem(f"sem_{e}") for e in Sched.ENGS}
    ringF_ch = [Chan(sem(f"ringFsem{i}")) for i in range(NSF)]
    ringM_ch = [Chan(sem(f"ringMsem{i}")) for i in range(NSM)]
    ringF_chH = [Chan(sem(f"ringFsemH{i}")) for i in range(NSF)]
    ringM_chH = [Chan(sem(f"ringMsemH{i}")) for i in range(NSM)]
    ringF_chS = [Chan(sem(f"ringFsemS{i}")) for i in range(NSF)]
    ringM_chS = [Chan(sem(f"ringMsemS{i}")) for i in range(NSM)]
    misc3_ch = Chan(sem("misc3sem"))
    x_ch = [Chan(sem(f"xsem{i}")) for i in range(2)]
    o_ch = [Chan(sem(f"osem{i}")) for i in range(2)]
    misc_ch = Chan(sem("miscsem"))
    dbg_ch = Chan(sem("dbgsem"))
    misc2_ch = Chan(sem("misc2sem"))
    vecs_ch = Chan(sem("vecssem"))
    r_out = [R("out0"), R("out1")]

    st = {"bank": 4, "slot": 0, "bankA": 0, "bankB": 0}

    def bankA():
        b = 3 + st["bankA"]
        st["bankA"] = (st["bankA"] + 1) % 3
        return b

    def bankB():
        b = st["bankB"]
        st["bankB"] = (st["bankB"] + 1) % 3
        return b
    r_dbg = R("dbg")

    def dump(it_, idx, ap, regs):
        if dbg_d is None or it_ >= 2:
            return
        k_ = it_ * 40 + idx
        S.add("pool", lambda e: e.dma_start(out=dbg_d[k_], in_=ap), tuple(regs), (r_dbg,), is_dma=True, chan=dbg_ch)

    def next_bank():
        b = st["bank"]
        st["bank"] = (b + 1) % 8
        return b

    def fsz(ap):
        n_ = 1
        for d_ in ap.shape[1:]:
            n_ *= int(d_)
        return n_

    def edur(eng, ap, k=1.04):
        if eng == "pool":
            return 170.0 + 2.7 * fsz(ap)
        if eng == "act":
            return 140.0 + 1.25 * fsz(ap)
        return 90.0 + 1.5 * k * fsz(ap)

    def dma(q, out, in_, reads, writes, chan):
        d_ = 2000.0 + fsz(out) * 128 * 2 / (70.0 if q == "pool" else 150.0)
        S.add(q, lambda e, out=out, in_=in_: e.dma_start(out=out, in_=in_), reads, writes, is_dma=True, chan=chan, dur=d_)

    def act(out, in_, func, reads, writes, bias=None, scale=None):
        kw = {}
        if bias is not None:
            kw["bias"] = bias
        if scale is not None:
            kw["scale"] = scale
        S.add("act", lambda e: e.activation(out=out, in_=in_, func=func, **kw), reads, writes, dur=edur("act", out))

    def tt(eng, out, in0, in1, op, reads, writes):
        S.add(eng, lambda e: e.tensor_tensor(out=out, in0=in0, in1=in1, op=op), reads, writes, dur=edur(eng, out))

    def ts(eng, out, in0, s1, op0, reads, writes, s2=None, op1=None):
        if op1 is None:
            S.add(eng, lambda e: e.tensor_scalar(out=out, in0=in0, scalar1=s1, scalar2=None, op0=op0), reads, writes, dur=edur(eng, out, 0.7))
        else:
            S.add(eng, lambda e: e.tensor_scalar(out=out, in0=in0, scalar1=s1, scalar2=s2, op0=op0, op1=op1), reads, writes, dur=edur(eng, out, 0.7))

    def stt(out, in0, scalar, in1, op0, op1, reads, writes):
        S.add("dve", lambda e: e.scalar_tensor_tensor(out=out, in0=in0, scalar=scalar, in1=in1, op0=op0, op1=op1), reads, writes, dur=edur("dve", out))

    def cp(eng, out, in_, reads, writes):
        S.add(eng, lambda e: e.tensor_copy(out=out, in_=in_), reads, writes, dur=edur(eng, out, 0.8))

    def memset(eng, ap, val, writes):
        S.add(eng, lambda e: e.memset(ap, val), (), writes, dur=edur(eng, ap, 0.5))

    def scan(out, d0, d1, init, reads, writes):
        S.add("dve", lambda e: e.tensor_tensor_scan(out=out, data0=d0, data1=d1, initial=init, op0=ALU.mult, op1=ALU.add), reads, writes,
              dur=edur("dve", out))

    def mmgroup(mms, reads, writes):
        def fn(e):
            ins = None
            for m_ in mms:
                kw = {}
                if m_.get("tp") is not None:
                    kw["tile_position"] = m_["tp"]
                ins = e.matmul(m_["out"], lhsT=m_["lhsT"], rhs=m_["rhs"], start=m_["start"], stop=m_["stop"], **kw)
            return ins
        d_ = 0.0
        for m_ in mms:
            f_ = 4.0 if m_["rhs"].dtype == F32 else 1.0
            d_ += f_ * max(fsz(m_["rhs"]), 64) / 2.2 + 6.0
        S.add("pe", fn, reads, writes, dur=d_)

    dma("sp", vecs[:], vec_d[:], (), (r_vecs,), vecs_ch)
    dma("sp", consts[:], const_d[:], (), (r_consts,), misc2_ch)
    memset("pool", ones_b[:], 1.0, (r_ones,))
    memset("pool", epsb[:, 0:1], EPS, (r_epsb,))
    memset("pool", epsb[:, 1:2], 1.0, (r_epsb,))
    memset("pool", epsb[:, 2:3], 0.25, (r_epsb,))
    memset("pool", epsb[:, 3:4], 0.0, (r_epsb,))
    memset("pool", hstate[:], 0.0, tuple(r_hstate))

    r_pro = tuple(r_x[1]) + tuple(r_hh)
    pools = [(arena, 5632), (xbuf[1][:].rearrange("p a b -> p (a b)"), 4096)]
    pro_off = [0, 0]

    def palloc(n, pool=None):
        order = (0, 1) if pool is None else (pool,)
        for pi_ in order:
            if pro_off[pi_] + n <= pools[pi_][1]:
                o = pro_off[pi_]
                pro_off[pi_] += n
                return pools[pi_][0][:, o:o + n]
        raise AssertionError("prologue scratch exhausted")

    pro_i = gy[:].rearrange("p a b -> p (a b)").bitcast(I32)
    pro_f = zf[:].rearrange("p a b -> p (a b)")
    r_proi = tuple(r_gy) + tuple(r_zf)
    r_P = R("P")

    def P_ts(out, in0, s1, op0, s2=None, op1=None, extra_r=()):
        ts("dve", out, in0, s1, op0, (r_P,) + tuple(extra_r), (r_P,) + r_pro, s2, op1)

    def P_tt(out, in0, in1, op, extra_r=()):
        tt("dve", out, in0, in1, op, (r_P,) + tuple(extra_r), (r_P,) + r_pro)

    def P_act(out, in_, func, bias=None, scale=None, extra_r=()):
        act(out, in_, func, (r_P, r_epsb) + tuple(extra_r), (r_P,) + r_pro, bias, scale)

    def frac_round(out, x, n):
        assert n <= 2048
        ii = pro_i[:, 0:n]
        ff = pro_f[:, 0:n]
        S.add("dve", lambda e: e.tensor_copy(out=ii, in_=x), (r_P,) + r_proi, r_proi)
        S.add("dve", lambda e: e.tensor_copy(out=ff, in_=ii), r_proi, r_proi)
        tt("dve", out, x, ff, ALU.subtract, (r_P,) + r_proi, (r_P,) + r_pro)

    def cis(cos_out, sin_out, turns, n, tmp):
        frac_round(tmp, turns, n)
        P_act(sin_out, tmp, AF.Sin, scale=TWO_PI)
        P_ts(tmp, turns, 0.25, ALU.add)
        frac_round(tmp, tmp, n)
        P_act(cos_out, tmp, AF.Sin, scale=TWO_PI)

    def cmul(or_, oi_, ar, ai, br, bi, t1, t2):
        P_tt(t1, ar, br, ALU.mult)
        P_tt(t2, ai, bi, ALU.mult)
        P_tt(or_, t1, t2, ALU.subtract)
        P_tt(t1, ar, bi, ALU.mult)
        P_tt(t2, ai, br, ALU.mult)
        P_tt(oi_, t1, t2, ALU.add)

    def lam_powers(src_d, F, npow):
        lam = palloc(3 * F)
        dma("sp", lam, src_d, (), (r_P,) + r_pro, misc_ch)
        lr, li, ls = lam[:, 0:F], lam[:, F:2 * F], lam[:, 2 * F:3 * F]
        dl = palloc(F); a = palloc(F); fb = palloc(F); t1 = palloc(F); t2 = palloc(F); t3 = palloc(F)
        P_act(dl, ls, AF.Exp)
        P_ts(lr, lr, -1e-4, ALU.min)
        P_tt(a, lr, dl, ALU.mult)
        P_tt(fb, li, dl, ALU.mult)
        P_ts(fb, fb, 1.0 / (2.0 * np.pi), ALU.mult)
        em1 = palloc(F); E = palloc(F)
        P_ts(t1, a, 0.2, ALU.mult, 1.0, ALU.add)
        for cdiv in (0.25, 1.0 / 3.0, 0.5):
            P_tt(t1, t1, a, ALU.mult)
            P_ts(t1, t1, cdiv, ALU.mult, 1.0, ALU.add)
        P_tt(em1, t1, a, ALU.mult)
        P_ts(E, em1, 1.0, ALU.add)
        c1 = palloc(F); s1 = palloc(F); sh = palloc(F)
        cis(c1, s1, fb, F, t1)
        P_ts(t2, fb, 0.5, ALU.mult)
        frac_round(t1, t2, F)
        P_act(sh, t1, AF.Sin, scale=TWO_PI)
        Pw = {0: None}
        p1r = palloc(F); p1i = palloc(F)
        P_tt(p1r, E, c1, ALU.mult)
        P_tt(p1i, E, s1, ALU.mult)
        Pw[1] = (p1r, p1i)
        for k in range(2, npow + 1):
            pr = palloc(F); pi_ = palloc(F)
            cmul(pr, pi_, Pw[k - 1][0], Pw[k - 1][1], p1r, p1i, t1, t2)
            Pw[k] = (pr, pi_)
        nr = palloc(F)
        P_tt(nr, em1, c1, ALU.mult)
        P_tt(t1, sh, sh, ALU.mult)
        P_ts(t1, t1, -2.0, ALU.mult)
        P_tt(nr, nr, t1, ALU.add)
        den = palloc(F)
        P_tt(t1, lr, lr, ALU.mult)
        P_tt(t2, li, li, ALU.mult)
        P_tt(den, t1, t2, ALU.add)
        S.add("dve", lambda e: e.reciprocal(out=den, in_=den), (r_P,), (r_P,) + r_pro)
        kr = palloc(F); ki = palloc(F)
        P_tt(t1, nr, lr, ALU.mult)
        P_tt(t2, p1i, li, ALU.mult)
        P_tt(t1, t1, t2, ALU.add)
        P_tt(kr, t1, den, ALU.mult)
        P_tt(t1, p1i, lr, ALU.mult)
        P_tt(t2, nr, li, ALU.mult)
        P_tt(t1, t1, t2, ALU.subtract)
        P_tt(ki, t1, den, ALU.mult)
        return dict(P=Pw, kr=kr, ki=ki, fb=fb, E=E, t=(t1, t2, t3))

    FC = 256
    LC = lam_powers(lamC_d[:].rearrange("p a f -> p (a f)"), FC, 3)
    bC = palloc(2 * FC)
    dma("sp", bC, bC_d[:].rearrange("p a f -> p (a f)"), (), (r_P,) + r_pro, misc_ch)
    bre, bim = bC[:, 0:FC], bC[:, FC:2 * FC]
    bbr = palloc(FC); bbi = palloc(FC)
    t1, t2, t3 = LC["t"]
    cmul(bbr, bbi, LC["kr"], LC["ki"], bre, bim, t1, t2)
    wr = palloc(FC); wi = palloc(FC)
    for k in range(TCH):
        pw = TCH - 1 - k
        if pw == 0:
            srcr, srci = bbr, bbi
        else:
            cmul(wr, wi, LC["P"][pw][0], LC["P"][pw][1], bbr, bbi, t1, t2)
            srcr, srci = wr, wi
        for reim, src in ((0, srcr), (1, srci)):
            for g2 in range(2):
                ts("dve", Wsi[:, :, k, reim, 64 * g2:64 * g2 + 64], src.rearrange("p (m n) -> p m n", m=4),
                   mask_g2[:, g2:g2 + 1], ALU.mult, (r_P, r_consts), (r_Wsi,))
    pro_off[0] = 0; pro_off[1] = 0
    FB = 16
    LB = lam_powers(lamB_d[:].rearrange("p a f -> p (a f)"), FB, 4)
    t1, t2, t3 = LB["t"]
    E = LB["E"]
    P_tt(t1, E, E, ALU.mult)
    tt("dve", rho[:], t1, t1, ALU.mult, (r_P,), (r_rho,))
    f4 = palloc(FB); fr = palloc(FB)
    P_ts(f4, LB["fb"], float(TCH), ALU.mult)
    frac_round(fr, f4, FB)
    NJ = NCH + 1
    HQ = FB // 2
    mark_ = list(pro_off)
    xt_ = palloc(HQ * NJ)
    xt3 = xt_.rearrange("p (q j) -> p q j", q=HQ)
    ang = palloc(HQ * NJ)
    for hq in range(2):
        for q in range(HQ):
            qq = hq * HQ + q
            ts("dve", xt3[:, q, :], iota, fr[:, qq:qq + 1], ALU.mult, (r_P, r_consts), (r_P,) + r_pro)
        frac_round(ang, xt_, HQ * NJ)
        act(sinT[:, hq * HQ:(hq + 1) * HQ, :].rearrange("p q j -> p (q j)"), ang, AF.Sin, (r_P,), (r_tab,), scale=TWO_PI)
        P_ts(xt_, xt_, 0.25, ALU.add)
        frac_round(ang, xt_, HQ * NJ)
        act(cosT[:, hq * HQ:(hq + 1) * HQ, :].rearrange("p q j -> p (q j)"), ang, AF.Sin, (r_P,), (r_tab,), scale=TWO_PI)
    pro_off[0], pro_off[1] = mark_
    FQ = 256
    bB = palloc(2 * FQ); cB = palloc(2 * FQ)
    dma("sp", bB, bB_d[:].rearrange("p a f -> p (a f)"), (), (r_P,) + r_pro, misc_ch)
    dma("sp", cB, cB_d[:].rearrange("p a f -> p (a f)"), (), (r_P,) + r_pro, misc_ch)

    def bc(v):
        return v.unsqueeze(2).to_broadcast([128, 16, 16])

    def v3(v):
        return v.rearrange("p (q c) -> p q c", q=16)

    u1 = palloc(FQ); u2 = palloc(FQ)

    def cmul_bc(or_, oi_, sr, si, xr_, xi_):
        P_tt(v3(u1), v3(xr_), bc(sr), ALU.mult)
        P_tt(v3(u2), v3(xi_), bc(si), ALU.mult)
        P_tt(or_, u1, u2, ALU.subtract)
        P_tt(v3(u1), v3(xi_), bc(sr), ALU.mult)
        P_tt(v3(u2), v3(xr_), bc(si), ALU.mult)
        P_tt(oi_, u1, u2, ALU.add)

    bbBr = palloc(FQ); bbBi = palloc(FQ)
    cmul_bc(bbBr, bbBi, LB["kr"], LB["ki"], bB[:, 0:FQ], bB[:, FQ:2 * FQ])
    Bblk = palloc(2 * 16 * 32)
    Cblk = palloc(2 * 16 * 4 * 32, pool=1)
    S.add("dve", lambda e: e.memset(Bblk, 0.0), (r_P,), (r_P,) + r_pro)
    S.add("dve", lambda e: e.memset(Cblk, 0.0), (r_P,), (r_P,) + r_pro)
    S.add("pool", lambda e: e.memset(Wso[:].rearrange("p a b c d -> p (a b c d)"), 0.0), (), (r_Wso,))
    Bblk4 = Bblk.rearrange("p (a q c) -> p a q c", a=2, q=16)
    Cblk5 = Cblk.rearrange("p (a q d c) -> p a q d c", a=2, q=16, d=4)
    for g2 in range(2):
        pl = slice(64 * g2, 64 * g2 + 64)
        for reim, src in ((0, bbBr), (1, bbBi)):
            S.add("dve", lambda e, o=Bblk4[pl, reim, :, 16 * g2:16 * g2 + 16], i_=v3(src)[pl]: e.tensor_copy(out=o, in_=i_),
                  (r_P,), (r_P,) + r_pro)
    cpr = palloc(FQ); cpi = palloc(FQ)
    cre, cim = cB[:, 0:FQ], cB[:, FQ:2 * FQ]
    for pw in range(0, TCH + 1):
        if pw == 0:
            srcr, srci = cre, cim
        else:
            cmul_bc(cpr, cpi, LB["P"][pw][0], LB["P"][pw][1], cre, cim)
            srcr, srci = cpr, cpi
        for g2 in range(2):
            pl = slice(64 * g2, 64 * g2 + 64)
            cs = slice(16 * g2, 16 * g2 + 16)
            if pw < TCH:
                S.add("dve", lambda e, o=Cblk5[pl, 0, :, pw, cs], i_=v3(srcr)[pl]: e.tensor_copy(out=o, in_=i_), (r_P,), (r_P,) + r_pro)
                S.add("dve", lambda e, o=Cblk5[pl, 1, :, pw, cs], i_=v3(srci)[pl]: e.tensor_scalar(out=o, in0=i_, scalar1=-1.0, scalar2=None, op0=ALU.mult),
                      (r_P,), (r_P,) + r_pro)
            if pw >= 1:
                t_ = pw - 1
                S.add("dve", lambda e, o=Wso[pl, :, t_, 0, cs], i_=v3(srcr)[pl]: e.tensor_copy(out=o, in_=i_), (r_P,), (r_Wso,))
                S.add("dve", lambda e, o=Wso[pl, :, t_, 1, cs], i_=v3(srci)[pl]: e.tensor_scalar(out=o, in0=i_, scalar1=-1.0, scalar2=None, op0=ALU.mult),
                      (r_P,), (r_Wso,))
    kb = 4
    KD = banks[kb][:].rearrange("p (m d c) -> p m d c", m=4, d=4)
    mms = []
    for q in range(16):
        m_, r_ = q // 4, q % 4
        for reim in range(2):
            mms.append(dict(out=KD[32 * r_:32 * r_ + 32, m_, :, :], lhsT=Bblk4[:, reim, q, :], rhs=Cblk5[:, reim, q, :, :],
                            start=(reim == 0), stop=(reim == 1), tp=(0, 32 * r_)))
    mmgroup(mms, (r_P,), (r_bank[kb],))
    BDf = Cblk[:, 0:2048]
    BDf4 = BDf.rearrange("p (m d c) -> p m d c", m=4, d=4)
    for r_ in range(4):
        ts("dve", BDf4[:, :, :, 32 * r_:32 * r_ + 32], KD, mask_r[:, r_:r_ + 1], ALU.mult, (r_bank[kb], r_consts, r_P), (r_P,) + r_pro)
    for m_ in range(4):
        stt(BDf4[:, m_, 0, :], ident, vecs[:, V_D + m_:V_D + m_ + 1], BDf4[:, m_, 0, :], ALU.mult, ALU.add,
            (r_P, r_consts, r_vecs), (r_P,) + r_pro)
    cp("dve", BD[:].rearrange("p m d c -> p (m d c)"), BDf, (r_P,), (r_BD,))

    dma("pool", Wg_b[:].rearrange("p a m c -> p (a m c)"), gate_d[:].rearrange("p a f -> p (a f)"), (), (r_Wg,), misc3_ch)
    for m_ in range(4):
        for k in range(4):
            ts("dve", Dconv[:, m_, k, :], ident, vecs[:, 56 + 4 * m_ + k:56 + 4 * m_ + k + 1], ALU.mult, (r_consts, r_vecs), (r_Dconv,))
    ts("dve", dv[:, DV_S5GH:DV_S5GH + 4], vecs[:, V_S5G:V_S5G + 4], 0.5, ALU.mult, (r_vecs,), (r_dv,))
    ts("dve", dv[:, DV_HBGLU:DV_HBGLU + 4], vecs[:, V_BGLU:V_BGLU + 4], 0.5, ALU.mult, (r_vecs,), (r_dv,))
    ts("dve", dv[:, DV_HBA:DV_HBA + 4], vecs[:, V_BA:V_BA + 4], 0.5, ALU.mult, (r_vecs,), (r_dv,))
    ts("dve", dv[:, DV_HBX:DV_HBX + 4], vecs[:, V_BX:V_BX + 4], 0.5, ALU.mult, (r_vecs,), (r_dv,))
    ev = palloc(4); pl_ = palloc(4)
    P_act(ev, vecs[:, V_LAM:V_LAM + 4], AF.Exp, scale=-1.0, extra_r=(r_vecs,))
    P_ts(pl_, ev, -0.25, ALU.mult, 1.0 / 3.0, ALU.add)
    P_tt(pl_, pl_, ev, ALU.mult)
    P_ts(pl_, pl_, -0.5, ALU.add)
    P_tt(pl_, pl_, ev, ALU.mult)
    P_ts(pl_, pl_, 1.0, ALU.add)
    P_tt(pl_, pl_, ev, ALU.mult)
    ts("dve", dv[:, DV_C:DV_C + 4], pl_, -8.0, ALU.mult, (r_P,), (r_dv,))
    ts("dve", dv[:, DV_HC:DV_HC + 4], pl_, -4.0, ALU.mult, (r_P,), (r_dv,))

    wlistM = []
    wlistF = []
    for _it in range(NT):
        wlistM += [(w_in_d[b_], 2048) for b_ in range(6)]
        wlistM += [(w_glu_d[:], 2048)]
        wlistM += [(w_out_d[b_], 2048) for b_ in range(4)]
        wlistF += [(w_gu_d[f_], 2048) for f_ in range(NF)]
        wlistF += [(w_dn_d[o_], 2816) for o_ in range(8)]
    wsM = {"issued": 0, "used": 0}
    wsF = {"issued": 0, "used": 0}
    scrM = nc.dram_tensor("scrM", [11, 128, 2048], BF16, kind="Internal").ap()
    scrF = nc.dram_tensor("scrF", [NF + 8, 128, 2816], BF16, kind="Internal").ap()

    def load_gen(ws, wl, ring_, rr, chs_sw, chs_hw, chs_st, ns, scr, per_tile):
        r_scr = [R(f"scr{id(scr)}_{i}") for i in range(per_tile)]

        def load_block(n):
            i_ = ws["used"]
            assert wl[i_][1] == n
            while ws["issued"] < min(len(wl), i_ + ns):
                j_ = ws["issued"]
                sj = j_ % ns
                nj = wl[j_][1]
                bj = j_ % per_tile
                if j_ < per_tile:
                    dma("pool", ring_[:, sj, 0:nj], wl[j_][0], (), (rr[sj],), chs_sw[sj])
                    dma("sp", scr[bj, :, 0:nj], ring_[:, sj, 0:nj], (rr[sj],), (r_scr[bj],), chs_st[sj])
                else:
                    dma("sp", ring_[:, sj, 0:nj], scr[bj, :, 0:nj], (r_scr[bj],), (rr[sj],), chs_hw[sj])
                ws["issued"] += 1
            ws["used"] += 1
            return i_ % ns
        return load_block

    loadM = load_gen(wsM, wlistM, ringM, r_ringM, ringM_ch, ringM_chH, ringM_chS, NSM, scrM, 11)
    loadF = load_gen(wsF, wlistF, ringF, r_ringF, ringF_ch, ringF_chH, ringF_chS, NSF, scrF, NF + 8)

    def rms(bankfn, src_regions, src_aps, nchunks, dim, sq_aps, sq_regs, rs_ap, rs_reg, stage_scale=1.0):
        for k in range(nchunks):
            act(sq_aps[k], src_aps[k], AF.Square, (src_regions[k],), (sq_regs[k],), scale=stage_scale)
        b = bankfn()
        mms = [dict(out=banks[b][:], lhsT=ones_b[:], rhs=sq_aps[k], start=(k == 0), stop=(k == nchunks - 1)) for k in range(nchunks)]
        mmgroup(mms, tuple(sq_regs[:nchunks]) + (r_ones,), (r_bank[b],))
        act(rs_ap, banks[b][:], AF.Ln, (r_bank[b], r_epsb), (rs_reg,), bias=epsb[:, 0:1], scale=1.0 / dim)
        act(rs_ap, rs_ap, AF.Exp, (rs_reg,), (rs_reg,), scale=-0.5)

    sqA = [hb[:, k, :] for k in range(8)]
    r_sq = r_h
    sq2 = [h2b[:, k, :] for k in range(8)]

    def stageA(it):
        xi = it % 2
        xb = xbuf[xi]
        rx = r_x[xi]
        seq_start = (it % TPS == 0)
        rms(bankA, rx, [xb[:, k, :] for k in range(8)], 8, float(D), sqA, r_sq, rstd[:], r_rstd)
        for k in range(8):
            stt(hb[:, k, :], xb[:, k, :], vecs[:, V_G1 + k:V_G1 + k + 1], rstd[:], ALU.mult, ALU.mult,
                (rx[k], r_vecs, r_rstd), (r_h[k],))
        yield 8
        for blk in range(6):
            s_ = loadM(2048)
            wv = ringM[:, s_, 0:2048].rearrange("p (o k m) -> p o k m", o=2, k=8)
            for o2 in range(2):
                oc = 2 * blk + o2
                b = bankA()
                mms = [dict(out=banks[b][:], lhsT=wv[:, o2, k, :], rhs=hb[:, k, :], start=(k == 0), stop=(k == 7)) for k in range(8)]
                mmgroup(mms, tuple(r_h) + (r_ringM[s_],), (r_bank[b],))
                m_ = oc % 4
                if oc < 4:
                    cp("dve", ub[:, m_, :], banks[b][:], (r_bank[b],), (r_u[m_],))
                elif oc < 8:
                    if seq_start:
                        memset("pool", xrb[:, m_, 0:3], 0.0, (r_xr[m_],))
                    else:
                        cp("pool", xrb[:, m_, 0:3], xrb[:, m_, N:N + 3], (r_xr[m_],), (r_xr[m_],))
                    cp("dve", xrb[:, m_, 3:3 + N], banks[b][:], (r_bank[b], r_xr[m_]), (r_xr[m_],))
                else:
                    act(gy[:, m_, :], banks[b][:], AF.Gelu_apprx_tanh, (r_bank[b],), (r_gy[m_],))
                yield 8
        for m_ in range(4):
            dump(it, m_, ub[:, m_, :], (r_u[m_],))
        xhb = hb[:, 4:8, :]
        if seq_start and it > 0:
            for m_ in range(4):
                memset("dve", hstate[:, m_:m_ + 1], 0.0, (r_hstate[m_],))
        for half in range(2):
            xh = [ltmp[:, 512 * j:512 * j + 512] for j in range(2)]
            tr = [ltmp[:, 1024 + 512 * j:1024 + 512 * j + 512] for j in range(2)]
            ti = [ltmp[:, 2048 + 512 * j:2048 + 512 * j + 512] for j in range(2)]
            r_xh = [R2(r_lt[0], r_lt[1]), R2(r_lt[2], r_lt[3])]
            r_tr = [R2(r_lt[4], r_lt[5]), R2(r_lt[6], r_lt[7])]
            r_ti = [R2(r_lt[8], r_lt[9]), R2(r_lt[10], r_lt[11])]
            for j in range(2):
                m_ = 2 * half + j
                b = bankA()
                mms = [dict(out=banks[b][:], lhsT=Dconv[:, m_, k, :], rhs=xrb[:, m_, k:k + N], start=(k == 0), stop=(k == 3)) for k in range(4)]
                mmgroup(mms, (r_xr[m_], r_Dconv), (r_bank[b],))
                act(xh[j], banks[b][:], AF.Identity, (r_bank[b], r_vecs), (r_xh[j],), bias=vecs[:, V_CB + m_:V_CB + m_ + 1])
                cp("pool", xhb[:, m_, :], xh[j], (r_xh[j],), (r_h[4 + m_],))
                for gi, (dst, rdst, hbcol) in enumerate(((tr, r_tr, DV_HBA), (ti, r_ti, DV_HBX))):
                    b2 = bankA()
                    mmgroup([dict(out=banks[b2][:], lhsT=Wg_b[:, gi, m_, :], rhs=xhb[:, m_, :], start=True, stop=True)],
                            (r_h[4 + m_], r_Wg), (r_bank[b2],))
                    act(dst[j], banks[b2][:], AF.Tanh, (r_bank[b2], r_dv), (rdst[j],),
                        bias=dv[:, hbcol + m_:hbcol + m_ + 1], scale=0.5)
                yield 6
            for j in range(2):
                m_ = 2 * half + j
                a_ = tr[j]
                act(a_, tr[j], AF.Exp, (r_tr[j], r_dv), (r_tr[j],), bias=dv[:, DV_HC + m_:DV_HC + m_ + 1], scale=dv[:, DV_HC + m_:DV_HC + m_ + 1])
                w1 = sgA[j]; rw1 = r_sgA[j]
                act(w1, a_, AF.Square, (r_tr[j],), (rw1,))
                act(w1, w1, AF.Ln, (rw1, r_epsb), (rw1,), bias=epsb[:, 2:3], scale=-0.25)
                act(w1, w1, AF.Exp, (rw1,), (rw1,), scale=0.5)
                stt(ti[j], ti[j], 1.0, xh[j], ALU.add, ALU.mult, (r_ti[j], r_xh[j]), (r_ti[j],))
                tt("dve", ti[j], ti[j], w1, ALU.mult, (r_ti[j], rw1), (r_ti[j],))
                scan(xh[j], a_, ti[j], hstate[:, m_:m_ + 1], (r_tr[j], r_ti[j], r_hstate[m_]), (r_xh[j],))
                cp("dve", hstate[:, m_:m_ + 1], xh[j][:, N - 1:N], (r_xh[j],), (r_hstate[m_],))
                tt("dve", lo[:, m_, :], xh[j], gy[:, m_, :], ALU.mult, (r_xh[j], r_gy[m_]), (r_lo[m_],))
                yield 1
        for m_ in range(4):
            dump(it, 8 + m_, lo[:, m_, :], (r_lo[m_],))
        rms(bankA, r_lo, [lo[:, k, :] for k in range(4)], 4, 512.0, sqA, r_sq, rstd[:], r_rstd)
        for m_ in range(4):
            stt(mix[:, 4 + m_, :], lo[:, m_, :], vecs[:, V_LRUG + m_:V_LRUG + m_ + 1], rstd[:], ALU.mult, ALU.mult,
                (r_lo[m_], r_vecs, r_rstd), (r_mix[4 + m_],))
        yield 4
        Sc = Sbuf
        rS = r_S
        if seq_start:
            memset("pool", sinit[:], 0.0, (r_sinit,))
        for q in range(16):
            if seq_start:
                memset("pool", Sc[:, q, :, 0:1], 0.0, (rS[q],))
            else:
                cp("pool", Sc[:, q, :, 0:1], Sc[:, q, :, NCH:NCH + 1], (rS[q],), (rS[q],))

        def tbuf(idx, par):
            o_ = (2 * idx + par) * 256
            return ltmp[:, o_:o_ + 256].rearrange("p (a j) -> p a j", a=2), r_lt[2 * idx + par]
        def zhalf(par):
            return banks[6 + par][:, 0:256].rearrange("p (a j) -> p a j", a=2), r_bank[6 + par]

        def s0(q0):
            mms = []
            for reim in range(2):
                for k in range(TCH):
                    for q in (q0, q0 + 1):
                        m_, r_ = q // 4, q % 4
                        Z, rz = zhalf(q % 2)
                        mms.append(dict(out=Z[:, reim, :], lhsT=Wsi[32 * r_:32 * r_ + 32, m_, k, reim, :],
                                        rhs=ub[32 * r_:32 * r_ + 32, m_, k::TCH], start=(k == 0), stop=(k == TCH - 1),
                                        tp=(32 * r_, 0)))
            mmgroup(mms, (r_u[q0 // 4], r_Wsi), (r_bank[6], r_bank[7]))

        def s1(q):
            Z, rz = zhalf(q % 2)
            Zs, rZs = tbuf(0, q % 2)
            act(Zs, Z, AF.Copy, (rz,), (rZs,))

        def s2(q):
            Zs, rZs = tbuf(0, q % 2)
            T1, rT1 = tbuf(1, q % 2)
            M_, rM = tbuf(2, q % 2)
            cosb = cosT[:, q, 0:NCH].unsqueeze(1).to_broadcast([128, 2, NCH])
            sinq = sinT[:, q, 0:NCH]
            tt("pool", T1, Zs, cosb, ALU.mult, (rZs, r_tab), (rT1,))
            tt("pool", M_[:, 0, :], Zs[:, 1, :], sinq, ALU.mult, (rZs, r_tab), (rM,))
            tt("pool", M_[:, 1, :], Zs[:, 0, :], sinq, ALU.mult, (rZs, r_tab), (rM,))

        def s3(q):
            T1, rT1 = tbuf(1, q % 2)
            M_, rM = tbuf(2, q % 2)
            Wn, rWn = tbuf(3, q % 2)
            Wo, rWo = tbuf(4, q % 2)
            tt("dve", Wn[:, 0, :], T1[:, 0, :], M_[:, 0, :], ALU.add, (rT1, rM), (rWn,))
            tt("dve", Wn[:, 1, :], T1[:, 1, :], M_[:, 1, :], ALU.subtract, (rT1, rM), (rWn,))
            rho_b = rho[:, q:q + 1].to_broadcast([128, NCH])
            for reim in range(2):
                scan(Wo[:, reim, :], rho_b, Wn[:, reim, :], sinit[:, q, reim:reim + 1], (rWn, r_rho, r_sinit), (rWo,))

        def s4(q):
            Wo, rWo = tbuf(4, q % 2)
            P1, rP1 = tbuf(5, q % 2)
            Mp, rMp = tbuf(6, q % 2)
            cosb = cosT[:, q, 0:NCH].unsqueeze(1).to_broadcast([128, 2, NCH])
            sinq = sinT[:, q, 0:NCH]
            cp("pool", wlast[:, q, :], Wo[:, :, NCH - 1], (rWo,), (r_wlast,))
            tt("pool", P1, Wo, cosb, ALU.mult, (rWo, r_tab), (rP1,))
            tt("pool", Mp[:, 0, :], Wo[:, 1, :], sinq, ALU.mult, (rWo, r_tab), (rMp,))
            tt("pool", Mp[:, 1, :], Wo[:, 0, :], sinq, ALU.mult, (rWo, r_tab), (rMp,))

        def s5(q):
            P1, rP1 = tbuf(5, q % 2)
            Mp, rMp = tbuf(6, q % 2)
            tt("dve", Sc[:, q, 0, 1:NCH + 1], P1[:, 0, :], Mp[:, 0, :], ALU.subtract, (rP1, rMp), (rS[q],))
            tt("dve", Sc[:, q, 1, 1:NCH + 1], P1[:, 1, :], Mp[:, 1, :], ALU.add, (rP1, rMp), (rS[q],))

        def s6(m_):
            yb = bankA()
            Y = banks[yb][:].rearrange("p (t j) -> p t j", t=TCH)
            for t_ in range(TCH):
                mms = []
                for d_ in range(t_ + 1):
                    mms.append(dict(out=Y[:, t_, :], lhsT=BD[:, m_, d_, :], rhs=ub[:, m_, (t_ - d_)::TCH], start=(d_ == 0), stop=False))
                for r_ in range(4):
                    q = 4 * m_ + r_
                    for reim in range(2):
                        mms.append(dict(out=Y[32 * r_:32 * r_ + 32, t_, :], lhsT=Wso[:, q, t_, reim, :], rhs=Sc[:, q, reim, 0:NCH],
                                        start=False, stop=(r_ == 3 and reim == 1), tp=(0, 32 * r_)))
                mmgroup(mms, (r_u[m_], r_BD, r_Wso) + tuple(rS[4 * m_:4 * m_ + 4]), (r_bank[yb],))
            zview = zf[:, m_, :].rearrange("p (j t) -> p t j", t=TCH)
            act(zview, Y, AF.Gelu_apprx_tanh, (r_bank[yb],), (r_zf[m_],))
            cp("pool", zb[:, m_, :], zf[:, m_, :], (r_zf[m_],), (r_zb[m_],))
            dump(it, 4 + m_, zf[:, m_, :], (r_zf[m_],))

        for w in range(-1, 16 + 5):
            if 0 <= w < 16:
                s1(w)
            if 0 <= w + 1 < 16 and (w + 1) % 2 == 0:
                s0(w + 1)
            if 0 <= w - 1 < 16:
                s2(w - 1)
            if 0 <= w - 2 < 16:
                s3(w - 2)
            if 0 <= w - 3 < 16:
                s4(w - 3)
            if 0 <= w - 4 < 16:
                s5(w - 4)
                if (w - 4) % 4 == 3:
                    s6((w - 4) // 4)
            yield 3
        if not ((it + 1) % TPS == 0):
            c128 = cosT[:, :, NCH]; s128 = sinT[:, :, NCH]
            ta = ctmp[:, 0:16]; tb_ = ctmp[:, 16:32]; tc_ = ctmp[:, 32:48]; td = ctmp[:, 48:64]
            rc = r_ctmp
            tt("dve", ta, wlast[:, :, 0], c128, ALU.mult, (r_wlast, r_tab), (rc,))
            tt("dve", tb_, wlast[:, :, 1], s128, ALU.mult, (r_wlast, r_tab), (rc,))
            tt("dve", tc_, wlast[:, :, 0], s128, ALU.mult, (r_wlast, r_tab), (rc,))
            tt("dve", td, wlast[:, :, 1], c128, ALU.mult, (r_wlast, r_tab), (rc,))
            tt("dve", sinit[:, :, 0], ta, tb_, ALU.subtract, (rc,), (r_sinit,))
            tt("dve", sinit[:, :, 1], tc_, td, ALU.add, (rc,), (r_sinit,))
        s_ = loadM(2048)
        wv = ringM[:, s_, 0:2048].rearrange("p (k n) -> p k n", k=4)
        for oc in range(4):
            b = bankA()
            mms = [dict(out=banks[b][:], lhsT=wv[:, k, 128 * oc:128 * oc + 128], rhs=zb[:, k, :], start=(k == 0), stop=(k == 3)) for k in range(4)]
            mmgroup(mms, tuple(r_zb) + (r_ringM[s_],), (r_bank[b],))
            tg = sgA[oc % 2]; rtg = r_sgA[oc % 2]
            act(tg, banks[b][:], AF.Tanh, (r_bank[b], r_dv), (rtg,), bias=dv[:, DV_HBGLU + oc:DV_HBGLU + oc + 1], scale=0.5)
            stt(zf[:, oc, :], tg, 1.0, zf[:, oc, :], ALU.add, ALU.mult, (rtg, r_zf[oc]), (r_zf[oc],))
            yield 4
        rms(bankA, r_zf, [zf[:, k, :] for k in range(4)], 4, 512.0, sqA, r_sq, rstd[:], r_rstd, stage_scale=0.5)
        for m_ in range(4):
            stt(mix[:, m_, :], zf[:, m_, :], dv[:, DV_S5GH + m_:DV_S5GH + m_ + 1], rstd[:], ALU.mult, ALU.mult,
                (r_zf[m_], r_dv, r_rstd), (r_mix[m_],))
        yield 4
        for k in range(8):
            dump(it, 12 + k, mix[:, k, :], (r_mix[k],))
        for blk in range(4):
            s_ = loadM(2048)
            wv = ringM[:, s_, 0:2048].rearrange("p (o k m) -> p o k m", o=2, k=8)
            for o2 in range(2):
                oc = 2 * blk + o2
                b = bankA()
                mms = [dict(out=banks[b][:], lhsT=wv[:, o2, k, :], rhs=mix[:, k, :], start=(k == 0), stop=(k == 7)) for k in range(8)]
                mmgroup(mms, tuple(r_mix) + (r_ringM[s_],), (r_bank[b],))
                tt("dve", xb[:, oc, :], xb[:, oc, :], banks[b][:], ALU.add, (rx[oc], r_bank[b]), (rx[oc],))
                yield 8
        for k in range(8):
            dump(it, 20 + k, xb[:, k, :], (rx[k],))
        rms(bankA, rx, [xb[:, k, :] for k in range(8)], 8, float(D), sq2, r_h2, rstd[:], r_rstd)
        for k in range(8):
            stt(h2b[:, k, :], xb[:, k, :], vecs[:, V_G2 + k:V_G2 + k + 1], rstd[:], ALU.mult, ALU.mult,
                (rx[k], r_vecs, r_rstd), (r_h2[k],))
        yield 8

    def stageB(it):
        xi = it % 2
        xb = xbuf[xi]
        rx = r_x[xi]
        t0 = it * N
        for f in range(NF):
            s_ = loadF(2048)
            wv = ringF[:, s_, 0:2048].rearrange("p (g k m) -> p g k m", g=2, k=8)
            bg = bankB(); bu = bankB()
            mmgroup([dict(out=banks[bg][:], lhsT=wv[:, 0, k, :], rhs=h2b[:, k, :], start=(k == 0), stop=(k == 7)) for k in range(8)],
                    tuple(r_h2) + (r_ringF[s_],), (r_bank[bg],))
            mmgroup([dict(out=banks[bu][:], lhsT=wv[:, 1, k, :], rhs=h2b[:, k, :], start=(k == 0), stop=(k == 7)) for k in range(8)],
                    tuple(r_h2) + (r_ringF[s_],), (r_bank[bu],))
            sg = sgb[f % 2]; rsg = r_sg[f % 2]
            act(sg[:], banks[bg][:], AF.Tanh, (r_bank[bg],), (rsg,), scale=0.5)
            stt(sg[:], sg[:], 1.0, banks[bg][:], ALU.add, ALU.mult, (rsg, r_bank[bg]), (rsg,))
            stt(hh[:, f, :], sg[:], 0.5, banks[bu][:], ALU.mult, ALU.mult, (rsg, r_bank[bu]), (r_hh[f],))
            yield 16
        for oc in range(8):
            s_ = loadF(2816)
            wv = ringF[:, s_, 0:2816].rearrange("p (f m) -> p f m", f=NF)
            b = bankB()
            mmgroup([dict(out=banks[b][:], lhsT=wv[:, f, :], rhs=hh[:, f, :], start=(f == 0), stop=(f == NF - 1)) for f in range(NF)],
                    tuple(r_hh) + (r_ringF[s_],), (r_bank[b],))
            tt("dve", xb[:, oc, :], xb[:, oc, :], banks[b][:], ALU.add, (rx[oc], r_bank[b]), (rx[oc],))
            yield 22
        for k in range(8):
            dump(it, 28 + k, xb[:, k, :], (rx[k],))
        sqF = [hh[:, k, :] for k in range(8)]
        rms(bankB, rx, [xb[:, k, :] for k in range(8)], 8, float(D), sqF, r_hh, sgb[0][:], r_sg[0])
        for k in range(8):
            stt(xb[:, k, :], xb[:, k, :], vecs[:, V_GF + k:V_GF + k + 1], sgb[0][:], ALU.mult, ALU.mult,
                (rx[k], r_vecs, r_sg[0]), (rx[k],))
        dma("sp", out_d[:, :, t0:t0 + N], xb[:], tuple(rx), (r_out[xi],), o_ch[xi])
        if it + 2 < NT:
            dma("sp", xb[:], x_d[:, :, t0 + 2 * N:t0 + 3 * N], (), tuple(rx), x_ch[xi])
        yield 8

    sgA = [ctmpA[:, 0:512], ctmpA[:, 512:1024]]
    dma("sp", xbuf[0][:], x_d[:, :, 0:N], (), tuple(r_x[0]), x_ch[0])
    dma("sp", xbuf[1][:], x_d[:, :, N:2 * N], (), tuple(r_x[1]), x_ch[1])
    S.cur_prio = 0
    for _ in stageA(0):
        pass
    for it in range(NT):
        S.cur_prio = 2 * it + 3
        for _ in stageB(it):
            pass
        if it + 1 < NT:
            S.cur_prio = 2 * (it + 1)
            for _ in stageA(it + 1):
                pass
    S.cur_prio = 10 ** 6

    S.add("sp", lambda e: e.nop(), tuple(r_out) + (r_dbg,), ())

    with nc.Block() as block:
        S.emit(nc, block, eng_sems, None)
    build_program.last_makespan = getattr(S, "makespan", None)
    for cm in reversed(ctx):
        cm.__exit__(None, None, None)
    return nc


def _prep_shared(inp):
    f = np.float32
    g = lambda k: np.asarray(inp[k], dtype=f)
    sh = {}
    w_in = g("w_in")[0]
    sh["w_in"] = np.ascontiguousarray(w_in.reshape(8, 128, 6, 2, 128).transpose(2, 1, 3, 0, 4).reshape(6, 128, 2048))
    w_glu = g("s5_w_glu")[0]
    sh["w_glu"] = np.ascontiguousarray(w_glu.reshape(4, 128, 512).transpose(1, 0, 2).reshape(128, 2048))
    w_out = g("w_out")[0]
    sh["w_out"] = np.ascontiguousarray(w_out.reshape(8, 128, 4, 2, 128).transpose(2, 1, 3, 0, 4).reshape(4, 128, 2048))
    wg = g("w_gate")[0].reshape(8, 128, NF, 128)
    wu = g("w_up")[0].reshape(8, 128, NF, 128)
    gu = np.stack([wg, wu], axis=0)
    sh["w_gu"] = np.ascontiguousarray(gu.transpose(3, 2, 0, 1, 4).reshape(NF, 128, 2048))
    wd = g("w_down")[0].reshape(NF, 128, 8, 128)
    sh["w_dn"] = np.ascontiguousarray(wd.transpose(2, 1, 0, 3).reshape(8, 128, 2816))
    vecs = np.zeros((128, 72), f)
    vecs[:, 0:8] = g("norm1_g")[0].reshape(8, 128).T
    vecs[:, 8:16] = g("norm2_g")[0].reshape(8, 128).T
    vecs[:, 16:24] = g("final_g").reshape(8, 128).T
    vecs[:, 24:28] = g("s5_out_g")[0].reshape(4, 128).T
    vecs[:, 28:32] = g("lru_out_g")[0].reshape(4, 128).T
    vecs[:, 32:36] = g("s5_b_glu")[0].reshape(4, 128).T
    vecs[:, 36:40] = g("lru_conv_b")[0].reshape(4, 128).T
    vecs[:, 40:44] = g("lru_b_a")[0].reshape(4, 128).T
    vecs[:, 44:48] = g("lru_b_x")[0].reshape(4, 128).T
    vecs[:, 48:52] = g("lru_lambda")[0].reshape(4, 128).T
    vecs[:, 52:56] = g("s5_d")[0].reshape(4, 128).T
    cw = g("lru_conv_w")[0]
    cwl = cw.reshape(4, 4, 128).transpose(2, 1, 0)
    vecs[:, 56:72] = cwl.reshape(128, 16)
    sh["vecs"] = vecs
    lam_re = g("s5_lambda_re")[0]; lam_im = g("s5_lambda_im")[0]; ls = g("s5_log_step")[0]
    def layB(a):
        return a.reshape(16, 2, 64).transpose(1, 2, 0).reshape(128, 16)
    lsb = np.broadcast_to(ls[:, None], (32, 64))
    sh["lamB"] = np.ascontiguousarray(np.stack([layB(lam_re), layB(lam_im), layB(lsb)], axis=1))
    def layC(a):
        a4 = a.reshape(4, 4, 2, 64)
        a5 = np.broadcast_to(a4[:, :, :, None, :], (4, 4, 2, 16, 64))
        return a5.transpose(1, 2, 3, 0, 4).reshape(128, 256)
    sh["lamC"] = np.ascontiguousarray(np.stack([layC(lam_re), layC(lam_im), layC(lsb)], axis=1))
    b_re = g("s5_b_re")[0]; b_im = g("s5_b_im")[0]
    def layC_b(a):
        a5 = a.reshape(4, 4, 2, 64, 16)
        return a5.transpose(1, 2, 4, 0, 3).reshape(128, 256)
    sh["bC"] = np.ascontiguousarray(np.stack([layC_b(b_re), layC_b(b_im)], axis=1))
    def layB_b(a):
        return a.reshape(16, 2, 64, 16).transpose(1, 2, 0, 3).reshape(128, 256)
    sh["bB"] = np.ascontiguousarray(np.stack([layB_b(b_re), layB_b(b_im)], axis=1))
    c_re = g("s5_c_re")[0]; c_im = g("s5_c_im")[0]
    def layB_c(a):
        return a.reshape(16, 2, 16, 64).transpose(1, 3, 0, 2).reshape(128, 256)
    sh["cB"] = np.ascontiguousarray(np.stack([layB_c(c_re), layB_c(c_im)], axis=1))
    def bdiag(w):
        o = np.zeros((128, 4, 128), f)
        for h in range(8):
            m_, hh_ = h // 2, h % 2
            o[64 * hh_:64 * hh_ + 64, m_, 64 * hh_:64 * hh_ + 64] = w[h]
        return o.reshape(128, 512)
    sh["gatew"] = np.ascontiguousarray(np.stack([bdiag(g("lru_w_a")[0]), bdiag(g("lru_w_x")[0])], axis=1))
    consts = np.zeros((128, 263), f)
    consts[:, 0:128] = np.eye(128, dtype=f)
    consts[:, 128:257] = np.arange(129, dtype=f)[None, :]
    p = np.arange(128)
    for v in range(2):
        consts[:, 257 + v] = ((p % 32) // 16 == v).astype(f)
    for v in range(4):
        consts[:, 259 + v] = (p // 32 == v).astype(f)
    sh["consts"] = consts
    return sh


def _prep_x(x, core):
    xs = np.asarray(x[2 * core:2 * core + 2], dtype=np.float32).reshape(NTOK, 8, 128)
    return np.ascontiguousarray(xs.transpose(2, 1, 0))


_CACHE = {}


def kernel(**inputs):
    sh = _prep_shared(inputs)
    if "nc" not in _CACHE:
        _CACHE["nc"] = build_program()
    nc = _CACHE["nc"]
    in_maps = []
    for c in range(NCORES):
        m = dict(sh)
        m["x"] = _prep_x(inputs["x"], c)
        in_maps.append(m)
    res = run_bass_kernel_spmd(nc, in_maps, core_ids=list(range(NCORES)))
    outs = []
    for c in range(NCORES):
        o = np.asarray(res.results[c]["out"], dtype=np.float32)
        outs.append(o.transpose(2, 1, 0).reshape(2, L, D))
    return np.concatenate(outs, axis=0)
```

```python
import numpy as np
import concourse.bass as bass
import concourse.mybir as mybir
from concourse.bass_utils import run_bass_kernel_spmd

F32 = mybir.dt.float32
BF16 = mybir.dt.bfloat16
I32 = mybir.dt.int32
ALU = mybir.AluOpType
AF = mybir.ActivationFunctionType

NCORES = 8
D = 1024
L = 2048
NTOK = 4096
N = 512
NT = NTOK // N
TPS = L // N
TCH = 4
NCH = N // TCH
DFF = 2816
NF = DFF // 128
EPS = 1e-6
TWO_PI = 6.2831845
SLOT = 2816


class Region:
    __slots__ = ("name", "writer", "readers")

    def __init__(self, name):
        self.name = name
        self.writer = None
        self.readers = []


class Chan:
    def __init__(self, sem):
        self.sem = sem
        self.count = 0


class Op:
    __slots__ = ("idx", "eng", "fn", "deps", "signals", "tick", "sem", "is_dma", "chan", "dur", "prio", "odeps", "succs",
                 "ndeps", "ready_t", "fin")

    def __init__(self, idx, eng, fn, is_dma=False, chan=None):
        self.idx = idx
        self.eng = eng
        self.fn = fn
        self.deps = {}
        self.signals = False
        self.tick = None
        self.sem = None
        self.is_dma = is_dma
        self.chan = chan
        self.dur = 100.0
        self.prio = 0
        self.odeps = None
        self.succs = []
        self.ndeps = 0
        self.ready_t = 0.0
        self.fin = 0.0


class Sched:
    ENGS = ("pe", "act", "dve", "pool", "sp")

    def __init__(self):
        self.ops = []
        self.cur_prio = -1

    @staticmethod
    def _flat(x):
        out = []
        for i in x:
            if isinstance(i, (tuple, list)):
                out.extend(Sched._flat(i))
            else:
                out.append(i)
        return out

    def add(self, eng, fn, reads=(), writes=(), is_dma=False, chan=None, dur=100.0):
        op = Op(len(self.ops), eng, fn, is_dma, chan)
        op.dur = dur
        op.prio = self.cur_prio
        reads = self._flat(reads)
        writes = self._flat(writes)
        for r in reads:
            if r.writer is not None:
                op.deps[r.writer] = "raw"
            r.readers.append(op)
        for w in writes:
            if w.writer is not None and w.writer not in op.deps:
                op.deps[w.writer] = "waw"
            for rd in w.readers:
                if rd is not op and rd not in op.deps:
                    op.deps[rd] = "war"
            w.writer = op
            w.readers = []
        op.odeps = list(op.deps)
        for d in list(op.deps):
            if (not d.is_dma) and (not op.is_dma) and d.eng == op.eng and op.deps[d] != "raw":
                del op.deps[d]
        for d in op.deps:
            d.signals = True
        self.ops.append(op)
        return op

    def schedule(self):
        XLAT = 150.0
        for op in self.ops:
            op.succs = []
            op.ndeps = len(op.odeps)
            op.ready_t = 0.0
        for op in self.ops:
            for d in op.odeps:
                d.succs.append(op)
        ready = {e: [] for e in self.ENGS}
        free = {e: 0.0 for e in self.ENGS}
        for op in self.ops:
            if op.ndeps == 0:
                ready[op.eng].append(op)
        order = []
        n = len(self.ops)
        while len(order) < n:
            best = None
            bkey = None
            for e in self.ENGS:
                if not ready[e]:
                    continue
                fe = free[e]
                o = min(ready[e], key=lambda o_: (max(o_.ready_t, fe), o_.prio, o_.idx))
                key = (max(o.ready_t, fe), o.prio, o.idx)
                if bkey is None or key < bkey:
                    best, bkey = o, key
            st_ = bkey[0]
            e = best.eng
            ready[e].remove(best)
            if best.is_dma:
                free[e] = st_ + 60.0
                best.fin = st_ + best.dur
            else:
                best.fin = st_ + best.dur
                free[e] = best.fin
            order.append(best)
            for s_ in best.succs:
                lat = best.fin + (XLAT if (s_.eng != e or best.is_dma) else 30.0)
                if lat > s_.ready_t:
                    s_.ready_t = lat
                s_.ndeps -= 1
                if s_.ndeps == 0:
                    ready[s_.eng].append(s_)
        self.ops = order
        self.makespan = max(o.fin for o in order)

    def emit(self, nc, block, eng_sems, engines):
        self.schedule()
        cnt = {e: 0 for e in self.ENGS}
        for op in self.ops:
            if op.is_dma:
                op.chan.count += 16
                op.tick = op.chan.count
                op.sem = op.chan.sem
            elif op.signals:
                cnt[op.eng] += 1
                op.tick = cnt[op.eng]
                op.sem = eng_sems[op.eng]

        def run(engname, e):
            waited = {}
            for op in self.ops:
                if op.eng != engname:
                    continue
                need = {}
                for d in op.deps:
                    k = id(d.sem)
                    if k not in need or need[k][1] < d.tick:
                        need[k] = (d.sem, d.tick)
                for k, (sem, v) in need.items():
                    if waited.get(k, 0) >= v:
                        continue
                    e.wait_ge(sem, v)
                    waited[k] = v
                ins = op.fn(e)
                if op.is_dma:
                    ins.then_inc(op.sem, 16)
                elif op.signals:
                    ins.then_inc(op.sem, 1)

        @block.tensor
        def _(e):
            run("pe", e)

        @block.scalar
        def _(e):
            run("act", e)

        @block.vector
        def _(e):
            run("dve", e)

        @block.gpsimd
        def _(e):
            run("pool", e)

        @block.sync
        def _(e):
            run("sp", e)


def build_program(debug=None):
    nc = bass.Bass("TRN2", target_bir_lowering=False)
    S = Sched()

    def din(name, shape, dt=F32):
        return nc.dram_tensor(name, list(shape), dt, kind="ExternalInput").ap()

    x_d = din("x", (128, 8, NTOK))
    out_d = nc.dram_tensor("out", [128, 8, NTOK], F32, kind="ExternalOutput").ap()
    w_in_d = din("w_in", (6, 128, 2048))
    w_glu_d = din("w_glu", (128, 2048))
    w_out_d = din("w_out", (4, 128, 2048))
    w_gu_d = din("w_gu", (NF, 128, 2048))
    w_dn_d = din("w_dn", (8, 128, 2816))
    vec_d = din("vecs", (128, 72))
    lamB_d = din("lamB", (128, 3, 16))
    lamC_d = din("lamC", (128, 3, 256))
    bC_d = din("bC", (128, 2, 256))
    bB_d = din("bB", (128, 2, 256))
    cB_d = din("cB", (128, 2, 256))
    gate_d = din("gatew", (128, 2, 512))
    const_d = din("consts", (128, 128 + 129 + 2 + 4))
    dbg_d = None
    if debug is not None:
        dbg_d = nc.dram_tensor("dbg", list(debug), F32, kind="ExternalOutput").ap()

    ctx = []

    def sb(name, shape, dt=F32):
        cm = nc.sbuf_tensor("s_" + name, list(shape), dt)
        t = cm.__enter__()
        ctx.append(cm)
        return t

    def ps(name, shape, dt=F32):
        cm = nc.psum_tensor(name, list(shape), dt)
        t = cm.__enter__()
        ctx.append(cm)
        return t

    def sem(name):
        cm = nc.semaphore(name)
        s = cm.__enter__()
        ctx.append(cm)
        return s

    R = Region
    R2 = lambda a_, b_: (a_, b_)

    vecs = sb("vecs", (128, 72)); r_vecs = R("vecs")
    consts = sb("consts", (128, 263)); r_consts = R("consts")
    ident = consts[:, 0:128]
    iota = consts[:, 128:257]
    mask_g2 = consts[:, 257:259]
    mask_r = consts[:, 259:263]
    V_G1, V_G2, V_GF = 0, 8, 16
    V_S5G, V_LRUG, V_BGLU, V_CB, V_BA, V_BX, V_LAM, V_D = 24, 28, 32, 36, 40, 44, 48, 52
    dv = sb("dv", (128, 32)); r_dv = R("dv")
    DV_S5GH, DV_HBGLU, DV_HBA, DV_HBX, DV_C, DV_HC = 0, 4, 8, 12, 16, 20
    ones_b = sb("ones_b", (128, 128), BF16); r_ones = R("ones")
    epsb = sb("epsb", (128, 4)); r_epsb = R("epsb")

    cosT = sb("cosT", (128, 16, 129)); sinT = sb("sinT", (128, 16, 129)); r_tab = R("tab")
    rho = sb("rho", (128, 16)); r_rho = R("rho")
    Wsi = sb("Wsi", (128, 4, 4, 2, 128), BF16); r_Wsi = R("Wsi")
    Wso = sb("Wso", (128, 16, 4, 2, 32), BF16); r_Wso = R("Wso")
    BD = sb("BD", (128, 4, 4, 128), BF16); r_BD = R("BD")
    Dconv = sb("Dconv", (128, 4, 4, 128), BF16); r_Dconv = R("Dconv")
    Wg_b = sb("Wg_b", (128, 2, 4, 128), BF16); r_Wg = R("Wg")
    hstate = sb("hstate", (128, 4)); r_hstate = [R(f"hst{m}") for m in range(4)]
    sinit = sb("sinit", (128, 16, 2)); r_sinit = R("sinit")
    wlast = sb("wlast", (128, 16, 2)); r_wlast = R("wlast")
    ctmp = sb("ctmp", (128, 64)); r_ctmp = R("ctmp")

    xbuf = [sb(f"xbuf{i}", (128, 8, N)) for i in range(2)]
    r_x = [[R(f"x{i}_{k}") for k in range(8)] for i in range(2)]
    rstd = sb("rstd", (128, N)); r_rstd = R("rstd")
    hb = sb("hb", (128, 8, N), BF16); r_h = [R(f"h{k}") for k in range(8)]
    h2b = sb("h2b", (128, 8, N), BF16); r_h2 = [R(f"h2_{k}") for k in range(8)]
    ub = sb("ub", (128, 4, N), BF16); r_u = [R(f"u{k}") for k in range(4)]
    xrb = sb("xrb", (128, 4, N + 4), BF16); r_xr = [R(f"xr{k}") for k in range(4)]
    gy = sb("gy", (128, 4, N)); r_gy = [R(f"gy{k}") for k in range(4)]
    lo = gy; r_lo = r_gy
    ltmp = sb("ltmp", (128, 3584)); r_lt = [R(f"lt{k}") for k in range(14)]
    r_zh = [R("zh0"), R("zh1")]
    zf = sb("zf", (128, 4, N)); r_zf = [R(f"zf{k}") for k in range(4)]
    zb = hb[:, 4:8, :]; r_zb = r_h[4:8]
    mix = sb("mix", (128, 8, N), BF16); r_mix = [R(f"mix{k}") for k in range(8)]
    Sbuf = sb("Sbuf", (128, 16, 2, NCH + 1), BF16); r_S = [R(f"S{q}") for q in range(16)]
    hh = sb("hh", (128, NF, N), BF16); r_hh = [R(f"hh{f}") for f in range(NF)]
    arena = hh[:].rearrange("p f n -> p (f n)").bitcast(F32)
    sgb = [sb(f"sg{i}", (128, N)) for i in range(2)]; r_sg = [R("sg0"), R("sg1")]
    ctmpA = sb("ctmpA", (128, 1024)); r_sgA = [R("sgA0"), R("sgA1")]
    NSF, NSM = 3, 2
    ringF = sb("ringF", (128, NSF, 2816), BF16); r_ringF = [R(f"ringF{i}") for i in range(NSF)]
    ringM = sb("ringM", (128, NSM, 2048), BF16); r_ringM = [R(f"ringM{i}") for i in range(NSM)]
    banks = [ps(f"bank{i}", (128, N)) for i in range(8)]
    r_bank = [R(f"bank{i}") for i in range(8)]

    eng_sems = {e: sem(f"sem_{e}") for e in Sched.ENGS}
    ringF_ch = [Chan(sem(f"ringFsem{i}")) for i in range(NSF)]
    ringM_ch = [Chan(sem(f"ringMsem{i}")) for i in range(NSM)]
    ringF_chH = [Chan(sem(f"ringFsemH{i}")) for i in range(NSF)]
    ringM_chH = [Chan(sem(f"ringMsemH{i}")) for i in range(NSM)]
    ringF_chS = [Chan(sem(f"ringFsemS{i}")) for i in range(NSF)]
    ringM_chS = [Chan(sem(f"ringMsemS{i}")) for i in range(NSM)]
    misc3_ch = Chan(sem("misc3sem"))
    x_ch = [Chan(sem(f"xsem{i}")) for i in range(2)]
    o_ch = [Chan(sem(f"osem{i}")) for i in range(2)]
    misc_ch = Chan(sem("miscsem"))
    dbg_ch = Chan(sem("dbgsem"))
    misc2_ch = Chan(sem("misc2sem"))
    vecs_ch = Chan(sem("vecssem"))
    r_out = [R("out0"), R("out1")]

    st = {"bank": 4, "slot": 0, "bankA": 0, "bankB": 0}

    def bankA():
        b = 3 + st["bankA"]
        st["bankA"] = (st["bankA"] + 1) % 3
        return b

    def bankB():
        b = st["bankB"]
        st["bankB"] = (st["bankB"] + 1) % 3
        return b
    r_dbg = R("dbg")

    def dump(it_, idx, ap, regs):
        if dbg_d is None or it_ >= 2:
            return
        k_ = it_ * 40 + idx
        S.add("pool", lambda e: e.dma_start(out=dbg_d[k_], in_=ap), tuple(regs), (r_dbg,), is_dma=True, chan=dbg_ch)

    def next_bank():
        b = st["bank"]
        st["bank"] = (b + 1) % 8
        return b

    def fsz(ap):
        n_ = 1
        for d_ in ap.shape[1:]:
            n_ *= int(d_)
        return n_

    def edur(eng, ap, k=1.04):
        if eng == "pool":
            return 170.0 + 2.7 * fsz(ap)
        if eng == "act":
            return 140.0 + 1.25 * fsz(ap)
        return 90.0 + 1.5 * k * fsz(ap)

    def dma(q, out, in_, reads, writes, chan):
        d_ = 2000.0 + fsz(out) * 128 * 2 / (70.0 if q == "pool" else 150.0)
        S.add(q, lambda e, out=out, in_=in_: e.dma_start(out=out, in_=in_), reads, writes, is_dma=True, chan=chan, dur=d_)

    def act(out, in_, func, reads, writes, bias=None, scale=None):
        kw = {}
        if bias is not None:
            kw["bias"] = bias
        if scale is not None:
            kw["scale"] = scale
        S.add("act", lambda e: e.activation(out=out, in_=in_, func=func, **kw), reads, writes, dur=edur("act", out))

    def tt(eng, out, in0, in1, op, reads, writes):
        S.add(eng, lambda e: e.tensor_tensor(out=out, in0=in0, in1=in1, op=op), reads, writes, dur=edur(eng, out))

    def ts(eng, out, in0, s1, op0, reads, writes, s2=None, op1=None):
        if op1 is None:
            S.add(eng, lambda e: e.tensor_scalar(out=out, in0=in0, scalar1=s1, scalar2=None, op0=op0), reads, writes, dur=edur(eng, out, 0.7))
        else:
            S.add(eng, lambda e: e.tensor_scalar(out=out, in0=in0, scalar1=s1, scalar2=s2, op0=op0, op1=op1), reads, writes, dur=edur(eng, out, 0.7))

    def stt(out, in0, scalar, in1, op0, op1, reads, writes):
        S.add("dve", lambda e: e.scalar_tensor_tensor(out=out, in0=in0, scalar=scalar, in1=in1, op0=op0, op1=op1), reads, writes, dur=edur("dve", out))

    def cp(eng, out, in_, reads, writes):
        S.add(eng, lambda e: e.tensor_copy(out=out, in_=in_), reads, writes, dur=edur(eng, out, 0.8))

    def memset(eng, ap, val, writes):
        S.add(eng, lambda e: e.memset(ap, val), (), writes, dur=edur(eng, ap, 0.5))

    def scan(out, d0, d1, init, reads, writes):
        S.add("dve", lambda e: e.tensor_tensor_scan(out=out, data0=d0, data1=d1, initial=init, op0=ALU.mult, op1=ALU.add), reads, writes,
              dur=edur("dve", out))

    def mmgroup(mms, reads, writes):
        def fn(e):
            ins = None
            for m_ in mms:
                kw = {}
                if m_.get("tp") is not None:
                    kw["tile_position"] = m_["tp"]
                ins = e.matmul(m_["out"], lhsT=m_["lhsT"], rhs=m_["rhs"], start=m_["start"], stop=m_["stop"], **kw)
            return ins
        d_ = 0.0
        for m_ in mms:
            f_ = 4.0 if m_["rhs"].dtype == F32 else 1.0
            d_ += f_ * max(fsz(m_["rhs"]), 64) / 2.2 + 6.0
        S.add("pe", fn, reads, writes, dur=d_)

    dma("sp", vecs[:], vec_d[:], (), (r_vecs,), vecs_ch)
    dma("sp", consts[:], const_d[:], (), (r_consts,), misc2_ch)
    memset("pool", ones_b[:], 1.0, (r_ones,))
    memset("pool", epsb[:, 0:1], EPS, (r_epsb,))
    memset("pool", epsb[:, 1:2], 1.0, (r_epsb,))
    memset("pool", epsb[:, 2:3], 0.25, (r_epsb,))
    memset("pool", epsb[:, 3:4], 0.0, (r_epsb,))
    memset("pool", hstate[:], 0.0, tuple(r_hstate))

    r_pro = tuple(r_x[1]) + tuple(r_hh)
    pools = [(arena, 5632), (xbuf[1][:].rearrange("p a b -> p (a b)"), 4096)]
    pro_off = [0, 0]

    def palloc(n, pool=None):
        order = (0, 1) if pool is None else (pool,)
        for pi_ in order:
            if pro_off[pi_] + n <= pools[pi_][1]:
                o = pro_off[pi_]
                pro_off[pi_] += n
                return pools[pi_][0][:, o:o + n]
        raise AssertionError("prologue scratch exhausted")

    pro_i = gy[:].rearrange("p a b -> p (a b)").bitcast(I32)
    pro_f = zf[:].rearrange("p a b -> p (a b)")
    r_proi = tuple(r_gy) + tuple(r_zf)
    r_P = R("P")

    def P_ts(out, in0, s1, op0, s2=None, op1=None, extra_r=()):
        ts("dve", out, in0, s1, op0, (r_P,) + tuple(extra_r), (r_P,) + r_pro, s2, op1)

    def P_tt(out, in0, in1, op, extra_r=()):
        tt("dve", out, in0, in1, op, (r_P,) + tuple(extra_r), (r_P,) + r_pro)

    def P_act(out, in_, func, bias=None, scale=None, extra_r=()):
        act(out, in_, func, (r_P, r_epsb) + tuple(extra_r), (r_P,) + r_pro, bias, scale)

    def frac_round(out, x, n):
        assert n <= 2048
        ii = pro_i[:, 0:n]
        ff = pro_f[:, 0:n]
        S.add("dve", lambda e: e.tensor_copy(out=ii, in_=x), (r_P,) + r_proi, r_proi)
        S.add("dve", lambda e: e.tensor_copy(out=ff, in_=ii), r_proi, r_proi)
        tt("dve", out, x, ff, ALU.subtract, (r_P,) + r_proi, (r_P,) + r_pro)

    def cis(cos_out, sin_out, turns, n, tmp):
        frac_round(tmp, turns, n)
        P_act(sin_out, tmp, AF.Sin, scale=TWO_PI)
        P_ts(tmp, turns, 0.25, ALU.add)
        frac_round(tmp, tmp, n)
        P_act(cos_out, tmp, AF.Sin, scale=TWO_PI)

    def cmul(or_, oi_, ar, ai, br, bi, t1, t2):
        P_tt(t1, ar, br, ALU.mult)
        P_tt(t2, ai, bi, ALU.mult)
        P_tt(or_, t1, t2, ALU.subtract)
        P_tt(t1, ar, bi, ALU.mult)
        P_tt(t2, ai, br, ALU.mult)
        P_tt(oi_, t1, t2, ALU.add)

    def lam_powers(src_d, F, npow):
        lam = palloc(3 * F)
        dma("sp", lam, src_d, (), (r_P,) + r_pro, misc_ch)
        lr, li, ls = lam[:, 0:F], lam[:, F:2 * F], lam[:, 2 * F:3 * F]
        dl = palloc(F); a = palloc(F); fb = palloc(F); t1 = palloc(F); t2 = palloc(F); t3 = palloc(F)
        P_act(dl, ls, AF.Exp)
        P_ts(lr, lr, -1e-4, ALU.min)
        P_tt(a, lr, dl, ALU.mult)
        P_tt(fb, li, dl, ALU.mult)
        P_ts(fb, fb, 1.0 / (2.0 * np.pi), ALU.mult)
        em1 = palloc(F); E = palloc(F)
        P_ts(t1, a, 0.2, ALU.mult, 1.0, ALU.add)
        for cdiv in (0.25, 1.0 / 3.0, 0.5):
            P_tt(t1, t1, a, ALU.mult)
            P_ts(t1, t1, cdiv, ALU.mult, 1.0, ALU.add)
        P_tt(em1, t1, a, ALU.mult)
        P_ts(E, em1, 1.0, ALU.add)
        c1 = palloc(F); s1 = palloc(F); sh = palloc(F)
        cis(c1, s1, fb, F, t1)
        P_ts(t2, fb, 0.5, ALU.mult)
        frac_round(t1, t2, F)
        P_act(sh, t1, AF.Sin, scale=TWO_PI)
        Pw = {0: None}
        p1r = palloc(F); p1i = palloc(F)
        P_tt(p1r, E, c1, ALU.mult)
        P_tt(p1i, E, s1, ALU.mult)
        Pw[1] = (p1r, p1i)
        for k in range(2, npow + 1):
            pr = palloc(F); pi_ = palloc(F)
            cmul(pr, pi_, Pw[k - 1][0], Pw[k - 1][1], p1r, p1i, t1, t2)
            Pw[k] = (pr, pi_)
        nr = palloc(F)
        P_tt(nr, em1, c1, ALU.mult)
        P_tt(t1, sh, sh, ALU.mult)
        P_ts(t1, t1, -2.0, ALU.mult)
        P_tt(nr, nr, t1, ALU.add)
        den = palloc(F)
        P_tt(t1, lr, lr, ALU.mult)
        P_tt(t2, li, li, ALU.mult)
        P_tt(den, t1, t2, ALU.add)
        S.add("dve", lambda e: e.reciprocal(out=den, in_=den), (r_P,), (r_P,) + r_pro)
        kr = palloc(F); ki = palloc(F)
        P_tt(t1, nr, lr, ALU.mult)
        P_tt(t2, p1i, li, ALU.mult)
        P_tt(t1, t1, t2, ALU.add)
        P_tt(kr, t1, den, ALU.mult)
        P_tt(t1, p1i, lr, ALU.mult)
        P_tt(t2, nr, li, ALU.mult)
        P_tt(t1, t1, t2, ALU.subtract)
        P_tt(ki, t1, den, ALU.mult)
        return dict(P=Pw, kr=kr, ki=ki, fb=fb, E=E, t=(t1, t2, t3))

    FC = 256
    LC = lam_powers(lamC_d[:].rearrange("p a f -> p (a f)"), FC, 3)
    bC = palloc(2 * FC)
    dma("sp", bC, bC_d[:].rearrange("p a f -> p (a f)"), (), (r_P,) + r_pro, misc_ch)
    bre, bim = bC[:, 0:FC], bC[:, FC:2 * FC]
    bbr = palloc(FC); bbi = palloc(FC)
    t1, t2, t3 = LC["t"]
    cmul(bbr, bbi, LC["kr"], LC["ki"], bre, bim, t1, t2)
    wr = palloc(FC); wi = palloc(FC)
    for k in range(TCH):
        pw = TCH - 1 - k
        if pw == 0:
            srcr, srci = bbr, bbi
        else:
            cmul(wr, wi, LC["P"][pw][0], LC["P"][pw][1], bbr, bbi, t1, t2)
            srcr, srci = wr, wi
        for reim, src in ((0, srcr), (1, srci)):
            for g2 in range(2):
                ts("dve", Wsi[:, :, k, reim, 64 * g2:64 * g2 + 64], src.rearrange("p (m n) -> p m n", m=4),
                   mask_g2[:, g2:g2 + 1], ALU.mult, (r_P, r_consts), (r_Wsi,))
    pro_off[0] = 0; pro_off[1] = 0
    FB = 16
    LB = lam_powers(lamB_d[:].rearrange("p a f -> p (a f)"), FB, 4)
    t1, t2, t3 = LB["t"]
    E = LB["E"]
    P_tt(t1, E, E, ALU.mult)
    tt("dve", rho[:], t1, t1, ALU.mult, (r_P,), (r_rho,))
    f4 = palloc(FB); fr = palloc(FB)
    P_ts(f4, LB["fb"], float(TCH), ALU.mult)
    frac_round(fr, f4, FB)
    NJ = NCH + 1
    HQ = FB // 2
    mark_ = list(pro_off)
    xt_ = palloc(HQ * NJ)
    xt3 = xt_.rearrange("p (q j) -> p q j", q=HQ)
    ang = palloc(HQ * NJ)
    for hq in range(2):
        for q in range(HQ):
            qq = hq * HQ + q
            ts("dve", xt3[:, q, :], iota, fr[:, qq:qq + 1], ALU.mult, (r_P, r_consts), (r_P,) + r_pro)
        frac_round(ang, xt_, HQ * NJ)
        act(sinT[:, hq * HQ:(hq + 1) * HQ, :].rearrange("p q j -> p (q j)"), ang, AF.Sin, (r_P,), (r_tab,), scale=TWO_PI)
        P_ts(xt_, xt_, 0.25, ALU.add)
        frac_round(ang, xt_, HQ * NJ)
        act(cosT[:, hq * HQ:(hq + 1) * HQ, :].rearrange("p q j -> p (q j)"), ang, AF.Sin, (r_P,), (r_tab,), scale=TWO_PI)
    pro_off[0], pro_off[1] = mark_
    FQ = 256
    bB = palloc(2 * FQ); cB = palloc(2 * FQ)
    dma("sp", bB, bB_d[:].rearrange("p a f -> p (a f)"), (), (r_P,) + r_pro, misc_ch)
    dma("sp", cB, cB_d[:].rearrange("p a f -> p (a f)"), (), (r_P,) + r_pro, misc_ch)

    def bc(v):
        return v.unsqueeze(2).to_broadcast([128, 16, 16])

    def v3(v):
        return v.rearrange("p (q c) -> p q c", q=16)

    u1 = palloc(FQ); u2 = palloc(FQ)

    def cmul_bc(or_, oi_, sr, si, xr_, xi_):
        P_tt(v3(u1), v3(xr_), bc(sr), ALU.mult)
        P_tt(v3(u2), v3(xi_), bc(si), ALU.mult)
        P_tt(or_, u1, u2, ALU.subtract)
        P_tt(v3(u1), v3(xi_), bc(sr), ALU.mult)
        P_tt(v3(u2), v3(xr_), bc(si), ALU.mult)
        P_tt(oi_, u1, u2, ALU.add)

    bbBr = palloc(FQ); bbBi = palloc(FQ)
    cmul_bc(bbBr, bbBi, LB["kr"], LB["ki"], bB[:, 0:FQ], bB[:, FQ:2 * FQ])
    Bblk = palloc(2 * 16 * 32)
    Cblk = palloc(2 * 16 * 4 * 32, pool=1)
    S.add("dve", lambda e: e.memset(Bblk, 0.0), (r_P,), (r_P,) + r_pro)
    S.add("dve", lambda e: e.memset(Cblk, 0.0), (r_P,), (r_P,) + r_pro)
    S.add("pool", lambda e: e.memset(Wso[:].rearrange("p a b c d -> p (a b c d)"), 0.0), (), (r_Wso,))
    Bblk4 = Bblk.rearrange("p (a q c) -> p a q c", a=2, q=16)
    Cblk5 = Cblk.rearrange("p (a q d c) -> p a q d c", a=2, q=16, d=4)
    for g2 in range(2):
        pl = slice(64 * g2, 64 * g2 + 64)
        for reim, src in ((0, bbBr), (1, bbBi)):
            S.add("dve", lambda e, o=Bblk4[pl, reim, :, 16 * g2:16 * g2 + 16], i_=v3(src)[pl]: e.tensor_copy(out=o, in_=i_),
                  (r_P,), (r_P,) + r_pro)
    cpr = palloc(FQ); cpi = palloc(FQ)
    cre, cim = cB[:, 0:FQ], cB[:, FQ:2 * FQ]
    for pw in range(0, TCH + 1):
        if pw == 0:
            srcr, srci = cre, cim
        else:
            cmul_bc(cpr, cpi, LB["P"][pw][0], LB["P"][pw][1], cre, cim)
            srcr, srci = cpr, cpi
        for g2 in range(2):
            pl = slice(64 * g2, 64 * g2 + 64)
            cs = slice(16 * g2, 16 * g2 + 16)
            if pw < TCH:
                S.add("dve", lambda e, o=Cblk5[pl, 0, :, pw, cs], i_=v3(srcr)[pl]: e.tensor_copy(out=o, in_=i_), (r_P,), (r_P,) + r_pro)
                S.add("dve", lambda e, o=Cblk5[pl, 1, :, pw, cs], i_=v3(srci)[pl]: e.tensor_scalar(out=o, in0=i_, scalar1=-1.0, scalar2=None, op0=ALU.mult),
                      (r_P,), (r_P,) + r_pro)
            if pw >= 1:
                t_ = pw - 1
                S.add("dve", lambda e, o=Wso[pl, :, t_, 0, cs], i_=v3(srcr)[pl]: e.tensor_copy(out=o, in_=i_), (r_P,), (r_Wso,))
                S.add("dve", lambda e, o=Wso[pl, :, t_, 1, cs], i_=v3(srci)[pl]: e.tensor_scalar(out=o, in0=i_, scalar1=-1.0, scalar2=None, op0=ALU.mult),
                      (r_P,), (r_Wso,))
    kb = 0
    KD = banks[kb][:].rearrange("p (m d c) -> p m d c", m=4, d=4)
    mms = []
    for q in range(16):
        m_, r_ = q // 4, q % 4
        for reim in range(2):
            mms.append(dict(out=KD[32 * r_:32 * r_ + 32, m_, :, :], lhsT=Bblk4[:, reim, q, :], rhs=Cblk5[:, reim, q, :, :],
                            start=(reim == 0), stop=(reim == 1), tp=(0, 32 * r_)))
    mmgroup(mms, (r_P,), (r_bank[kb],))
    BDf = Cblk[:, 0:2048]
    BDf4 = BDf.rearrange("p (m d c) -> p m d c", m=4, d=4)
    for r_ in range(4):
        ts("dve", BDf4[:, :, :, 32 * r_:32 * r_ + 32], KD, mask_r[:, r_:r_ + 1], ALU.mult, (r_bank[kb], r_consts, r_P), (r_P,) + r_pro)
    for m_ in range(4):
        stt(BDf4[:, m_, 0, :], ident, vecs[:, V_D + m_:V_D + m_ + 1], BDf4[:, m_, 0, :], ALU.mult, ALU.add,
            (r_P, r_consts, r_vecs), (r_P,) + r_pro)
    cp("dve", BD[:].rearrange("p m d c -> p (m d c)"), BDf, (r_P,), (r_BD,))

    dma("pool", Wg_b[:].rearrange("p a m c -> p (a m c)"), gate_d[:].rearrange("p a f -> p (a f)"), (), (r_Wg,), misc3_ch)
    for m_ in range(4):
        for k in range(4):
            ts("dve", Dconv[:, m_, k, :], ident, vecs[:, 56 + 4 * m_ + k:56 + 4 * m_ + k + 1], ALU.mult, (r_consts, r_vecs), (r_Dconv,))
    ts("dve", dv[:, DV_S5GH:DV_S5GH + 4], vecs[:, V_S5G:V_S5G + 4], 0.5, ALU.mult, (r_vecs,), (r_dv,))
    ts("dve", dv[:, DV_HBGLU:DV_HBGLU + 4], vecs[:, V_BGLU:V_BGLU + 4], 0.5, ALU.mult, (r_vecs,), (r_dv,))
    ts("dve", dv[:, DV_HBA:DV_HBA + 4], vecs[:, V_BA:V_BA + 4], 0.5, ALU.mult, (r_vecs,), (r_dv,))
    ts("dve", dv[:, DV_HBX:DV_HBX + 4], vecs[:, V_BX:V_BX + 4], 0.5, ALU.mult, (r_vecs,), (r_dv,))
    ev = palloc(4); pl_ = palloc(4)
    P_act(ev, vecs[:, V_LAM:V_LAM + 4], AF.Exp, scale=-1.0, extra_r=(r_vecs,))
    P_ts(pl_, ev, -0.25, ALU.mult, 1.0 / 3.0, ALU.add)
    P_tt(pl_, pl_, ev, ALU.mult)
    P_ts(pl_, pl_, -0.5, ALU.add)
    P_tt(pl_, pl_, ev, ALU.mult)
    P_ts(pl_, pl_, 1.0, ALU.add)
    P_tt(pl_, pl_, ev, ALU.mult)
    ts("dve", dv[:, DV_C:DV_C + 4], pl_, -8.0, ALU.mult, (r_P,), (r_dv,))
    ts("dve", dv[:, DV_HC:DV_HC + 4], pl_, -4.0, ALU.mult, (r_P,), (r_dv,))

    wlistM = []
    wlistF = []
    for _it in range(NT):
        wlistM += [(w_in_d[b_], 2048) for b_ in range(6)]
        wlistM += [(w_glu_d[:], 2048)]
        wlistM += [(w_out_d[b_], 2048) for b_ in range(4)]
        wlistF += [(w_gu_d[f_], 2048) for f_ in range(NF)]
        wlistF += [(w_dn_d[o_], 2816) for o_ in range(8)]
    wsM = {"issued": 0, "used": 0}
    wsF = {"issued": 0, "used": 0}
    scrM = nc.dram_tensor("scrM", [11, 128, 2048], BF16, kind="Internal").ap()
    scrF = nc.dram_tensor("scrF", [NF + 8, 128, 2816], BF16, kind="Internal").ap()

    def load_gen(ws, wl, ring_, rr, chs_sw, chs_hw, chs_st, ns, scr, per_tile):
        r_scr = [R(f"scr{id(scr)}_{i}") for i in range(per_tile)]

        def load_block(n):
            i_ = ws["used"]
            assert wl[i_][1] == n
            while ws["issued"] < min(len(wl), i_ + ns):
                j_ = ws["issued"]
                sj = j_ % ns
                nj = wl[j_][1]
                bj = j_ % per_tile
                if j_ < per_tile:
                    dma("pool", ring_[:, sj, 0:nj], wl[j_][0], (), (rr[sj],), chs_sw[sj])
                    dma("sp", scr[bj, :, 0:nj], ring_[:, sj, 0:nj], (rr[sj],), (r_scr[bj],), chs_st[sj])
                else:
                    dma("sp", ring_[:, sj, 0:nj], scr[bj, :, 0:nj], (r_scr[bj],), (rr[sj],), chs_hw[sj])
                ws["issued"] += 1
            ws["used"] += 1
            return i_ % ns
        return load_block

    loadM = load_gen(wsM, wlistM, ringM, r_ringM, ringM_ch, ringM_chH, ringM_chS, NSM, scrM, 11)
    loadF = load_gen(wsF, wlistF, ringF, r_ringF, ringF_ch, ringF_chH, ringF_chS, NSF, scrF, NF + 8)

    def rms(bankfn, src_regions, src_aps, nchunks, dim, sq_aps, sq_regs, rs_ap, rs_reg, stage_scale=1.0):
        for k in range(nchunks):
            act(sq_aps[k], src_aps[k], AF.Square, (src_regions[k],), (sq_regs[k],), scale=stage_scale)
        b = bankfn()
        mms = [dict(out=banks[b][:], lhsT=ones_b[:], rhs=sq_aps[k], start=(k == 0), stop=(k == nchunks - 1)) for k in range(nchunks)]
        mmgroup(mms, tuple(sq_regs[:nchunks]) + (r_ones,), (r_bank[b],))
        act(rs_ap, banks[b][:], AF.Ln, (r_bank[b], r_epsb), (rs_reg,), bias=epsb[:, 0:1], scale=1.0 / dim)
        act(rs_ap, rs_ap, AF.Exp, (rs_reg,), (rs_reg,), scale=-0.5)

    sqA = [hb[:, k, :] for k in range(8)]
    r_sq = r_h
    sq2 = [h2b[:, k, :] for k in range(8)]

    def stageA(it):
        xi = it % 2
        xb = xbuf[xi]
        rx = r_x[xi]
        seq_start = (it % TPS == 0)
        rms(bankA, rx, [xb[:, k, :] for k in range(8)], 8, float(D), sqA, r_sq, rstd[:], r_rstd)
        for k in range(8):
            stt(hb[:, k, :], xb[:, k, :], vecs[:, V_G1 + k:V_G1 + k + 1], rstd[:], ALU.mult, ALU.mult,
                (rx[k], r_vecs, r_rstd), (r_h[k],))
        yield 8
        for blk in range(6):
            s_ = loadM(2048)
            wv = ringM[:, s_, 0:2048].rearrange("p (o k m) -> p o k m", o=2, k=8)
            for o2 in range(2):
                oc = 2 * blk + o2
                b = bankA()
                mms = [dict(out=banks[b][:], lhsT=wv[:, o2, k, :], rhs=hb[:, k, :], start=(k == 0), stop=(k == 7)) for k in range(8)]
                mmgroup(mms, tuple(r_h) + (r_ringM[s_],), (r_bank[b],))
                m_ = oc % 4
                if oc < 4:
                    cp("dve", ub[:, m_, :], banks[b][:], (r_bank[b],), (r_u[m_],))
                elif oc < 8:
                    if seq_start:
                        memset("pool", xrb[:, m_, 0:3], 0.0, (r_xr[m_],))
                    else:
                        cp("pool", xrb[:, m_, 0:3], xrb[:, m_, N:N + 3], (r_xr[m_],), (r_xr[m_],))
                    cp("dve", xrb[:, m_, 3:3 + N], banks[b][:], (r_bank[b], r_xr[m_]), (r_xr[m_],))
                else:
                    act(gy[:, m_, :], banks[b][:], AF.Gelu_apprx_tanh, (r_bank[b],), (r_gy[m_],))
                yield 8
        for m_ in range(4):
            dump(it, m_, ub[:, m_, :], (r_u[m_],))
        xhb = hb[:, 4:8, :]
        if seq_start and it > 0:
            for m_ in range(4):
                memset("dve", hstate[:, m_:m_ + 1], 0.0, (r_hstate[m_],))
        for half in range(2):
            xh = [ltmp[:, 512 * j:512 * j + 512] for j in range(2)]
            tr = [ltmp[:, 1024 + 512 * j:1024 + 512 * j + 512] for j in range(2)]
            ti = [ltmp[:, 2048 + 512 * j:2048 + 512 * j + 512] for j in range(2)]
            r_xh = [R2(r_lt[0], r_lt[1]), R2(r_lt[2], r_lt[3])]
            r_tr = [R2(r_lt[4], r_lt[5]), R2(r_lt[6], r_lt[7])]
            r_ti = [R2(r_lt[8], r_lt[9]), R2(r_lt[10], r_lt[11])]
            for j in range(2):
                m_ = 2 * half + j
                b = bankA()
                mms = [dict(out=banks[b][:], lhsT=Dconv[:, m_, k, :], rhs=xrb[:, m_, k:k + N], start=(k == 0), stop=(k == 3)) for k in range(4)]
                mmgroup(mms, (r_xr[m_], r_Dconv), (r_bank[b],))
                act(xh[j], banks[b][:], AF.Identity, (r_bank[b], r_vecs), (r_xh[j],), bias=vecs[:, V_CB + m_:V_CB + m_ + 1])
                cp("pool", xhb[:, m_, :], xh[j], (r_xh[j],), (r_h[4 + m_],))
                for gi, (dst, rdst, hbcol) in enumerate(((tr, r_tr, DV_HBA), (ti, r_ti, DV_HBX))):
                    b2 = bankA()
                    mmgroup([dict(out=banks[b2][:], lhsT=Wg_b[:, gi, m_, :], rhs=xhb[:, m_, :], start=True, stop=True)],
                            (r_h[4 + m_], r_Wg), (r_bank[b2],))
                    act(dst[j], banks[b2][:], AF.Tanh, (r_bank[b2], r_dv), (rdst[j],),
                        bias=dv[:, hbcol + m_:hbcol + m_ + 1], scale=0.5)
                yield 6
            for j in range(2):
                m_ = 2 * half + j
                a_ = tr[j]
                act(a_, tr[j], AF.Exp, (r_tr[j], r_dv), (r_tr[j],), bias=dv[:, DV_HC + m_:DV_HC + m_ + 1], scale=dv[:, DV_HC + m_:DV_HC + m_ + 1])
                w1 = sgA[j]; rw1 = r_sgA[j]
                act(w1, a_, AF.Square, (r_tr[j],), (rw1,))
                act(w1, w1, AF.Ln, (rw1, r_epsb), (rw1,), bias=epsb[:, 2:3], scale=-0.25)
                act(w1, w1, AF.Exp, (rw1,), (rw1,), scale=0.5)
                stt(ti[j], ti[j], 1.0, xh[j], ALU.add, ALU.mult, (r_ti[j], r_xh[j]), (r_ti[j],))
                tt("dve", ti[j], ti[j], w1, ALU.mult, (r_ti[j], rw1), (r_ti[j],))
                scan(xh[j], a_, ti[j], hstate[:, m_:m_ + 1], (r_tr[j], r_ti[j], r_hstate[m_]), (r_xh[j],))
                cp("dve", hstate[:, m_:m_ + 1], xh[j][:, N - 1:N], (r_xh[j],), (r_hstate[m_],))
                tt("dve", lo[:, m_, :], xh[j], gy[:, m_, :], ALU.mult, (r_xh[j], r_gy[m_]), (r_lo[m_],))
                yield 1
        for m_ in range(4):
            dump(it, 8 + m_, lo[:, m_, :], (r_lo[m_],))
        rms(bankA, r_lo, [lo[:, k, :] for k in range(4)], 4, 512.0, sqA, r_sq, rstd[:], r_rstd)
        for m_ in range(4):
            stt(mix[:, 4 + m_, :], lo[:, m_, :], vecs[:, V_LRUG + m_:V_LRUG + m_ + 1], rstd[:], ALU.mult, ALU.mult,
                (r_lo[m_], r_vecs, r_rstd), (r_mix[4 + m_],))
        yield 4
        Sc = Sbuf
        rS = r_S
        if seq_start:
            memset("pool", sinit[:], 0.0, (r_sinit,))
        for q in range(16):
            if seq_start:
                memset("pool", Sc[:, q, :, 0:1], 0.0, (rS[q],))
            else:
                cp("pool", Sc[:, q, :, 0:1], Sc[:, q, :, NCH:NCH + 1], (rS[q],), (rS[q],))

        def tbuf(idx, par):
            o_ = (2 * idx + par) * 256
            return ltmp[:, o_:o_ + 256].rearrange("p (a j) -> p a j", a=2), r_lt[2 * idx + par]
        def zhalf(par):
            return banks[6 + par][:, 0:256].rearrange("p (a j) -> p a j", a=2), r_bank[6 + par]

        def s0(q0):
            mms = []
            for reim in range(2):
                for k in range(TCH):
                    for q in (q0, q0 + 1):
                        m_, r_ = q // 4, q % 4
                        Z, rz = zhalf(q % 2)
                        mms.append(dict(out=Z[:, reim, :], lhsT=Wsi[32 * r_:32 * r_ + 32, m_, k, reim, :],
                                        rhs=ub[32 * r_:32 * r_ + 32, m_, k::TCH], start=(k == 0), stop=(k == TCH - 1),
                                        tp=(32 * r_, 0)))
            mmgroup(mms, (r_u[q0 // 4], r_Wsi), (r_bank[6], r_bank[7]))

        def s1(q):
            Z, rz = zhalf(q % 2)
            Zs, rZs = tbuf(0, q % 2)
            act(Zs, Z, AF.Copy, (rz,), (rZs,))

        def s2(q):
            Zs, rZs = tbuf(0, q % 2)
            T1, rT1 = tbuf(1, q % 2)
            M_, rM = tbuf(2, q % 2)
            cosb = cosT[:, q, 0:NCH].unsqueeze(1).to_broadcast([128, 2, NCH])
            sinq = sinT[:, q, 0:NCH]
            tt("pool", T1, Zs, cosb, ALU.mult, (rZs, r_tab), (rT1,))
            tt("pool", M_[:, 0, :], Zs[:, 1, :], sinq, ALU.mult, (rZs, r_tab), (rM,))
            tt("pool", M_[:, 1, :], Zs[:, 0, :], sinq, ALU.mult, (rZs, r_tab), (rM,))

        def s3(q):
            T1, rT1 = tbuf(1, q % 2)
            M_, rM = tbuf(2, q % 2)
            Wn, rWn = tbuf(3, q % 2)
            Wo, rWo = tbuf(4, q % 2)
            tt("dve", Wn[:, 0, :], T1[:, 0, :], M_[:, 0, :], ALU.add, (rT1, rM), (rWn,))
            tt("dve", Wn[:, 1, :], T1[:, 1, :], M_[:, 1, :], ALU.subtract, (rT1, rM), (rWn,))
            rho_b = rho[:, q:q + 1].to_broadcast([128, NCH])
            for reim in range(2):
                scan(Wo[:, reim, :], rho_b, Wn[:, reim, :], sinit[:, q, reim:reim + 1], (rWn, r_rho, r_sinit), (rWo,))

        def s4(q):
            Wo, rWo = tbuf(4, q % 2)
            P1, rP1 = tbuf(5, q % 2)
            Mp, rMp = tbuf(6, q % 2)
            cosb = cosT[:, q, 0:NCH].unsqueeze(1).to_broadcast([128, 2, NCH])
            sinq = sinT[:, q, 0:NCH]
            cp("pool", wlast[:, q, :], Wo[:, :, NCH - 1], (rWo,), (r_wlast,))
            tt("pool", P1, Wo, cosb, ALU.mult, (rWo, r_tab), (rP1,))
            tt("pool", Mp[:, 0, :], Wo[:, 1, :], sinq, ALU.mult, (rWo, r_tab), (rMp,))
            tt("pool", Mp[:, 1, :], Wo[:, 0, :], sinq, ALU.mult, (rWo, r_tab), (rMp,))

        def s5(q):
            P1, rP1 = tbuf(5, q % 2)
            Mp, rMp = tbuf(6, q % 2)
            tt("dve", Sc[:, q, 0, 1:NCH + 1], P1[:, 0, :], Mp[:, 0, :], ALU.subtract, (rP1, rMp), (rS[q],))
            tt("dve", Sc[:, q, 1, 1:NCH + 1], P1[:, 1, :], Mp[:, 1, :], ALU.add, (rP1, rMp), (rS[q],))

        def s6(m_):
            yb = bankA()
            Y = banks[yb][:].rearrange("p (t j) -> p t j", t=TCH)
            for t_ in range(TCH):
                mms = []
                for d_ in range(t_ + 1):
                    mms.append(dict(out=Y[:, t_, :], lhsT=BD[:, m_, d_, :], rhs=ub[:, m_, (t_ - d_)::TCH], start=(d_ == 0), stop=False))
                for r_ in range(4):
                    q = 4 * m_ + r_
                    for reim in range(2):
                        mms.append(dict(out=Y[32 * r_:32 * r_ + 32, t_, :], lhsT=Wso[:, q, t_, reim, :], rhs=Sc[:, q, reim, 0:NCH],
                                        start=False, stop=(r_ == 3 and reim == 1), tp=(0, 32 * r_)))
                mmgroup(mms, (r_u[m_], r_BD, r_Wso) + tuple(rS[4 * m_:4 * m_ + 4]), (r_bank[yb],))
            zview = zf[:, m_, :].rearrange("p (j t) -> p t j", t=TCH)
            act(zview, Y, AF.Gelu_apprx_tanh, (r_bank[yb],), (r_zf[m_],))
            cp("pool", zb[:, m_, :], zf[:, m_, :], (r_zf[m_],), (r_zb[m_],))
            dump(it, 4 + m_, zf[:, m_, :], (r_zf[m_],))

        for w in range(-1, 16 + 5):
            if 0 <= w < 16:
                s1(w)
            if 0 <= w + 1 < 16 and (w + 1) % 2 == 0:
                s0(w + 1)
            if 0 <= w - 1 < 16:
                s2(w - 1)
            if 0 <= w - 2 < 16:
                s3(w - 2)
            if 0 <= w - 3 < 16:
                s4(w - 3)
            if 0 <= w - 4 < 16:
                s5(w - 4)
                if (w - 4) % 4 == 3:
                    s6((w - 4) // 4)
            yield 3
        if not ((it + 1) % TPS == 0):
            c128 = cosT[:, :, NCH]; s128 = sinT[:, :, NCH]
            ta = ctmp[:, 0:16]; tb_ = ctmp[:, 16:32]; tc_ = ctmp[:, 32:48]; td = ctmp[:, 48:64]
            rc = r_ctmp
            tt("dve", ta, wlast[:, :, 0], c128, ALU.mult, (r_wlast, r_tab), (rc,))
            tt("dve", tb_, wlast[:, :, 1], s128, ALU.mult, (r_wlast, r_tab), (rc,))
            tt("dve", tc_, wlast[:, :, 0], s128, ALU.mult, (r_wlast, r_tab), (rc,))
            tt("dve", td, wlast[:, :, 1], c128, ALU.mult, (r_wlast, r_tab), (rc,))
            tt("dve", sinit[:, :, 0], ta, tb_, ALU.subtract, (rc,), (r_sinit,))
            tt("dve", sinit[:, :, 1], tc_, td, ALU.add, (rc,), (r_sinit,))
        s_ = loadM(2048)
        wv = ringM[:, s_, 0:2048].rearrange("p (k n) -> p k n", k=4)
        for oc in range(4):
            b = bankA()
            mms = [dict(out=banks[b][:], lhsT=wv[:, k, 128 * oc:128 * oc + 128], rhs=zb[:, k, :], start=(k == 0), stop=(k == 3)) for k in range(4)]
            mmgroup(mms, tuple(r_zb) + (r_ringM[s_],), (r_bank[b],))
            tg = sgA[oc % 2]; rtg = r_sgA[oc % 2]
            act(tg, banks[b][:], AF.Tanh, (r_bank[b], r_dv), (rtg,), bias=dv[:, DV_HBGLU + oc:DV_HBGLU + oc + 1], scale=0.5)
            stt(zf[:, oc, :], tg, 1.0, zf[:, oc, :], ALU.add, ALU.mult, (rtg, r_zf[oc]), (r_zf[oc],))
            yield 4
        rms(bankA, r_zf, [zf[:, k, :] for k in range(4)], 4, 512.0, sqA, r_sq, rstd[:], r_rstd, stage_scale=0.5)
        for m_ in range(4):
            stt(mix[:, m_, :], zf[:, m_, :], dv[:, DV_S5GH + m_:DV_S5GH + m_ + 1], rstd[:], ALU.mult, ALU.mult,
                (r_zf[m_], r_dv, r_rstd), (r_mix[m_],))
        yield 4
        for k in range(8):
            dump(it, 12 + k, mix[:, k, :], (r_mix[k],))
        for blk in range(4):
            s_ = loadM(2048)
            wv = ringM[:, s_, 0:2048].rearrange("p (o k m) -> p o k m", o=2, k=8)
            for o2 in range(2):
                oc = 2 * blk + o2
                b = bankA()
                mms = [dict(out=banks[b][:], lhsT=wv[:, o2, k, :], rhs=mix[:, k, :], start=(k == 0), stop=(k == 7)) for k in range(8)]
                mmgroup(mms, tuple(r_mix) + (r_ringM[s_],), (r_bank[b],))
                tt("dve", xb[:, oc, :], xb[:, oc, :], banks[b][:], ALU.add, (rx[oc], r_bank[b]), (rx[oc],))
                yield 8
        for k in range(8):
            dump(it, 20 + k, xb[:, k, :], (rx[k],))
        rms(bankA, rx, [xb[:, k, :] for k in range(8)], 8, float(D), sq2, r_h2, rstd[:], r_rstd)
        for k in range(8):
            stt(h2b[:, k, :], xb[:, k, :], vecs[:, V_G2 + k:V_G2 + k + 1], rstd[:], ALU.mult, ALU.mult,
                (rx[k], r_vecs, r_rstd), (r_h2[k],))
        yield 8

    def stageB(it):
        xi = it % 2
        xb = xbuf[xi]
        rx = r_x[xi]
        t0 = it * N
        for f in range(NF):
            s_ = loadF(2048)
            wv = ringF[:, s_, 0:2048].rearrange("p (g k m) -> p g k m", g=2, k=8)
            bg = bankB(); bu = bankB()
            mmgroup([dict(out=banks[bg][:], lhsT=wv[:, 0, k, :], rhs=h2b[:, k, :], start=(k == 0), stop=(k == 7)) for k in range(8)],
                    tuple(r_h2) + (r_ringF[s_],), (r_bank[bg],))
            mmgroup([dict(out=banks[bu][:], lhsT=wv[:, 1, k, :], rhs=h2b[:, k, :], start=(k == 0), stop=(k == 7)) for k in range(8)],
                    tuple(r_h2) + (r_ringF[s_],), (r_bank[bu],))
            sg = sgb[f % 2]; rsg = r_sg[f % 2]
            act(sg[:], banks[bg][:], AF.Tanh, (r_bank[bg],), (rsg,), scale=0.5)
            stt(sg[:], sg[:], 1.0, banks[bg][:], ALU.add, ALU.mult, (rsg, r_bank[bg]), (rsg,))
            stt(hh[:, f, :], sg[:], 0.5, banks[bu][:], ALU.mult, ALU.mult, (rsg, r_bank[bu]), (r_hh[f],))
            yield 16
        for oc in range(8):
            s_ = loadF(2816)
            wv = ringF[:, s_, 0:2816].rearrange("p (f m) -> p f m", f=NF)
            b = bankB()
            mmgroup([dict(out=banks[b][:], lhsT=wv[:, f, :], rhs=hh[:, f, :], start=(f == 0), stop=(f == NF - 1)) for f in range(NF)],
                    tuple(r_hh) + (r_ringF[s_],), (r_bank[b],))
            tt("dve", xb[:, oc, :], xb[:, oc, :], banks[b][:], ALU.add, (rx[oc], r_bank[b]), (rx[oc],))
            yield 22
        for k in range(8):
            dump(it, 28 + k, xb[:, k, :], (rx[k],))
        sqF = [hh[:, k, :] for k in range(8)]
        rms(bankB, rx, [xb[:, k, :] for k in range(8)], 8, float(D), sqF, r_hh, sgb[0][:], r_sg[0])
        for k in range(8):
            stt(xb[:, k, :], xb[:, k, :], vecs[:, V_GF + k:V_GF + k + 1], sgb[0][:], ALU.mult, ALU.mult,
                (rx[k], r_vecs, r_sg[0]), (rx[k],))
        dma("sp", out_d[:, :, t0:t0 + N], xb[:], tuple(rx), (r_out[xi],), o_ch[xi])
        if it + 2 < NT:
            dma("sp", xb[:], x_d[:, :, t0 + 2 * N:t0 + 3 * N], (), tuple(rx), x_ch[xi])
        yield 8

    sgA = [ctmpA[:, 0:512], ctmpA[:, 512:1024]]
    dma("sp", xbuf[0][:], x_d[:, :, 0:N], (), tuple(r_x[0]), x_ch[0])
    dma("sp", xbuf[1][:], x_d[:, :, N:2 * N], (), tuple(r_x[1]), x_ch[1])
    S.cur_prio = 0
    for _ in stageA(0):
        pass
    for it in range(NT):
        S.cur_prio = 2 * it + 3
        for _ in stageB(it):
            pass
        if it + 1 < NT:
            S.cur_prio = 2 * (it + 1)
            for _ in stageA(it + 1):
                pass
    S.cur_prio = 10 ** 6

    S.add("sp", lambda e: e.nop(), tuple(r_out) + (r_dbg,), ())

    with nc.Block() as block:
        S.emit(nc, block, eng_sems, None)
    build_program.last_makespan = getattr(S, "makespan", None)
    for cm in reversed(ctx):
        cm.__exit__(None, None, None)
    return nc


def _prep_shared(inp):
    f = np.float32
    g = lambda k: np.asarray(inp[k], dtype=f)
    sh = {}
    w_in = g("w_in")[0]
    sh["w_in"] = np.ascontiguousarray(w_in.reshape(8, 128, 6, 2, 128).transpose(2, 1, 3, 0, 4).reshape(6, 128, 2048))
    w_glu = g("s5_w_glu")[0]
    sh["w_glu"] = np.ascontiguousarray(w_glu.reshape(4, 128, 512).transpose(1, 0, 2).reshape(128, 2048))
    w_out = g("w_out")[0]
    sh["w_out"] = np.ascontiguousarray(w_out.reshape(8, 128, 4, 2, 128).transpose(2, 1, 3, 0, 4).reshape(4, 128, 2048))
    wg = g("w_gate")[0].reshape(8, 128, NF, 128)
    wu = g("w_up")[0].reshape(8, 128, NF, 128)
    gu = np.stack([wg, wu], axis=0)
    sh["w_gu"] = np.ascontiguousarray(gu.transpose(3, 2, 0, 1, 4).reshape(NF, 128, 2048))
    wd = g("w_down")[0].reshape(NF, 128, 8, 128)
    sh["w_dn"] = np.ascontiguousarray(wd.transpose(2, 1, 0, 3).reshape(8, 128, 2816))
    vecs = np.zeros((128, 72), f)
    vecs[:, 0:8] = g("norm1_g")[0].reshape(8, 128).T
    vecs[:, 8:16] = g("norm2_g")[0].reshape(8, 128).T
    vecs[:, 16:24] = g("final_g").reshape(8, 128).T
    vecs[:, 24:28] = g("s5_out_g")[0].reshape(4, 128).T
    vecs[:, 28:32] = g("lru_out_g")[0].reshape(4, 128).T
    vecs[:, 32:36] = g("s5_b_glu")[0].reshape(4, 128).T
    vecs[:, 36:40] = g("lru_conv_b")[0].reshape(4, 128).T
    vecs[:, 40:44] = g("lru_b_a")[0].reshape(4, 128).T
    vecs[:, 44:48] = g("lru_b_x")[0].reshape(4, 128).T
    vecs[:, 48:52] = g("lru_lambda")[0].reshape(4, 128).T
    vecs[:, 52:56] = g("s5_d")[0].reshape(4, 128).T
    cw = g("lru_conv_w")[0]
    cwl = cw.reshape(4, 4, 128).transpose(2, 1, 0)
    vecs[:, 56:72] = cwl.reshape(128, 16)
    sh["vecs"] = vecs
    lam_re = g("s5_lambda_re")[0]; lam_im = g("s5_lambda_im")[0]; ls = g("s5_log_step")[0]
    def layB(a):
        return a.reshape(16, 2, 64).transpose(1, 2, 0).reshape(128, 16)
    lsb = np.broadcast_to(ls[:, None], (32, 64))
    sh["lamB"] = np.ascontiguousarray(np.stack([layB(lam_re), layB(lam_im), layB(lsb)], axis=1))
    def layC(a):
        a4 = a.reshape(4, 4, 2, 64)
        a5 = np.broadcast_to(a4[:, :, :, None, :], (4, 4, 2, 16, 64))
        return a5.transpose(1, 2, 3, 0, 4).reshape(128, 256)
    sh["lamC"] = np.ascontiguousarray(np.stack([layC(lam_re), layC(lam_im), layC(lsb)], axis=1))
    b_re = g("s5_b_re")[0]; b_im = g("s5_b_im")[0]
    def layC_b(a):
        a5 = a.reshape(4, 4, 2, 64, 16)
        return a5.transpose(1, 2, 4, 0, 3).reshape(128, 256)
    sh["bC"] = np.ascontiguousarray(np.stack([layC_b(b_re), layC_b(b_im)], axis=1))
    def layB_b(a):
        return a.reshape(16, 2, 64, 16).transpose(1, 2, 0, 3).reshape(128, 256)
    sh["bB"] = np.ascontiguousarray(np.stack([layB_b(b_re), layB_b(b_im)], axis=1))
    c_re = g("s5_c_re")[0]; c_im = g("s5_c_im")[0]
    def layB_c(a):
        return a.reshape(16, 2, 16, 64).transpose(1, 3, 0, 2).reshape(128, 256)
    sh["cB"] = np.ascontiguousarray(np.stack([layB_c(c_re), layB_c(c_im)], axis=1))
    def bdiag(w):
        o = np.zeros((128, 4, 128), f)
        for h in range(8):
            m_, hh_ = h // 2, h % 2
            o[64 * hh_:64 * hh_ + 64, m_, 64 * hh_:64 * hh_ + 64] = w[h]
        return o.reshape(128, 512)
    sh["gatew"] = np.ascontiguousarray(np.stack([bdiag(g("lru_w_a")[0]), bdiag(g("lru_w_x")[0])], axis=1))
    consts = np.zeros((128, 263), f)
    consts[:, 0:128] = np.eye(128, dtype=f)
    consts[:, 128:257] = np.arange(129, dtype=f)[None, :]
    p = np.arange(128)
    for v in range(2):
        consts[:, 257 + v] = ((p % 32) // 16 == v).astype(f)
    for v in range(4):
        consts[:, 259 + v] = (p // 32 == v).astype(f)
    sh["consts"] = consts
    return sh


def _prep_x(x, core):
    xs = np.asarray(x[2 * core:2 * core + 2], dtype=np.float32).reshape(NTOK, 8, 128)
    return np.ascontiguousarray(xs.transpose(2, 1, 0))


_CACHE = {}


def kernel(**inputs):
    sh = _prep_shared(inputs)
    if "nc" not in _CACHE:
        _CACHE["nc"] = build_program()
    nc = _CACHE["nc"]
    in_maps = []
    for c in range(NCORES):
        m = dict(sh)
        m["x"] = _prep_x(inputs["x"], c)
        in_maps.append(m)
    res = run_bass_kernel_spmd(nc, in_maps, core_ids=list(range(NCORES)))
    outs = []
    for c in range(NCORES):
        o = np.asarray(res.results[c]["out"], dtype=np.float32)
        outs.append(o.transpose(2, 1, 0).reshape(2, L, D))
    return np.concatenate(outs, axis=0)
```

```python
import numpy as np
import concourse.bass as bass
import concourse.mybir as mybir
from concourse.bass_utils import run_bass_kernel_spmd

F32 = mybir.dt.float32
BF16 = mybir.dt.bfloat16
I32 = mybir.dt.int32
ALU = mybir.AluOpType
AF = mybir.ActivationFunctionType

NCORES = 8
D = 1024
L = 2048
NTOK = 4096
N = 512
NT = NTOK // N
TPS = L // N
TCH = 4
NCH = N // TCH
DFF = 2816
NF = DFF // 128
EPS = 1e-6
TWO_PI = 6.2831845
SLOT = 2816


class Region:
    __slots__ = ("name", "writer", "readers")

    def __init__(self, name):
        self.name = name
        self.writer = None
        self.readers = []


class Chan:
    def __init__(self, sem):
        self.sem = sem
        self.count = 0


class Op:
    __slots__ = ("idx", "eng", "fn", "deps", "signals", "tick", "sem", "is_dma", "chan", "dur", "prio", "odeps", "succs",
                 "ndeps", "ready_t", "fin")

    def __init__(self, idx, eng, fn, is_dma=False, chan=None):
        self.idx = idx
        self.eng = eng
        self.fn = fn
        self.deps = {}
        self.signals = False
        self.tick = None
        self.sem = None
        self.is_dma = is_dma
        self.chan = chan
        self.dur = 100.0
        self.prio = 0
        self.odeps = None
        self.succs = []
        self.ndeps = 0
        self.ready_t = 0.0
        self.fin = 0.0


class Sched:
    ENGS = ("pe", "act", "dve", "pool", "sp")

    def __init__(self):
        self.ops = []
        self.cur_prio = -1

    @staticmethod
    def _flat(x):
        out = []
        for i in x:
            if isinstance(i, (tuple, list)):
                out.extend(Sched._flat(i))
            else:
                out.append(i)
        return out

    def add(self, eng, fn, reads=(), writes=(), is_dma=False, chan=None, dur=100.0):
        op = Op(len(self.ops), eng, fn, is_dma, chan)
        op.dur = dur
        op.prio = self.cur_prio
        reads = self._flat(reads)
        writes = self._flat(writes)
        for r in reads:
            if r.writer is not None:
                op.deps[r.writer] = "raw"
            r.readers.append(op)
        for w in writes:
            if w.writer is not None and w.writer not in op.deps:
                op.deps[w.writer] = "waw"
            for rd in w.readers:
                if rd is not op and rd not in op.deps:
                    op.deps[rd] = "war"
            w.writer = op
            w.readers = []
        op.odeps = list(op.deps)
        for d in list(op.deps):
            if (not d.is_dma) and (not op.is_dma) and d.eng == op.eng and op.deps[d] != "raw":
                del op.deps[d]
        for d in op.deps:
            d.signals = True
        self.ops.append(op)
        return op

    def schedule(self):
        XLAT = 150.0
        for op in self.ops:
            op.succs = []
            op.ndeps = len(op.odeps)
            op.ready_t = 0.0
        for op in self.ops:
            for d in op.odeps:
                d.succs.append(op)
        ready = {e: [] for e in self.ENGS}
        free = {e: 0.0 for e in self.ENGS}
        for op in self.ops:
            if op.ndeps == 0:
                ready[op.eng].append(op)
        order = []
        n = len(self.ops)
        while len(order) < n:
            best = None
            bkey = None
            for e in self.ENGS:
                if not ready[e]:
                    continue
                fe = free[e]
                o = min(ready[e], key=lambda o_: (max(o_.ready_t, fe), o_.prio, o_.idx))
                key = (max(o.ready_t, fe), o.prio, o.idx)
                if bkey is None or key < bkey:
                    best, bkey = o, key
            st_ = bkey[0]
            e = best.eng
            ready[e].remove(best)
            if best.is_dma:
                free[e] = st_ + 60.0
                best.fin = st_ + best.dur
            else:
                best.fin = st_ + best.dur
                free[e] = best.fin
            order.append(best)
            for s_ in best.succs:
                lat = best.fin + (XLAT if (s_.eng != e or best.is_dma) else 30.0)
                if lat > s_.ready_t:
                    s_.ready_t = lat
                s_.ndeps -= 1
                if s_.ndeps == 0:
                    ready[s_.eng].append(s_)
        self.ops = order
        self.makespan = max(o.fin for o in order)

    def emit(self, nc, block, eng_sems, engines):
        self.schedule()
        cnt = {e: 0 for e in self.ENGS}
        for op in self.ops:
            if op.is_dma:
                op.chan.count += 16
                op.tick = op.chan.count
                op.sem = op.chan.sem
            elif op.signals:
                cnt[op.eng] += 1
                op.tick = cnt[op.eng]
                op.sem = eng_sems[op.eng]

        def run(engname, e):
            waited = {}
            for op in self.ops:
                if op.eng != engname:
                    continue
                need = {}
                for d in op.deps:
                    k = id(d.sem)
                    if k not in need or need[k][1] < d.tick:
                        need[k] = (d.sem, d.tick)
                for k, (sem, v) in need.items():
                    if waited.get(k, 0) >= v:
                        continue
                    e.wait_ge(sem, v)
                    waited[k] = v
                ins = op.fn(e)
                if op.is_dma:
                    ins.then_inc(op.sem, 16)
                elif op.signals:
                    ins.then_inc(op.sem, 1)

        @block.tensor
        def _(e):
            run("pe", e)

        @block.scalar
        def _(e):
            run("act", e)

        @block.vector
        def _(e):
            run("dve", e)

        @block.gpsimd
        def _(e):
            run("pool", e)

        @block.sync
        def _(e):
            run("sp", e)


def build_program(debug=None):
    nc = bass.Bass("TRN2", target_bir_lowering=False)
    S = Sched()

    def din(name, shape, dt=F32):
        return nc.dram_tensor(name, list(shape), dt, kind="ExternalInput").ap()

    x_d = din("x", (128, 8, NTOK))
    out_d = nc.dram_tensor("out", [128, 8, NTOK], F32, kind="ExternalOutput").ap()
    w_in_d = din("w_in", (6, 128, 2048))
    w_glu_d = din("w_glu", (128, 2048))
    w_out_d = din("w_out", (4, 128, 2048))
    w_gu_d = din("w_gu", (NF, 128, 2048))
    w_dn_d = din("w_dn", (8, 128, 2816))
    vec_d = din("vecs", (128, 72))
    lamB_d = din("lamB", (128, 3, 16))
    lamC_d = din("lamC", (128, 3, 256))
    bC_d = din("bC", (128, 2, 256))
    bB_d = din("bB", (128, 2, 256))
    cB_d = din("cB", (128, 2, 256))
    gate_d = din("gatew", (128, 2, 512))
    const_d = din("consts", (128, 128 + 129 + 2 + 4))
    dbg_d = None
    if debug is not None:
        dbg_d = nc.dram_tensor("dbg", list(debug), F32, kind="ExternalOutput").ap()

    ctx = []

    def sb(name, shape, dt=F32):
        cm = nc.sbuf_tensor("s_" + name, list(shape), dt)
        t = cm.__enter__()
        ctx.append(cm)
        return t

    def ps(name, shape, dt=F32):
        cm = nc.psum_tensor(name, list(shape), dt)
        t = cm.__enter__()
        ctx.append(cm)
        return t

    def sem(name):
        cm = nc.semaphore(name)
        s = cm.__enter__()
        ctx.append(cm)
        return s

    R = Region
    R2 = lambda a_, b_: (a_, b_)

    vecs = sb("vecs", (128, 72)); r_vecs = R("vecs")
    consts = sb("consts", (128, 263)); r_consts = R("consts")
    ident = consts[:, 0:128]
    iota = consts[:, 128:257]
    mask_g2 = consts[:, 257:259]
    mask_r = consts[:, 259:263]
    V_G1, V_G2, V_GF = 0, 8, 16
    V_S5G, V_LRUG, V_BGLU, V_CB, V_BA, V_BX, V_LAM, V_D = 24, 28, 32, 36, 40, 44, 48, 52
    dv = sb("dv", (128, 32)); r_dv = R("dv")
    DV_S5GH, DV_HBGLU, DV_HBA, DV_HBX, DV_C, DV_HC = 0, 4, 8, 12, 16, 20
    ones_b = sb("ones_b", (128, 128), BF16); r_ones = R("ones")
    epsb = sb("epsb", (128, 4)); r_epsb = R("epsb")

    cosT = sb("cosT", (128, 16, 129)); sinT = sb("sinT", (128, 16, 129)); r_tab = R("tab")
    rho = sb("rho", (128, 16)); r_rho = R("rho")
    Wsi = sb("Wsi", (128, 4, 4, 2, 128), BF16); r_Wsi = R("Wsi")
    Wso = sb("Wso", (128, 16, 4, 2, 32), BF16); r_Wso = R("Wso")
    BD = sb("BD", (128, 4, 4, 128), BF16); r_BD = R("BD")
    Dconv = sb("Dconv", (128, 4, 4, 128), BF16); r_Dconv = R("Dconv")
    Wg_b = sb("Wg_b", (128, 2, 4, 128), BF16); r_Wg = R("Wg")
    hstate = sb("hstate", (128, 4)); r_hstate = [R(f"hst{m}") for m in range(4)]
    sinit = sb("sinit", (128, 16, 2)); r_sinit = R("sinit")
    wlast = sb("wlast", (128, 16, 2)); r_wlast = R("wlast")
    ctmp = sb("ctmp", (128, 64)); r_ctmp = R("ctmp")

    xbuf = [sb(f"xbuf{i}", (128, 8, N)) for i in range(2)]
    r_x = [[R(f"x{i}_{k}") for k in range(8)] for i in range(2)]
    rstd = sb("rstd", (128, N)); r_rstd = R("rstd")
    hb = sb("hb", (128, 8, N), BF16); r_h = [R(f"h{k}") for k in range(8)]
    h2b = sb("h2b", (128, 8, N), BF16); r_h2 = [R(f"h2_{k}") for k in range(8)]
    ub = sb("ub", (128, 4, N), BF16); r_u = [R(f"u{k}") for k in range(4)]
    xrb = sb("xrb", (128, 4, N + 4), BF16); r_xr = [R(f"xr{k}") for k in range(4)]
    gy = sb("gy", (128, 4, N)); r_gy = [R(f"gy{k}") for k in range(4)]
    lo = gy; r_lo = r_gy
    ltmp = sb("ltmp", (128, 3584)); r_lt = [R(f"lt{k}") for k in range(14)]
    r_zh = [R("zh0"), R("zh1")]
    zf = sb("zf", (128, 4, N)); r_zf = [R(f"zf{k}") for k in range(4)]
    zb = hb[:, 4:8, :]; r_zb = r_h[4:8]
    mix = sb("mix", (128, 8, N), BF16); r_mix = [R(f"mix{k}") for k in range(8)]
    Sbuf = sb("Sbuf", (128, 16, 2, NCH + 1), BF16); r_S = [R(f"S{q}") for q in range(16)]
    hh = sb("hh", (128, NF, N), BF16); r_hh = [R(f"hh{f}") for f in range(NF)]
    arena = hh[:].rearrange("p f n -> p (f n)").bitcast(F32)
    sgb = [sb(f"sg{i}", (128, N)) for i in range(2)]; r_sg = [R("sg0"), R("sg1")]
    ctmpA = sb("ctmpA", (128, 1024)); r_sgA = [R("sgA0"), R("sgA1")]
    NSF, NSM = 3, 2
    ringF = sb("ringF", (128, NSF, 2816), BF16); r_ringF = [R(f"ringF{i}") for i in range(NSF)]
    ringM = sb("ringM", (128, NSM, 2048), BF16); r_ringM = [R(f"ringM{i}") for i in range(NSM)]
    banks = [ps(f"bank{i}", (128, N)) for i in range(8)]
    r_bank = [R(f"bank{i}") for i in range(8)]

    eng_sems = {e: sem(f"sem_{e}") for e in Sched.ENGS}
    ringF_ch = [Chan(sem(f"ringFsem{i}")) for i in range(NSF)]
    ringM_ch = [Chan(sem(f"ringMsem{i}")) for i in range(NSM)]
    ringF_chH = [Chan(sem(f"ringFsemH{i}")) for i in range(NSF)]
    ringM_chH = [Chan(sem(f"ringMsemH{i}")) for i in range(NSM)]
    ringF_chS = [Chan(sem(f"ringFsemS{i}")) for i in range(NSF)]
    ringM_chS = [Chan(sem(f"ringMsemS{i}")) for i in range(NSM)]
    misc3_ch = Chan(sem("misc3sem"))
    x_ch = [Chan(sem(f"xsem{i}")) for i in range(2)]
    o_ch = [Chan(sem(f"osem{i}")) for i in range(2)]
    misc_ch = Chan(sem("miscsem"))
    dbg_ch = Chan(sem("dbgsem"))
    misc2_ch = Chan(sem("misc2sem"))
    vecs_ch = Chan(sem("vecssem"))
    r_out = [R("out0"), R("out1")]

    st = {"bank": 4, "slot": 0, "bankA": 0, "bankB": 0}

    def bankA():
        b = 3 + st["bankA"]
        st["bankA"] = (st["bankA"] + 1) % 3
        return b

    def bankB():
        b = st["bankB"]
        st["bankB"] = (st["bankB"] + 1) % 3
        return b
    r_dbg = R("dbg")

    def dump(it_, idx, ap, regs):
        if dbg_d is None or it_ >= 2:
            return
        k_ = it_ * 40 + idx
        S.add("pool", lambda e: e.dma_start(out=dbg_d[k_], in_=ap), tuple(regs), (r_dbg,), is_dma=True, chan=dbg_ch)

    def next_bank():
        b = st["bank"]
        st["bank"] = (b + 1) % 8
        return b

    def fsz(ap):
        n_ = 1
        for d_ in ap.shape[1:]:
            n_ *= int(d_)
        return n_

    def edur(eng, ap, k=1.04):
        if eng == "pool":
            return 170.0 + 2.7 * fsz(ap)
        if eng == "act":
            return 140.0 + 1.25 * fsz(ap)
        return 90.0 + 1.5 * k * fsz(ap)

    def dma(q, out, in_, reads, writes, chan):
        d_ = 2000.0 + fsz(out) * 128 * 2 / (70.0 if q == "pool" else 150.0)
        S.add(q, lambda e, out=out, in_=in_: e.dma_start(out=out, in_=in_), reads, writes, is_dma=True, chan=chan, dur=d_)

    def act(out, in_, func, reads, writes, bias=None, scale=None):
        kw = {}
        if bias is not None:
            kw["bias"] = bias
        if scale is not None:
            kw["scale"] = scale
        S.add("act", lambda e: e.activation(out=out, in_=in_, func=func, **kw), reads, writes, dur=edur("act", out))

    def tt(eng, out, in0, in1, op, reads, writes):
        S.add(eng, lambda e: e.tensor_tensor(out=out, in0=in0, in1=in1, op=op), reads, writes, dur=edur(eng, out))

    def ts(eng, out, in0, s1, op0, reads, writes, s2=None, op1=None):
        if op1 is None:
            S.add(eng, lambda e: e.tensor_scalar(out=out, in0=in0, scalar1=s1, scalar2=None, op0=op0), reads, writes, dur=edur(eng, out, 0.7))
        else:
            S.add(eng, lambda e: e.tensor_scalar(out=out, in0=in0, scalar1=s1, scalar2=s2, op0=op0, op1=op1), reads, writes, dur=edur(eng, out, 0.7))

    def stt(out, in0, scalar, in1, op0, op1, reads, writes):
        S.add("dve", lambda e: e.scalar_tensor_tensor(out=out, in0=in0, scalar=scalar, in1=in1, op0=op0, op1=op1), reads, writes, dur=edur("dve", out))

    def cp(eng, out, in_, reads, writes):
        S.add(eng, lambda e: e.tensor_copy(out=out, in_=in_), reads, writes, dur=edur(eng, out, 0.8))

    def memset(eng, ap, val, writes):
        S.add(eng, lambda e: e.memset(ap, val), (), writes, dur=edur(eng, ap, 0.5))

    def scan(out, d0, d1, init, reads, writes):
        S.add("dve", lambda e: e.tensor_tensor_scan(out=out, data0=d0, data1=d1, initial=init, op0=ALU.mult, op1=ALU.add), reads, writes,
              dur=edur("dve", out))

    def mmgroup(mms, reads, writes):
        def fn(e):
            ins = None
            for m_ in mms:
                kw = {}
                if m_.get("tp") is not None:
                    kw["tile_position"] = m_["tp"]
                ins = e.matmul(m_["out"], lhsT=m_["lhsT"], rhs=m_["rhs"], start=m_["start"], stop=m_["stop"], **kw)
            return ins
        d_ = 0.0
        for m_ in mms:
            f_ = 4.0 if m_["rhs"].dtype == F32 else 1.0
            d_ += f_ * max(fsz(m_["rhs"]), 64) / 2.2 + 6.0
        S.add("pe", fn, reads, writes, dur=d_)

    dma("sp", vecs[:], vec_d[:], (), (r_vecs,), vecs_ch)
    dma("sp", consts[:], const_d[:], (), (r_consts,), misc2_ch)
    memset("pool", ones_b[:], 1.0, (r_ones,))
    memset("pool", epsb[:, 0:1], EPS, (r_epsb,))
    memset("pool", epsb[:, 1:2], 1.0, (r_epsb,))
    memset("pool", epsb[:, 2:3], 0.25, (r_epsb,))
    memset("pool", epsb[:, 3:4], 0.0, (r_epsb,))
    memset("pool", hstate[:], 0.0, tuple(r_hstate))

    r_pro = tuple(r_x[1]) + tuple(r_hh)
    pools = [(arena, 5632), (xbuf[1][:].rearrange("p a b -> p (a b)"), 4096)]
    pro_off = [0, 0]

    def palloc(n, pool=None):
        order = (0, 1) if pool is None else (pool,)
        for pi_ in order:
            if pro_off[pi_] + n <= pools[pi_][1]:
                o = pro_off[pi_]
                pro_off[pi_] += n
                return pools[pi_][0][:, o:o + n]
        raise AssertionError("prologue scratch exhausted")

    pro_i = gy[:].rearrange("p a b -> p (a b)").bitcast(I32)
    pro_f = zf[:].rearrange("p a b -> p (a b)")
    r_proi = tuple(r_gy) + tuple(r_zf)
    r_P = R("P")

    def P_ts(out, in0, s1, op0, s2=None, op1=None, extra_r=()):
        ts("dve", out, in0, s1, op0, (r_P,) + tuple(extra_r), (r_P,) + r_pro, s2, op1)

    def P_tt(out, in0, in1, op, extra_r=()):
        tt("dve", out, in0, in1, op, (r_P,) + tuple(extra_r), (r_P,) + r_pro)

    def P_act(out, in_, func, bias=None, scale=None, extra_r=()):
        act(out, in_, func, (r_P, r_epsb) + tuple(extra_r), (r_P,) + r_pro, bias, scale)

    def frac_round(out, x, n):
        assert n <= 2048
        ii = pro_i[:, 0:n]
        ff = pro_f[:, 0:n]
        S.add("dve", lambda e: e.tensor_copy(out=ii, in_=x), (r_P,) + r_proi, r_proi)
        S.add("dve", lambda e: e.tensor_copy(out=ff, in_=ii), r_proi, r_proi)
        tt("dve", out, x, ff, ALU.subtract, (r_P,) + r_proi, (r_P,) + r_pro)

    def cis(cos_out, sin_out, turns, n, tmp):
        frac_round(tmp, turns, n)
        P_act(sin_out, tmp, AF.Sin, scale=TWO_PI)
        P_ts(tmp, turns, 0.25, ALU.add)
        frac_round(tmp, tmp, n)
        P_act(cos_out, tmp, AF.Sin, scale=TWO_PI)

    def cmul(or_, oi_, ar, ai, br, bi, t1, t2):
        P_tt(t1, ar, br, ALU.mult)
        P_tt(t2, ai, bi, ALU.mult)
        P_tt(or_, t1, t2, ALU.subtract)
        P_tt(t1, ar, bi, ALU.mult)
        P_tt(t2, ai, br, ALU.mult)
        P_tt(oi_, t1, t2, ALU.add)

    def lam_powers(src_d, F, npow):
        lam = palloc(3 * F)
        dma("sp", lam, src_d, (), (r_P,) + r_pro, misc_ch)
        lr, li, ls = lam[:, 0:F], lam[:, F:2 * F], lam[:, 2 * F:3 * F]
        dl = palloc(F); a = palloc(F); fb = palloc(F); t1 = palloc(F); t2 = palloc(F); t3 = palloc(F)
        P_act(dl, ls, AF.Exp)
        P_ts(lr, lr, -1e-4, ALU.min)
        P_tt(a, lr, dl, ALU.mult)
        P_tt(fb, li, dl, ALU.mult)
        P_ts(fb, fb, 1.0 / (2.0 * np.pi), ALU.mult)
        em1 = palloc(F); E = palloc(F)
        P_ts(t1, a, 0.2, ALU.mult, 1.0, ALU.add)
        for cdiv in (0.25, 1.0 / 3.0, 0.5):
            P_tt(t1, t1, a, ALU.mult)
            P_ts(t1, t1, cdiv, ALU.mult, 1.0, ALU.add)
        P_tt(em1, t1, a, ALU.mult)
        P_ts(E, em1, 1.0, ALU.add)
        c1 = palloc(F); s1 = palloc(F); sh = palloc(F)
        cis(c1, s1, fb, F, t1)
        P_ts(t2, fb, 0.5, ALU.mult)
        frac_round(t1, t2, F)
        P_act(sh, t1, AF.Sin, scale=TWO_PI)
        Pw = {0: None}
        p1r = palloc(F); p1i = palloc(F)
        P_tt(p1r, E, c1, ALU.mult)
        P_tt(p1i, E, s1, ALU.mult)
        Pw[1] = (p1r, p1i)
        for k in range(2, npow + 1):
            pr = palloc(F); pi_ = palloc(F)
            cmul(pr, pi_, Pw[k - 1][0], Pw[k - 1][1], p1r, p1i, t1, t2)
            Pw[k] = (pr, pi_)
        nr = palloc(F)
        P_tt(nr, em1, c1, ALU.mult)
        P_tt(t1, sh, sh, ALU.mult)
        P_ts(t1, t1, -2.0, ALU.mult)
        P_tt(nr, nr, t1, ALU.add)
        den = palloc(F)
        P_tt(t1, lr, lr, ALU.mult)
        P_tt(t2, li, li, ALU.mult)
        P_tt(den, t1, t2, ALU.add)
        S.add("dve", lambda e: e.reciprocal(out=den, in_=den), (r_P,), (r_P,) + r_pro)
        kr = palloc(F); ki = palloc(F)
        P_tt(t1, nr, lr, ALU.mult)
        P_tt(t2, p1i, li, ALU.mult)
        P_tt(t1, t1, t2, ALU.add)
        P_tt(kr, t1, den, ALU.mult)
        P_tt(t1, p1i, lr, ALU.mult)
        P_tt(t2, nr, li, ALU.mult)
        P_tt(t1, t1, t2, ALU.subtract)
        P_tt(ki, t1, den, ALU.mult)
        return dict(P=Pw, kr=kr, ki=ki, fb=fb, E=E, t=(t1, t2, t3))

    FC = 256
    LC = lam_powers(lamC_d[:].rearrange("p a f -> p (a f)"), FC, 3)
    bC = palloc(2 * FC)
    dma("sp", bC, bC_d[:].rearrange("p a f -> p (a f)"), (), (r_P,) + r_pro, misc_ch)
    bre, bim = bC[:, 0:FC], bC[:, FC:2 * FC]
    bbr = palloc(FC); bbi = palloc(FC)
    t1, t2, t3 = LC["t"]
    cmul(bbr, bbi, LC["kr"], LC["ki"], bre, bim, t1, t2)
    wr = palloc(FC); wi = palloc(FC)
    for k in range(TCH):
        pw = TCH - 1 - k
        if pw == 0:
            srcr, srci = bbr, bbi
        else:
            cmul(wr, wi, LC["P"][pw][0], LC["P"][pw][1], bbr, bbi, t1, t2)
            srcr, srci = wr, wi
        for reim, src in ((0, srcr), (1, srci)):
            for g2 in range(2):
                ts("dve", Wsi[:, :, k, reim, 64 * g2:64 * g2 + 64], src.rearrange("p (m n) -> p m n", m=4),
                   mask_g2[:, g2:g2 + 1], ALU.mult, (r_P, r_consts), (r_Wsi,))
    pro_off[0] = 0; pro_off[1] = 0
    FB = 16
    LB = lam_powers(lamB_d[:].rearrange("p a f -> p (a f)"), FB, 4)
    t1, t2, t3 = LB["t"]
    E = LB["E"]
    P_tt(t1, E, E, ALU.mult)
    tt("dve", rho[:], t1, t1, ALU.mult, (r_P,), (r_rho,))
    f4 = palloc(FB); fr = palloc(FB)
    P_ts(f4, LB["fb"], float(TCH), ALU.mult)
    frac_round(fr, f4, FB)
    NJ = NCH + 1
    HQ = FB // 2
    mark_ = list(pro_off)
    xt_ = palloc(HQ * NJ)
    xt3 = xt_.rearrange("p (q j) -> p q j", q=HQ)
    ang = palloc(HQ * NJ)
    for hq in range(2):
        for q in range(HQ):
            qq = hq * HQ + q
            ts("dve", xt3[:, q, :], iota, fr[:, qq:qq + 1], ALU.mult, (r_P, r_consts), (r_P,) + r_pro)
        frac_round(ang, xt_, HQ * NJ)
        act(sinT[:, hq * HQ:(hq + 1) * HQ, :].rearrange("p q j -> p (q j)"), ang, AF.Sin, (r_P,), (r_tab,), scale=TWO_PI)
        P_ts(xt_, xt_, 0.25, ALU.add)
        frac_round(ang, xt_, HQ * NJ)
        act(cosT[:, hq * HQ:(hq + 1) * HQ, :].rearrange("p q j -> p (q j)"), ang, AF.Sin, (r_P,), (r_tab,), scale=TWO_PI)
    pro_off[0], pro_off[1] = mark_
    FQ = 256
    bB = palloc(2 * FQ); cB = palloc(2 * FQ)
    dma("sp", bB, bB_d[:].rearrange("p a f -> p (a f)"), (), (r_P,) + r_pro, misc_ch)
    dma("sp", cB, cB_d[:].rearrange("p a f -> p (a f)"), (), (r_P,) + r_pro, misc_ch)

    def bc(v):
        return v.unsqueeze(2).to_broadcast([128, 16, 16])

    def v3(v):
        return v.rearrange("p (q c) -> p q c", q=16)

    u1 = palloc(FQ); u2 = palloc(FQ)

    def cmul_bc(or_, oi_, sr, si, xr_, xi_):
        P_tt(v3(u1), v3(xr_), bc(sr), ALU.mult)
        P_tt(v3(u2), v3(xi_), bc(si), ALU.mult)
        P_tt(or_, u1, u2, ALU.subtract)
        P_tt(v3(u1), v3(xi_), bc(sr), ALU.mult)
        P_tt(v3(u2), v3(xr_), bc(si), ALU.mult)
        P_tt(oi_, u1, u2, ALU.add)

    bbBr = palloc(FQ); bbBi = palloc(FQ)
    cmul_bc(bbBr, bbBi, LB["kr"], LB["ki"], bB[:, 0:FQ], bB[:, FQ:2 * FQ])
    Bblk = palloc(2 * 16 * 32)
    Cblk = palloc(2 * 16 * 4 * 32, pool=1)
    S.add("dve", lambda e: e.memset(Bblk, 0.0), (r_P,), (r_P,) + r_pro)
    S.add("dve", lambda e: e.memset(Cblk, 0.0), (r_P,), (r_P,) + r_pro)
    S.add("pool", lambda e: e.memset(Wso[:].rearrange("p a b c d -> p (a b c d)"), 0.0), (), (r_Wso,))
    Bblk4 = Bblk.rearrange("p (a q c) -> p a q c", a=2, q=16)
    Cblk5 = Cblk.rearrange("p (a q d c) -> p a q d c", a=2, q=16, d=4)
    for g2 in range(2):
        pl = slice(64 * g2, 64 * g2 + 64)
        for reim, src in ((0, bbBr), (1, bbBi)):
            S.add("dve", lambda e, o=Bblk4[pl, reim, :, 16 * g2:16 * g2 + 16], i_=v3(src)[pl]: e.tensor_copy(out=o, in_=i_),
                  (r_P,), (r_P,) + r_pro)
    cpr = palloc(FQ); cpi = palloc(FQ)
    cre, cim = cB[:, 0:FQ], cB[:, FQ:2 * FQ]
    for pw in range(0, TCH + 1):
        if pw == 0:
            srcr, srci = cre, cim
        else:
            cmul_bc(cpr, cpi, LB["P"][pw][0], LB["P"][pw][1], cre, cim)
            srcr, srci = cpr, cpi
        for g2 in range(2):
            pl = slice(64 * g2, 64 * g2 + 64)
            cs = slice(16 * g2, 16 * g2 + 16)
            if pw < TCH:
                S.add("dve", lambda e, o=Cblk5[pl, 0, :, pw, cs], i_=v3(srcr)[pl]: e.tensor_copy(out=o, in_=i_), (r_P,), (r_P,) + r_pro)
                S.add("dve", lambda e, o=Cblk5[pl, 1, :, pw, cs], i_=v3(srci)[pl]: e.tensor_scalar(out=o, in0=i_, scalar1=-1.0, scalar2=None, op0=ALU.mult),
                      (r_P,), (r_P,) + r_pro)
            if pw >= 1:
                t_ = pw - 1
                S.add("dve", lambda e, o=Wso[pl, :, t_, 0, cs], i_=v3(srcr)[pl]: e.tensor_copy(out=o, in_=i_), (r_P,), (r_Wso,))
                S.add("dve", lambda e, o=Wso[pl, :, t_, 1, cs], i_=v3(srci)[pl]: e.tensor_scalar(out=o, in0=i_, scalar1=-1.0, scalar2=None, op0=ALU.mult),
                      (r_P,), (r_Wso,))
    kb = 0
    KD = banks[kb][:].rearrange("p (m d c) -> p m d c", m=4, d=4)
    mms = []
    for q in range(16):
        m_, r_ = q // 4, q % 4
        for reim in range(2):
            mms.append(dict(out=KD[32 * r_:32 * r_ + 32, m_, :, :], lhsT=Bblk4[:, reim, q, :], rhs=Cblk5[:, reim, q, :, :],
                            start=(reim == 0), stop=(reim == 1), tp=(0, 32 * r_)))
    mmgroup(mms, (r_P,), (r_bank[kb],))
    BDf = Cblk[:, 0:2048]
    BDf4 = BDf.rearrange("p (m d c) -> p m d c", m=4, d=4)
    for r_ in range(4):
        ts("dve", BDf4[:, :, :, 32 * r_:32 * r_ + 32], KD, mask_r[:, r_:r_ + 1], ALU.mult, (r_bank[kb], r_consts, r_P), (r_P,) + r_pro)
    for m_ in range(4):
        stt(BDf4[:, m_, 0, :], ident, vecs[:, V_D + m_:V_D + m_ + 1], BDf4[:, m_, 0, :], ALU.mult, ALU.add,
            (r_P, r_consts, r_vecs), (r_P,) + r_pro)
    cp("dve", BD[:].rearrange("p m d c -> p (m d c)"), BDf, (r_P,), (r_BD,))

    dma("pool", Wg_b[:].rearrange("p a m c -> p (a m c)"), gate_d[:].rearrange("p a f -> p (a f)"), (), (r_Wg,), misc3_ch)
    for m_ in range(4):
        for k in range(4):
            ts("dve", Dconv[:, m_, k, :], ident, vecs[:, 56 + 4 * m_ + k:56 + 4 * m_ + k + 1], ALU.mult, (r_consts, r_vecs), (r_Dconv,))
    ts("dve", dv[:, DV_S5GH:DV_S5GH + 4], vecs[:, V_S5G:V_S5G + 4], 0.5, ALU.mult, (r_vecs,), (r_dv,))
    ts("dve", dv[:, DV_HBGLU:DV_HBGLU + 4], vecs[:, V_BGLU:V_BGLU + 4], 0.5, ALU.mult, (r_vecs,), (r_dv,))
    ts("dve", dv[:, DV_HBA:DV_HBA + 4], vecs[:, V_BA:V_BA + 4], 0.5, ALU.mult, (r_vecs,), (r_dv,))
    ts("dve", dv[:, DV_HBX:DV_HBX + 4], vecs[:, V_BX:V_BX + 4], 0.5, ALU.mult, (r_vecs,), (r_dv,))
    ev = palloc(4); pl_ = palloc(4)
    P_act(ev, vecs[:, V_LAM:V_LAM + 4], AF.Exp, scale=-1.0, extra_r=(r_vecs,))
    P_ts(pl_, ev, -0.25, ALU.mult, 1.0 / 3.0, ALU.add)
    P_tt(pl_, pl_, ev, ALU.mult)
    P_ts(pl_, pl_, -0.5, ALU.add)
    P_tt(pl_, pl_, ev, ALU.mult)
    P_ts(pl_, pl_, 1.0, ALU.add)
    P_tt(pl_, pl_, ev, ALU.mult)
    ts("dve", dv[:, DV_C:DV_C + 4], pl_, -8.0, ALU.mult, (r_P,), (r_dv,))
    ts("dve", dv[:, DV_HC:DV_HC + 4], pl_, -4.0, ALU.mult, (r_P,), (r_dv,))

    wlistM = []
    wlistF = []
    for _it in range(NT):
        wlistM += [(w_in_d[b_], 2048) for b_ in range(6)]
        wlistM += [(w_glu_d[:], 2048)]
        wlistM += [(w_out_d[b_], 2048) for b_ in range(4)]
        wlistF += [(w_gu_d[f_], 2048) for f_ in range(NF)]
        wlistF += [(w_dn_d[o_], 2816) for o_ in range(8)]
    wsM = {"issued": 0, "used": 0}
    wsF = {"issued": 0, "used": 0}
    scrM = nc.dram_tensor("scrM", [11, 128, 2048], BF16, kind="Internal").ap()
    scrF = nc.dram_tensor("scrF", [NF + 8, 128, 2816], BF16, kind="Internal").ap()

    def load_gen(ws, wl, ring_, rr, chs_sw, chs_hw, chs_st, ns, scr, per_tile):
        r_scr = [R(f"scr{id(scr)}_{i}") for i in range(per_tile)]

        def load_block(n):
            i_ = ws["used"]
            assert wl[i_][1] == n
            while ws["issued"] < min(len(wl), i_ + ns):
                j_ = ws["issued"]
                sj = j_ % ns
                nj = wl[j_][1]
                bj = j_ % per_tile
                if j_ < per_tile:
                    dma("pool", ring_[:, sj, 0:nj], wl[j_][0], (), (rr[sj],), chs_sw[sj])
                    dma("sp", scr[bj, :, 0:nj], ring_[:, sj, 0:nj], (rr[sj],), (r_scr[bj],), chs_st[sj])
                else:
                    dma("sp", ring_[:, sj, 0:nj], scr[bj, :, 0:nj], (r_scr[bj],), (rr[sj],), chs_hw[sj])
                ws["issued"] += 1
            ws["used"] += 1
            return i_ % ns
        return load_block

    loadM = load_gen(wsM, wlistM, ringM, r_ringM, ringM_ch, ringM_chH, ringM_chS, NSM, scrM, 11)
    loadF = load_gen(wsF, wlistF, ringF, r_ringF, ringF_ch, ringF_chH, ringF_chS, NSF, scrF, NF + 8)

    def rms(bankfn, src_regions, src_aps, nchunks, dim, sq_aps, sq_regs, rs_ap, rs_reg, stage_scale=1.0):
        for k in range(nchunks):
            act(sq_aps[k], src_aps[k], AF.Square, (src_regions[k],), (sq_regs[k],), scale=stage_scale)
        b = bankfn()
        mms = [dict(out=banks[b][:], lhsT=ones_b[:], rhs=sq_aps[k], start=(k == 0), stop=(k == nchunks - 1)) for k in range(nchunks)]
        mmgroup(mms, tuple(sq_regs[:nchunks]) + (r_ones,), (r_bank[b],))
        act(rs_ap, banks[b][:], AF.Ln, (r_bank[b], r_epsb), (rs_reg,), bias=epsb[:, 0:1], scale=1.0 / dim)
        act(rs_ap, rs_ap, AF.Exp, (rs_reg,), (rs_reg,), scale=-0.5)

    sqA = [hb[:, k, :] for k in range(8)]
    r_sq = r_h
    sq2 = [h2b[:, k, :] for k in range(8)]

    def stageA(it):
        xi = it % 2
        xb = xbuf[xi]
        rx = r_x[xi]
        seq_start = (it % TPS == 0)
        rms(bankA, rx, [xb[:, k, :] for k in range(8)], 8, float(D), sqA, r_sq, rstd[:], r_rstd)
        for k in range(8):
            stt(hb[:, k, :], xb[:, k, :], vecs[:, V_G1 + k:V_G1 + k + 1], rstd[:], ALU.mult, ALU.mult,
                (rx[k], r_vecs, r_rstd), (r_h[k],))
        yield 8
        for blk in range(6):
            s_ = loadM(2048)
            wv = ringM[:, s_, 0:2048].rearrange("p (o k m) -> p o k m", o=2, k=8)
            for o2 in range(2):
                oc = 2 * blk + o2
                b = bankA()
                mms = [dict(out=banks[b][:], lhsT=wv[:, o2, k, :], rhs=hb[:, k, :], start=(k == 0), stop=(k == 7)) for k in range(8)]
                mmgroup(mms, tuple(r_h) + (r_ringM[s_],), (r_bank[b],))
                m_ = oc % 4
                if oc < 4:
                    cp("dve", ub[:, m_, :], banks[b][:], (r_bank[b],), (r_u[m_],))
                elif oc < 8:
                    if seq_start:
                        memset("pool", xrb[:, m_, 0:3], 0.0, (r_xr[m_],))
                    else:
                        cp("pool", xrb[:, m_, 0:3], xrb[:, m_, N:N + 3], (r_xr[m_],), (r_xr[m_],))
                    cp("dve", xrb[:, m_, 3:3 + N], banks[b][:], (r_bank[b], r_xr[m_]), (r_xr[m_],))
                else:
                    act(gy[:, m_, :], banks[b][:], AF.Gelu_apprx_tanh, (r_bank[b],), (r_gy[m_],))
                yield 8
        for m_ in range(4):
            dump(it, m_, ub[:, m_, :], (r_u[m_],))
        xhb = hb[:, 4:8, :]
        if seq_start and it > 0:
            for m_ in range(4):
                memset("dve", hstate[:, m_:m_ + 1], 0.0, (r_hstate[m_],))
        for half in range(2):
            xh = [ltmp[:, 512 * j:512 * j + 512] for j in range(2)]
            tr = [ltmp[:, 1024 + 512 * j:1024 + 512 * j + 512] for j in range(2)]
            ti = [ltmp[:, 2048 + 512 * j:2048 + 512 * j + 512] for j in range(2)]
            r_xh = [R2(r_lt[0], r_lt[1]), R2(r_lt[2], r_lt[3])]
            r_tr = [R2(r_lt[4], r_lt[5]), R2(r_lt[6], r_lt[7])]
            r_ti = [R2(r_lt[8], r_lt[9]), R2(r_lt[10], r_lt[11])]
            for j in range(2):
                m_ = 2 * half + j
                b = bankA()
                mms = [dict(out=banks[b][:], lhsT=Dconv[:, m_, k, :], rhs=xrb[:, m_, k:k + N], start=(k == 0), stop=(k == 3)) for k in range(4)]
                mmgroup(mms, (r_xr[m_], r_Dconv), (r_bank[b],))
                act(xh[j], banks[b][:], AF.Identity, (r_bank[b], r_vecs), (r_xh[j],), bias=vecs[:, V_CB + m_:V_CB + m_ + 1])
                cp("pool", xhb[:, m_, :], xh[j], (r_xh[j],), (r_h[4 + m_],))
                for gi, (dst, rdst, hbcol) in enumerate(((tr, r_tr, DV_HBA), (ti, r_ti, DV_HBX))):
                    b2 = bankA()
                    mmgroup([dict(out=banks[b2][:], lhsT=Wg_b[:, gi, m_, :], rhs=xhb[:, m_, :], start=True, stop=True)],
                            (r_h[4 + m_], r_Wg), (r_bank[b2],))
                    act(dst[j], banks[b2][:], AF.Tanh, (r_bank[b2], r_dv), (rdst[j],),
                        bias=dv[:, hbcol + m_:hbcol + m_ + 1], scale=0.5)
                yield 6
            for j in range(2):
                m_ = 2 * half + j
                a_ = tr[j]
                act(a_, tr[j], AF.Exp, (r_tr[j], r_dv), (r_tr[j],), bias=dv[:, DV_HC + m_:DV_HC + m_ + 1], scale=dv[:, DV_HC + m_:DV_HC + m_ + 1])
                w1 = sgA[j]; rw1 = r_sgA[j]
                act(w1, a_, AF.Square, (r_tr[j],), (rw1,))
                act(w1, w1, AF.Ln, (rw1, r_epsb), (rw1,), bias=epsb[:, 2:3], scale=-0.25)
                act(w1, w1, AF.Exp, (rw1,), (rw1,), scale=0.5)
                stt(ti[j], ti[j], 1.0, xh[j], ALU.add, ALU.mult, (r_ti[j], r_xh[j]), (r_ti[j],))
                tt("dve", ti[j], ti[j], w1, ALU.mult, (r_ti[j], rw1), (r_ti[j],))
                scan(xh[j], a_, ti[j], hstate[:, m_:m_ + 1], (r_tr[j], r_ti[j], r_hstate[m_]), (r_xh[j],))
                cp("dve", hstate[:, m_:m_ + 1], xh[j][:, N - 1:N], (r_xh[j],), (r_hstate[m_],))
                tt("dve", lo[:, m_, :], xh[j], gy[:, m_, :], ALU.mult, (r_xh[j], r_gy[m_]), (r_lo[m_],))
                yield 1
        for m_ in range(4):
            dump(it, 8 + m_, lo[:, m_, :], (r_lo[m_],))
        rms(bankA, r_lo, [lo[:, k, :] for k in range(4)], 4, 512.0, sqA, r_sq, rstd[:], r_rstd)
        for m_ in range(4):
            stt(mix[:, 4 + m_, :], lo[:, m_, :], vecs[:, V_LRUG + m_:V_LRUG + m_ + 1], rstd[:], ALU.mult, ALU.mult,
                (r_lo[m_], r_vecs, r_rstd), (r_mix[4 + m_],))
        yield 4
        Sc = Sbuf
        rS = r_S
        if seq_start:
            memset("pool", sinit[:], 0.0, (r_sinit,))
        for q in range(16):
            if seq_start:
                memset("pool", Sc[:, q, :, 0:1], 0.0, (rS[q],))
            else:
                cp("pool", Sc[:, q, :, 0:1], Sc[:, q, :, NCH:NCH + 1], (rS[q],), (rS[q],))

        def tbuf(idx, par):
            o_ = (2 * idx + par) * 256
            return ltmp[:, o_:o_ + 256].rearrange("p (a j) -> p a j", a=2), r_lt[2 * idx + par]
        def zhalf(par):
            return banks[6 + par][:, 0:256].rearrange("p (a j) -> p a j", a=2), r_bank[6 + par]

        def s0(q0):
            mms = []
            for reim in range(2):
                for k in range(TCH):
                    for q in (q0, q0 + 1):
                        m_, r_ = q // 4, q % 4
                        Z, rz = zhalf(q % 2)
                        mms.append(dict(out=Z[:, reim, :], lhsT=Wsi[32 * r_:32 * r_ + 32, m_, k, reim, :],
                                        rhs=ub[32 * r_:32 * r_ + 32, m_, k::TCH], start=(k == 0), stop=(k == TCH - 1),
                                        tp=(32 * r_, 0)))
            mmgroup(mms, (r_u[q0 // 4], r_Wsi), (r_bank[6], r_bank[7]))

        def s1(q):
            Z, rz = zhalf(q % 2)
            Zs, rZs = tbuf(0, q % 2)
            act(Zs, Z, AF.Copy, (rz,), (rZs,))

        def s2(q):
            Zs, rZs = tbuf(0, q % 2)
            T1, rT1 = tbuf(1, q % 2)
            M_, rM = tbuf(2, q % 2)
            cosb = cosT[:, q, 0:NCH].unsqueeze(1).to_broadcast([128, 2, NCH])
            sinq = sinT[:, q, 0:NCH]
            tt("pool", T1, Zs, cosb, ALU.mult, (rZs, r_tab), (rT1,))
            tt("pool", M_[:, 0, :], Zs[:, 1, :], sinq, ALU.mult, (rZs, r_tab), (rM,))
            tt("pool", M_[:, 1, :], Zs[:, 0, :], sinq, ALU.mult, (rZs, r_tab), (rM,))

        def s3(q):
            T1, rT1 = tbuf(1, q % 2)
            M_, rM = tbuf(2, q % 2)
            Wn, rWn = tbuf(3, q % 2)
            Wo, rWo = tbuf(4, q % 2)
            tt("dve", Wn[:, 0, :], T1[:, 0, :], M_[:, 0, :], ALU.add, (rT1, rM), (rWn,))
            tt("dve", Wn[:, 1, :], T1[:, 1, :], M_[:, 1, :], ALU.subtract, (rT1, rM), (rWn,))
            rho_b = rho[:, q:q + 1].to_broadcast([128, NCH])
            for reim in range(2):
                scan(Wo[:, reim, :], rho_b, Wn[:, reim, :], sinit[:, q, reim:reim + 1], (rWn, r_rho, r_sinit), (rWo,))

        def s4(q):
            Wo, rWo = tbuf(4, q % 2)
            P1, rP1 = tbuf(5, q % 2)
            Mp, rMp = tbuf(6, q % 2)
            cosb = cosT[:, q, 0:NCH].unsqueeze(1).to_broadcast([128, 2, NCH])
            sinq = sinT[:, q, 0:NCH]
            cp("pool", wlast[:, q, :], Wo[:, :, NCH - 1], (rWo,), (r_wlast,))
            tt("pool", P1, Wo, cosb, ALU.mult, (rWo, r_tab), (rP1,))
            tt("pool", Mp[:, 0, :], Wo[:, 1, :], sinq, ALU.mult, (rWo, r_tab), (rMp,))
            tt("pool", Mp[:, 1, :], Wo[:, 0, :], sinq, ALU.mult, (rWo, r_tab), (rMp,))

        def s5(q):
            P1, rP1 = tbuf(5, q % 2)
            Mp, rMp = tbuf(6, q % 2)
            tt("dve", Sc[:, q, 0, 1:NCH + 1], P1[:, 0, :], Mp[:, 0, :], ALU.subtract, (rP1, rMp), (rS[q],))
            tt("dve", Sc[:, q, 1, 1:NCH + 1], P1[:, 1, :], Mp[:, 1, :], ALU.add, (rP1, rMp), (rS[q],))

        def s6(m_):
            yb = bankA()
            Y = banks[yb][:].rearrange("p (t j) -> p t j", t=TCH)
            for t_ in range(TCH):
                mms = []
                for d_ in range(t_ + 1):
                    mms.append(dict(out=Y[:, t_, :], lhsT=BD[:, m_, d_, :], rhs=ub[:, m_, (t_ - d_)::TCH], start=(d_ == 0), stop=False))
                for r_ in range(4):
                    q = 4 * m_ + r_
                    for reim in range(2):
                        mms.append(dict(out=Y[32 * r_:32 * r_ + 32, t_, :], lhsT=Wso[:, q, t_, reim, :], rhs=Sc[:, q, reim, 0:NCH],
                                        start=False, stop=(reim == 1), tp=(0, 32 * r_)))
                mmgroup(mms, (r_u[m_], r_BD, r_Wso) + tuple(rS[4 * m_:4 * m_ + 4]), (r_bank[yb],))
            zview = zf[:, m_, :].rearrange("p (j t) -> p t j", t=TCH)
            act(zview, Y, AF.Gelu_apprx_tanh, (r_bank[yb],), (r_zf[m_],))
            cp("pool", zb[:, m_, :], zf[:, m_, :], (r_zf[m_],), (r_zb[m_],))
            dump(it, 4 + m_, zf[:, m_, :], (r_zf[m_],))

        for w in range(-1, 16 + 5):
            if 0 <= w < 16:
                s1(w)
            if 0 <= w + 1 < 16 and (w + 1) % 2 == 0:
                s0(w + 1)
            if 0 <= w - 1 < 16:
                s2(w - 1)
            if 0 <= w - 2 < 16:
                s3(w - 2)
            if 0 <= w - 3 < 16:
                s4(w - 3)
            if 0 <= w - 4 < 16:
                s5(w - 4)
                if (w - 4) % 4 == 3:
                    s6((w - 4) // 4)
            yield 3
        if not ((it + 1) % TPS == 0):
            c128 = cosT[:, :, NCH]; s128 = sinT[:, :, NCH]
            ta = ctmp[:, 0:16]; tb_ = ctmp[:, 16:32]; tc_ = ctmp[:, 32:48]; td = ctmp[:, 48:64]
            rc = r_ctmp
            tt("dve", ta, wlast[:, :, 0], c128, ALU.mult, (r_wlast, r_tab), (rc,))
            tt("dve", tb_, wlast[:, :, 1], s128, ALU.mult, (r_wlast, r_tab), (rc,))
            tt("dve", tc_, wlast[:, :, 0], s128, ALU.mult, (r_wlast, r_tab), (rc,))
            tt("dve", td, wlast[:, :, 1], c128, ALU.mult, (r_wlast, r_tab), (rc,))
            tt("dve", sinit[:, :, 0], ta, tb_, ALU.subtract, (rc,), (r_sinit,))
            tt("dve", sinit[:, :, 1], tc_, td, ALU.add, (rc,), (r_sinit,))
        s_ = loadM(2048)
        wv = ringM[:, s_, 0:2048].rearrange("p (k n) -> p k n", k=4)
        for oc in range(4):
            b = bankA()
            mms = [dict(out=banks[b][:], lhsT=wv[:, k, 128 * oc:128 * oc + 128], rhs=zb[:, k, :], start=(k == 0), stop=(k == 3)) for k in range(4)]
            mmgroup(mms, tuple(r_zb) + (r_ringM[s_],), (r_bank[b],))
            tg = sgA[oc % 2]; rtg = r_sgA[oc % 2]
            act(tg, banks[b][:], AF.Tanh, (r_bank[b], r_dv), (rtg,), bias=dv[:, DV_HBGLU + oc:DV_HBGLU + oc + 1], scale=0.5)
            stt(zf[:, oc, :], tg, 1.0, zf[:, oc, :], ALU.add, ALU.mult, (rtg, r_zf[oc]), (r_zf[oc],))
            yield 4
        rms(bankA, r_zf, [zf[:, k, :] for k in range(4)], 4, 512.0, sqA, r_sq, rstd[:], r_rstd, stage_scale=0.5)
        for m_ in range(4):
            stt(mix[:, m_, :], zf[:, m_, :], dv[:, DV_S5GH + m_:DV_S5GH + m_ + 1], rstd[:], ALU.mult, ALU.mult,
                (r_zf[m_], r_dv, r_rstd), (r_mix[m_],))
        yield 4
        for k in range(8):
            dump(it, 12 + k, mix[:, k, :], (r_mix[k],))
        for blk in range(4):
            s_ = loadM(2048)
            wv = ringM[:, s_, 0:2048].rearrange("p (o k m) -> p o k m", o=2, k=8)
            for o2 in range(2):
                oc = 2 * blk + o2
                b = bankA()
                mms = [dict(out=banks[b][:], lhsT=wv[:, o2, k, :], rhs=mix[:, k, :], start=(k == 0), stop=(k == 7)) for k in range(8)]
                mmgroup(mms, tuple(r_mix) + (r_ringM[s_],), (r_bank[b],))
                tt("dve", xb[:, oc, :], xb[:, oc, :], banks[b][:], ALU.add, (rx[oc], r_bank[b]), (rx[oc],))
                yield 8
        for k in range(8):
            dump(it, 20 + k, xb[:, k, :], (rx[k],))
        rms(bankA, rx, [xb[:, k, :] for k in range(8)], 8, float(D), sq2, r_h2, rstd[:], r_rstd)
        for k in range(8):
            stt(h2b[:, k, :], xb[:, k, :], vecs[:, V_G2 + k:V_G2 + k + 1], rstd[:], ALU.mult, ALU.mult,
                (rx[k], r_vecs, r_rstd), (r_h2[k],))
        yield 8

    def stageB(it):
        xi = it % 2
        xb = xbuf[xi]
        rx = r_x[xi]
        t0 = it * N
        for f in range(NF):
            s_ = loadF(2048)
            wv = ringF[:, s_, 0:2048].rearrange("p (g k m) -> p g k m", g=2, k=8)
            bg = bankB(); bu = bankB()
            mmgroup([dict(out=banks[bg][:], lhsT=wv[:, 0, k, :], rhs=h2b[:, k, :], start=(k == 0), stop=(k == 7)) for k in range(8)],
                    tuple(r_h2) + (r_ringF[s_],), (r_bank[bg],))
            mmgroup([dict(out=banks[bu][:], lhsT=wv[:, 1, k, :], rhs=h2b[:, k, :], start=(k == 0), stop=(k == 7)) for k in range(8)],
                    tuple(r_h2) + (r_ringF[s_],), (r_bank[bu],))
            sg = sgb[f % 2]; rsg = r_sg[f % 2]
            act(sg[:], banks[bg][:], AF.Tanh, (r_bank[bg],), (rsg,), scale=0.5)
            stt(sg[:], sg[:], 1.0, banks[bg][:], ALU.add, ALU.mult, (rsg, r_bank[bg]), (rsg,))
            stt(hh[:, f, :], sg[:], 0.5, banks[bu][:], ALU.mult, ALU.mult, (rsg, r_bank[bu]), (r_hh[f],))
            yield 16
        for oc in range(8):
            s_ = loadF(2816)
            wv = ringF[:, s_, 0:2816].rearrange("p (f m) -> p f m", f=NF)
            b = bankB()
            mmgroup([dict(out=banks[b][:], lhsT=wv[:, f, :], rhs=hh[:, f, :], start=(f == 0), stop=(f == NF - 1)) for f in range(NF)],
                    tuple(r_hh) + (r_ringF[s_],), (r_bank[b],))
            tt("dve", xb[:, oc, :], xb[:, oc, :], banks[b][:], ALU.add, (rx[oc], r_bank[b]), (rx[oc],))
            yield 22
        for k in range(8):
            dump(it, 28 + k, xb[:, k, :], (rx[k],))
        sqF = [hh[:, k, :] for k in range(8)]
        rms(bankB, rx, [xb[:, k, :] for k in range(8)], 8, float(D), sqF, r_hh, sgb[0][:], r_sg[0])
        for k in range(8):
            stt(xb[:, k, :], xb[:, k, :], vecs[:, V_GF + k:V_GF + k + 1], sgb[0][:], ALU.mult, ALU.mult,
                (rx[k], r_vecs, r_sg[0]), (rx[k],))
        dma("sp", out_d[:, :, t0:t0 + N], xb[:], tuple(rx), (r_out[xi],), o_ch[xi])
        if it + 2 < NT:
            dma("sp", xb[:], x_d[:, :, t0 + 2 * N:t0 + 3 * N], (), tuple(rx), x_ch[xi])
        yield 8

    sgA = [ctmpA[:, 0:512], ctmpA[:, 512:1024]]
    dma("sp", xbuf[0][:], x_d[:, :, 0:N], (), tuple(r_x[0]), x_ch[0])
    dma("sp", xbuf[1][:], x_d[:, :, N:2 * N], (), tuple(r_x[1]), x_ch[1])
    S.cur_prio = 0
    for _ in stageA(0):
        pass
    for it in range(NT):
        S.cur_prio = 2 * it + 3
        for _ in stageB(it):
            pass
        if it + 1 < NT:
            S.cur_prio = 2 * (it + 1)
            for _ in stageA(it + 1):
                pass
    S.cur_prio = 10 ** 6

    S.add("sp", lambda e: e.nop(), tuple(r_out) + (r_dbg,), ())

    with nc.Block() as block:
        S.emit(nc, block, eng_sems, None)
    build_program.last_makespan = getattr(S, "makespan", None)
    for cm in reversed(ctx):
        cm.__exit__(None, None, None)
    return nc


def _prep_shared(inp):
    f = np.float32
    g = lambda k: np.asarray(inp[k], dtype=f)
    sh = {}
    w_in = g("w_in")[0]
    sh["w_in"] = np.ascontiguousarray(w_in.reshape(8, 128, 6, 2, 128).transpose(2, 1, 3, 0, 4).reshape(6, 128, 2048))
    w_glu = g("s5_w_glu")[0]
    sh["w_glu"] = np.ascontiguousarray(w_glu.reshape(4, 128, 512).transpose(1, 0, 2).reshape(128, 2048))
    w_out = g("w_out")[0]
    sh["w_out"] = np.ascontiguousarray(w_out.reshape(8, 128, 4, 2, 128).transpose(2, 1, 3, 0, 4).reshape(4, 128, 2048))
    wg = g("w_gate")[0].reshape(8, 128, NF, 128)
    wu = g("w_up")[0].reshape(8, 128, NF, 128)
    gu = np.stack([wg, wu], axis=0)
    sh["w_gu"] = np.ascontiguousarray(gu.transpose(3, 2, 0, 1, 4).reshape(NF, 128, 2048))
    wd = g("w_down")[0].reshape(NF, 128, 8, 128)
    sh["w_dn"] = np.ascontiguousarray(wd.transpose(2, 1, 0, 3).reshape(8, 128, 2816))
    vecs = np.zeros((128, 72), f)
    vecs[:, 0:8] = g("norm1_g")[0].reshape(8, 128).T
    vecs[:, 8:16] = g("norm2_g")[0].reshape(8, 128).T
    vecs[:, 16:24] = g("final_g").reshape(8, 128).T
    vecs[:, 24:28] = g("s5_out_g")[0].reshape(4, 128).T
    vecs[:, 28:32] = g("lru_out_g")[0].reshape(4, 128).T
    vecs[:, 32:36] = g("s5_b_glu")[0].reshape(4, 128).T
    vecs[:, 36:40] = g("lru_conv_b")[0].reshape(4, 128).T
    vecs[:, 40:44] = g("lru_b_a")[0].reshape(4, 128).T
    vecs[:, 44:48] = g("lru_b_x")[0].reshape(4, 128).T
    vecs[:, 48:52] = g("lru_lambda")[0].reshape(4, 128).T
    vecs[:, 52:56] = g("s5_d")[0].reshape(4, 128).T
    cw = g("lru_conv_w")[0]
    cwl = cw.reshape(4, 4, 128).transpose(2, 1, 0)
    vecs[:, 56:72] = cwl.reshape(128, 16)
    sh["vecs"] = vecs
    lam_re = g("s5_lambda_re")[0]; lam_im = g("s5_lambda_im")[0]; ls = g("s5_log_step")[0]
    def layB(a):
        return a.reshape(16, 2, 64).transpose(1, 2, 0).reshape(128, 16)
    lsb = np.broadcast_to(ls[:, None], (32, 64))
    sh["lamB"] = np.ascontiguousarray(np.stack([layB(lam_re), layB(lam_im), layB(lsb)], axis=1))
    def layC(a):
        a4 = a.reshape(4, 4, 2, 64)
        a5 = np.broadcast_to(a4[:, :, :, None, :], (4, 4, 2, 16, 64))
        return a5.transpose(1, 2, 3, 0, 4).reshape(128, 256)
    sh["lamC"] = np.ascontiguousarray(np.stack([layC(lam_re), layC(lam_im), layC(lsb)], axis=1))
    b_re = g("s5_b_re")[0]; b_im = g("s5_b_im")[0]
    def layC_b(a):
        a5 = a.reshape(4, 4, 2, 64, 16)
        return a5.transpose(1, 2, 4, 0, 3).reshape(128, 256)
    sh["bC"] = np.ascontiguousarray(np.stack([layC_b(b_re), layC_b(b_im)], axis=1))
    def layB_b(a):
        return a.reshape(16, 2, 64, 16).transpose(1, 2, 0, 3).reshape(128, 256)
    sh["bB"] = np.ascontiguousarray(np.stack([layB_b(b_re), layB_b(b_im)], axis=1))
    c_re = g("s5_c_re")[0]; c_im = g("s5_c_im")[0]
    def layB_c(a):
        return a.reshape(16, 2, 16, 64).transpose(1, 3, 0, 2).reshape(128, 256)
    sh["cB"] = np.ascontiguousarray(np.stack([layB_c(c_re), layB_c(c_im)], axis=1))
    def bdiag(w):
        o = np.zeros((128, 4, 128), f)
        for h in range(8):
            m_, hh_ = h // 2, h % 2
            o[64 * hh_:64 * hh_ + 64, m_, 64 * hh_:64 * hh_ + 64] = w[h]
        return o.reshape(128, 512)
    sh["gatew"] = np.ascontiguousarray(np.stack([bdiag(g("lru_w_a")[0]), bdiag(g("lru_w_x")[0])], axis=1))
    consts = np.zeros((128, 263), f)
    consts[:, 0:128] = np.eye(128, dtype=f)
    consts[:, 128:257] = np.arange(129, dtype=f)[None, :]
    p = np.arange(128)
    for v in range(2):
        consts[:, 257 + v] = ((p % 32) // 16 == v).astype(f)
    for v in range(4):
        consts[:, 259 + v] = (p // 32 == v).astype(f)
    sh["consts"] = consts
    return sh


def _prep_x(x, core):
    xs = np.asarray(x[2 * core:2 * core + 2], dtype=np.float32).reshape(NTOK, 8, 128)
    return np.ascontiguousarray(xs.transpose(2, 1, 0))


_CACHE = {}


def kernel(**inputs):
    sh = _prep_shared(inputs)
    if "nc" not in _CACHE:
        _CACHE["nc"] = build_program()
    nc = _CACHE["nc"]
    in_maps = []
    for c in range(NCORES):
        m = dict(sh)
        m["x"] = _prep_x(inputs["x"], c)
        in_maps.append(m)
    res = run_bass_kernel_spmd(nc, in_maps, core_ids=list(range(NCORES)))
    outs = []
    for c in range(NCORES):
        o = np.asarray(res.results[c]["out"], dtype=np.float32)
        outs.append(o.transpose(2, 1, 0).reshape(2, L, D))
    return np.concatenate(outs, axis=0)
```
